# Optimizing a Trainium2 kernel written in Bass

```python
import jax
import jax.numpy as jnp
from jax import lax
import numpy as np


D_MODEL = 1024
BATCH = 8
SEQ = 8192
DEPTH = 4

GRID_W = 64
CTX_LEN = 256
N_MIXERS = 2
N_CONV_LAYERS = (DEPTH + 1) // 2
N_ATTN_LAYERS = DEPTH // 2
CONV_WIDTH = 31
HEAD_DIM = 64
N_HEADS = D_MODEL // HEAD_DIM
N_KV_HEADS = N_HEADS // 4
GQA_GROUP = N_HEADS // N_KV_HEADS
QKV_DIM = (N_HEADS + 2 * N_KV_HEADS) * HEAD_DIM
WINDOW = 128
BLOCK = 128
ROPE_BASE = 10000.0
D_FF = 2816
FFN_CONV_WIDTH = 3
EPS = 1e-6
NEG = -1e30

kernel_name = 'hybrid_conformer_swa_convffn_dit'


def rmsnorm(x, g):
    xf = x.astype(jnp.float32)
    y = xf * lax.rsqrt(jnp.mean(xf * xf, axis=-1, keepdims=True) + EPS)
    return (y * g.astype(jnp.float32)).astype(x.dtype)


def layernorm(x, g, b):
    xf = x.astype(jnp.float32)
    mu = jnp.mean(xf, axis=-1, keepdims=True)
    var = jnp.mean(jnp.square(xf - mu), axis=-1, keepdims=True)
    y = (xf - mu) * lax.rsqrt(var + EPS)
    return (y * g.astype(jnp.float32) + b.astype(jnp.float32)).astype(x.dtype)


def modulate(x, shift, scale):
    return x * (1.0 + scale) + shift


def dwconv(x, w, b):
    k = w.shape[0]
    pad = (k - 1) // 2
    y = lax.conv_general_dilated(x, w[:, None, :], window_strides=(1,), padding=[(pad, pad)],
                                 dimension_numbers=('NWC', 'WIO', 'NWC'),
                                 feature_group_count=x.shape[-1])
    return y + b


def axial_rope_tables(seq_len):
    rows = seq_len // GRID_W
    row = jnp.broadcast_to(jnp.arange(rows)[:, None], (rows, GRID_W)).reshape(-1).astype(jnp.float32)
    col = jnp.broadcast_to(jnp.arange(GRID_W)[None, :], (rows, GRID_W)).reshape(-1).astype(jnp.float32)
    n_freq = HEAD_DIM // 4
    inv = ROPE_BASE ** (-jnp.arange(n_freq, dtype=jnp.float32) / n_freq)
    ang = jnp.concatenate([row[:, None] * inv, col[:, None] * inv], axis=-1)
    return jnp.cos(ang), jnp.sin(ang)


def apply_rope(x, cos, sin):
    xf = x.astype(jnp.float32)
    half = HEAD_DIM // 2
    x1, x2 = xf[..., :half], xf[..., half:]
    c = cos[None, :, None, :]
    s = sin[None, :, None, :]
    return jnp.concatenate([x1 * c - x2 * s, x2 * c + x1 * s], axis=-1).astype(x.dtype)


def conformer_conv(h, w_pw1, b_pw1, w_dw, b_dw, ln_g, ln_b, w_pw2, b_pw2):
    u = h @ w_pw1 + b_pw1
    a, g = jnp.split(u, 2, axis=-1)
    v = a * jax.nn.sigmoid(g)
    v = dwconv(v, w_dw, b_dw)
    v = jax.nn.silu(layernorm(v, ln_g, ln_b))
    return v @ w_pw2 + b_pw2


def conv_ffn(h, w_up, w_dw, b_dw, w_down):
    u = dwconv(h @ w_up, w_dw, b_dw)
    a, b = jnp.split(u, 2, axis=-1)
    return (jax.nn.silu(a) * b) @ w_down


def split_qkv(h, w_qkv):
    bsz, length, _ = h.shape
    qkv = h @ w_qkv
    q = qkv[..., :N_HEADS * HEAD_DIM].reshape(bsz, length, N_HEADS, HEAD_DIM)
    k = qkv[..., N_HEADS * HEAD_DIM:(N_HEADS + N_KV_HEADS) * HEAD_DIM].reshape(bsz, length, N_KV_HEADS, HEAD_DIM)
    v = qkv[..., (N_HEADS + N_KV_HEADS) * HEAD_DIM:].reshape(bsz, length, N_KV_HEADS, HEAD_DIM)
    return q, k, v


def sink_logits(sink, bsz, nq):
    s = sink.astype(jnp.float32).reshape(N_KV_HEADS, GQA_GROUP)[None, :, :, None, None]
    return jnp.broadcast_to(s, (bsz, N_KV_HEADS, GQA_GROUP, nq, 1))


def window_attention(h_lat, h_ctx, w_qkv, w_o, sink, cos, sin, with_ctx_out):
    bsz, seq, _ = h_lat.shape
    scale = HEAD_DIM ** -0.5
    qc, kc, vc = split_qkv(h_ctx, w_qkv)
    q, k, v = split_qkv(h_lat, w_qkv)
    q = apply_rope(q, cos, sin) * scale
    k = apply_rope(k, cos, sin)
    q = q.reshape(bsz, seq, N_KV_HEADS, GQA_GROUP, HEAD_DIM)
    nb = seq // BLOCK
    k_pad = jnp.pad(k, ((0, 0), (BLOCK, BLOCK), (0, 0), (0, 0)))
    v_pad = jnp.pad(v, ((0, 0), (BLOCK, BLOCK), (0, 0), (0, 0)))
    q_blocks = jnp.moveaxis(q.reshape(bsz, nb, BLOCK, N_KV_HEADS, GQA_GROUP, HEAD_DIM), 1, 0)
    s_sink_lat = sink_logits(sink, bsz, BLOCK)

    def block_fn(args):
        i, qb = args
        kb = lax.dynamic_slice_in_dim(k_pad, i * BLOCK, 3 * BLOCK, axis=1)
        vb = lax.dynamic_slice_in_dim(v_pad, i * BLOCK, 3 * BLOCK, axis=1)
        qpos = i * BLOCK + jnp.arange(BLOCK)
        kpos = i * BLOCK - BLOCK + jnp.arange(3 * BLOCK)
        valid = (jnp.abs(qpos[:, None] - kpos[None, :]) <= WINDOW) & (kpos[None, :] >= 0) & (kpos[None, :] < seq)
        s_band = jnp.einsum('bqhgd,bkhd->bhgqk', qb, kb).astype(jnp.float32)
        s_band = jnp.where(valid[None, None, None], s_band, NEG)
        s_ctx = jnp.einsum('bqhgd,bkhd->bhgqk', qb, kc).astype(jnp.float32)
        p = jax.nn.softmax(jnp.concatenate([s_band, s_ctx, s_sink_lat], axis=-1), axis=-1)
        p_band = p[..., :3 * BLOCK].astype(vb.dtype)
        p_ctx = p[..., 3 * BLOCK:3 * BLOCK + CTX_LEN].astype(vc.dtype)
        o = jnp.einsum('bhgqk,bkhd->bqhgd', p_band, vb) + jnp.einsum('bhgqk,bkhd->bqhgd', p_ctx, vc)
        return o

    o_blocks = lax.map(block_fn, (jnp.arange(nb), q_blocks))
    o_lat = jnp.moveaxis(o_blocks, 0, 1).reshape(bsz, seq, N_HEADS * HEAD_DIM) @ w_o

    o_ctx = None
    if with_ctx_out:
        qcs = (qc * scale).reshape(bsz, CTX_LEN, N_KV_HEADS, GQA_GROUP, HEAD_DIM)
        s_c = jnp.einsum('bqhgd,bkhd->bhgqk', qcs, kc).astype(jnp.float32)
        p_c = jax.nn.softmax(jnp.concatenate([s_c, sink_logits(sink, bsz, CTX_LEN)], axis=-1), axis=-1)
        oc = jnp.einsum('bhgqk,bkhd->bqhgd', p_c[..., :CTX_LEN].astype(vc.dtype), vc)
        o_ctx = oc.reshape(bsz, CTX_LEN, N_HEADS * HEAD_DIM) @ w_o
    return o_lat, o_ctx


def setup_inputs(seed: int = 0) -> dict:
    key = jax.random.key(seed)
    ks = jax.random.split(key, 32)
    D = D_MODEL

    def nrm(k, shape, scale):
        return jax.random.normal(k, shape, jnp.float32) * scale

    return {
        'x': nrm(ks[0], (BATCH, SEQ, D), 1.0),
        'c': nrm(ks[1], (BATCH, D), 1.0),
        'ctx': nrm(ks[2], (BATCH, CTX_LEN, D), 1.0),
        'c_ctx': nrm(ks[3], (D,), 1.0),
        'ada_w': nrm(ks[4], (DEPTH, D, 6 * D), 0.02),
        'ada_b': nrm(ks[5], (DEPTH, 6 * D), 0.02),
        'norm1_g': 1.0 + nrm(ks[6], (DEPTH, D), 0.05),
        'norm2_g': 1.0 + nrm(ks[7], (DEPTH, D), 0.05),
        'final_g': 1.0 + nrm(ks[8], (D,), 0.05),
        'conv_w_pw1': nrm(ks[9], (N_CONV_LAYERS, D, 2 * D), D ** -0.5),
        'conv_b_pw1': nrm(ks[10], (N_CONV_LAYERS, 2 * D), 0.02),
        'conv_w_dw': nrm(ks[11], (N_CONV_LAYERS, CONV_WIDTH, D), CONV_WIDTH ** -0.5),
        'conv_b_dw': nrm(ks[12], (N_CONV_LAYERS, D), 0.02),
        'conv_ln_g': 1.0 + nrm(ks[13], (N_CONV_LAYERS, D), 0.05),
        'conv_ln_b': nrm(ks[14], (N_CONV_LAYERS, D), 0.02),
        'conv_w_pw2': nrm(ks[15], (N_CONV_LAYERS, D, D), D ** -0.5),
        'conv_b_pw2': nrm(ks[16], (N_CONV_LAYERS, D), 0.02),
        'attn_w_qkv': nrm(ks[17], (N_ATTN_LAYERS, D, QKV_DIM), D ** -0.5),
        'attn_w_o': nrm(ks[18], (N_ATTN_LAYERS, N_HEADS * HEAD_DIM, D), (N_HEADS * HEAD_DIM) ** -0.5),
        'attn_sink': nrm(ks[19], (N_ATTN_LAYERS, N_HEADS), 1.0),
        'ffn_w_up': nrm(ks[20], (DEPTH, D, 2 * D_FF), D ** -0.5),
        'ffn_w_dw': nrm(ks[21], (DEPTH, FFN_CONV_WIDTH, 2 * D_FF), FFN_CONV_WIDTH ** -0.5),
        'ffn_b_dw': nrm(ks[22], (DEPTH, 2 * D_FF), 0.02),
        'ffn_w_down': nrm(ks[23], (DEPTH, D_FF, D), D_FF ** -0.5),
    }


def reference(x, c, ctx, c_ctx, ada_w, ada_b, norm1_g, norm2_g, final_g,
              conv_w_pw1, conv_b_pw1, conv_w_dw, conv_b_dw, conv_ln_g, conv_ln_b, conv_w_pw2, conv_b_pw2,
              attn_w_qkv, attn_w_o, attn_sink,
              ffn_w_up, ffn_w_dw, ffn_b_dw, ffn_w_down):
    cos, sin = axial_rope_tables(x.shape[1])
    sc = jax.nn.silu(c)
    scc = jax.nn.silu(c_ctx)
    h_lat = x
    h_ctx = ctx
    for i in range(DEPTH):
        last = i == DEPTH - 1
        mod_lat = (sc @ ada_w[i] + ada_b[i])[:, None, :]
        mod_ctx = scc @ ada_w[i] + ada_b[i]
        sh1, sc1, g1, sh2, sc2, g2 = jnp.split(mod_lat, 6, axis=-1)
        csh1, csc1, cg1, csh2, csc2, cg2 = jnp.split(mod_ctx, 6, axis=-1)
        a_lat = modulate(rmsnorm(h_lat, norm1_g[i]), sh1, sc1)
        j = i // N_MIXERS
        if i % N_MIXERS == 0:
            y_lat = conformer_conv(a_lat, conv_w_pw1[j], conv_b_pw1[j], conv_w_dw[j], conv_b_dw[j],
                                   conv_ln_g[j], conv_ln_b[j], conv_w_pw2[j], conv_b_pw2[j])
            if not last:
                a_ctx = modulate(rmsnorm(h_ctx, norm1_g[i]), csh1, csc1)
                y_ctx = conformer_conv(a_ctx, conv_w_pw1[j], conv_b_pw1[j], conv_w_dw[j], conv_b_dw[j],
                                       conv_ln_g[j], conv_ln_b[j], conv_w_pw2[j], conv_b_pw2[j])
        else:
            a_ctx = modulate(rmsnorm(h_ctx, norm1_g[i]), csh1, csc1)
            y_lat, y_ctx = window_attention(a_lat, a_ctx, attn_w_qkv[j], attn_w_o[j], attn_sink[j],
                                            cos, sin, not last)
        h_lat = h_lat + g1 * y_lat
        b_lat = modulate(rmsnorm(h_lat, norm2_g[i]), sh2, sc2)
        h_lat = h_lat + g2 * conv_ffn(b_lat, ffn_w_up[i], ffn_w_dw[i], ffn_b_dw[i], ffn_w_down[i])
        if not last:
            h_ctx = h_ctx + cg1 * y_ctx
            b_ctx = modulate(rmsnorm(h_ctx, norm2_g[i]), csh2, csc2)
            h_ctx = h_ctx + cg2 * conv_ffn(b_ctx, ffn_w_up[i], ffn_w_dw[i], ffn_b_dw[i], ffn_w_down[i])
    return rmsnorm(h_lat, final_g)
```

```python
import contextlib
import numpy as np
import concourse.bass as bass
import concourse.mybir as mybir
from concourse.bass_utils import run_bass_kernel_spmd

F32, BF16 = mybir.dt.float32, mybir.dt.bfloat16
AF = mybir.ActivationFunctionType
ALU = mybir.AluOpType

D = 1024
CTX = 256
DFF = 2816
NPAIR = 22
DEPTH = 4
EPS = 1e-6
CW = 31
HD = 64
NH = 16
NKV = 4


class Res:
    __slots__ = ("name", "last_w", "readers", "excl")

    def __init__(self, name, excl=False):
        self.name = name
        self.last_w = None
        self.readers = {}
        self.excl = excl or name.startswith("ps")


class Sched:
    def __init__(self, nc, es):
        self.nc = nc
        self.es = es
        self.eng = {"pe": nc.tensor, "act": nc.scalar, "dve": nc.vector, "pool": nc.gpsimd, "sp": nc.sync}
        self.sem = {k: es.enter_context(nc.semaphore("sem_" + k)) for k in ("pe", "act", "dve", "pool")}
        self.cnt = {k: 0 for k in self.sem}
        self.waited = {k: {} for k in self.eng}
        self.chans = {}

    def _deps(self, reads, writes, e=None):
        deps = []
        for r in reads:
            if r.last_w is not None:
                deps.append(r.last_w)
            if r.excl:
                deps.extend(t for k, t in r.readers.items() if k != e)
        for w in writes:
            if w.last_w is not None:
                deps.append(w.last_w)
            deps.extend(w.readers.values())
        return deps

    def _wait(self, e, deps):
        best = {}
        for s, v, key in deps:
            if key not in best or best[key][1] < v:
                best[key] = (s, v)
        for key, (s, v) in best.items():
            if self.waited[e].get(key, 0) < v:
                self.eng[e].wait_ge(s, v)
                self.waited[e][key] = v

    def _mark(self, tok, reads, writes):
        for w in writes:
            w.last_w = tok
            w.readers = {}
        for r in reads:
            if r not in writes:
                r.readers[tok[2]] = tok

    def op(self, e, fn, reads=(), writes=()):
        deps = self._deps(reads, writes, e)
        if e == "pe":
            deps = [d for d in deps if d[2] != "pe"]
        self._wait(e, deps)
        ins = fn(self.eng[e])
        self.cnt[e] += 1
        ins.then_inc(self.sem[e], 1)
        tok = (self.sem[e], self.cnt[e], e)
        self._mark(tok, reads, writes)
        return tok

    def dma(self, q, ch, out, in_, reads=(), writes=(), **kw):
        if ch not in self.chans:
            self.chans[ch] = [self.es.enter_context(self.nc.semaphore("ch_" + ch)), 0]
        c = self.chans[ch]
        deps = self._deps(reads, writes)
        if c[1] > 0:
            deps.append((c[0], c[1], "ch_" + ch))
        self._wait(q, deps)
        self.eng[q].dma_start(out=out, in_=in_, **kw).then_inc(c[0], 16)
        c[1] += 16
        tok = (c[0], c[1], "ch_" + ch)
        self._mark(tok, reads, writes)
        return tok

    def barrier(self, engines=("pe", "act", "dve", "pool", "sp")):
        deps = [(self.sem[k], self.cnt[k], k) for k in self.sem if self.cnt[k] > 0]
        deps += [(c[0], c[1], "ch_" + n) for n, c in self.chans.items() if c[1] > 0]
        for e in engines:
            self._wait(e, deps)


def _fm(v, nch):
    v = np.asarray(v, np.float32)
    lead = v.shape[:-1]
    return np.moveaxis(v.reshape(lead + (nch, 128)), -1, 0)


class SmallPack:
    def __init__(self):
        self.parts = []
        self.off = {}
        self.n = 0

    def add(self, name, arr):
        arr = np.ascontiguousarray(arr, np.float32).reshape(128, -1)
        self.off[name] = self.n
        self.parts.append(arr)
        self.n += arr.shape[1]

    def array(self):
        return np.ascontiguousarray(np.concatenate(self.parts, axis=1))


def small_layout():
    off = {}
    n = 0
    for name, w in (("c2", 16), ("ng2", DEPTH * 2 * 8 * 2), ("ffn_wdw", DEPTH * 44 * 3), ("ffn_bdw", DEPTH * 44),
                    ("cv_b1", 2 * 16), ("cv_wdw", 2 * 8 * CW), ("cv_bdw", 2 * 8), ("cv_lng", 2 * 8), ("cv_lnb", 2 * 8), ("sink", 2 * 16)):
        off[name] = (n, w)
        n += w
    return off, n


def pack_small(inp, b):
    sp = SmallPack()
    c2 = np.stack([_fm(inp["c"][b], 8), _fm(inp["c_ctx"], 8)], axis=-1)
    sp.add("c2", c2)
    ng = np.stack([_fm(inp["norm1_g"], 8), _fm(inp["norm2_g"], 8)], axis=2)
    ng2 = np.repeat(ng[..., None], 2, axis=-1)
    sp.add("ng2", ng2)
    wdw = _fm(inp["ffn_w_dw"], 44)
    sp.add("ffn_wdw", np.transpose(wdw, (0, 1, 3, 2)))
    sp.add("ffn_bdw", _fm(inp["ffn_b_dw"], 44))
    sp.add("cv_b1", _fm(inp["conv_b_pw1"], 16))
    cw = _fm(inp["conv_w_dw"], 8)
    sp.add("cv_wdw", np.transpose(cw, (0, 1, 3, 2)))
    sp.add("cv_bdw", _fm(inp["conv_b_dw"], 8))
    sp.add("cv_lng", _fm(inp["conv_ln_g"], 8))
    sp.add("cv_lnb", _fm(inp["conv_ln_b"], 8))
    sp.add("sink", np.broadcast_to(np.asarray(inp["attn_sink"], np.float32).reshape(1, 32), (128, 32)))
    off, n = small_layout()
    assert sp.n == n and all(sp.off[k] == off[k][0] for k in off)
    return sp.array()


class Prog:
    def __init__(self, S, phases=None, U=256, debug=False):
        self.S = S
        self.U = U
        self.phases = phases
        self.debug = debug
        self.nc = nc = bass.Bass("TRN2", target_bir_lowering=False)
        dt = nc.dram_tensor
        self.off, self.NS = small_layout()
        self.x = dt("x", [S, D], F32, kind="ExternalInput").ap()
        self.ctx = dt("ctx", [CTX, D], F32, kind="ExternalInput").ap()
        self.small_d = dt("small", [128, self.NS], F32, kind="ExternalInput").ap()
        self.ident_d = dt("ident", [128, 128], F32, kind="ExternalInput").ap()
        self.ada_w = dt("ada_w", [DEPTH, D, 6 * D], F32, kind="ExternalInput").ap()
        self.ada_b = dt("ada_b", [DEPTH, 6 * D], F32, kind="ExternalInput").ap()
        self.final_g = dt("final_g", [1, D], F32, kind="ExternalInput").ap()
        self.w_up = dt("ffn_w_up", [DEPTH, D, 2 * DFF], F32, kind="ExternalInput").ap()
        self.w_dn = dt("ffn_w_down", [DEPTH, DFF, D], F32, kind="ExternalInput").ap()
        self.cv_w1 = dt("conv_w_pw1", [2, D, 2 * D], F32, kind="ExternalInput").ap()
        self.cv_w2 = dt("conv_w_pw2", [2, D, D], F32, kind="ExternalInput").ap()
        self.cv_b2 = dt("conv_b_pw2", [2, D], F32, kind="ExternalInput").ap()
        self.at_wqkv = dt("attn_w_qkv", [2, D, 1536], F32, kind="ExternalInput").ap()
        self.at_wo = dt("attn_w_o", [2, D, D], F32, kind="ExternalInput").ap()
        self.masks_d = dt("masks", [128, 2, 512], F32, kind="ExternalInput").ap()
        self.rope_d = dt("rope", [128, S // 128, 64], F32, kind="ExternalInput").ap()
        self.out = dt("out", [S, D], F32, kind="ExternalOutput").ap()
        self.hbuf = [dt("hA", [S + CTX, D], F32).ap(), dt("hB", [S + CTX, D], F32).ap()]
        self.modd = dt("modd", [DEPTH, 2, 6 * D], F32).ap()
        if debug:
            self.dbg = dt("dbg", [CTX, D], F32, kind="ExternalOutput").ap()

    def uniq(self, name):
        self._uid = getattr(self, "_uid", 0) + 1
        return f"t{self._uid}_{name}"

    def sm(self, name, idx, width=1):
        o = self.off[name][0] + idx
        return self.small[:, o:o + width]

    def rows(self, buf, seq, r0, n):
        if buf == "in":
            src = self.x if seq == 0 else self.ctx
            return src[r0:r0 + n, :]
        if buf == "out":
            return self.out[r0:r0 + n, :]
        base = 0 if seq == 0 else self.S
        return self.hbuf[buf][base + r0:base + r0 + n, :]

    def build(self):
        nc = self.nc
        with contextlib.ExitStack() as es:
            self.es = es
            self.sch = sch = Sched(nc, es)
            T = lambda name, shape, dtp: es.enter_context(nc.sbuf_tensor(self.uniq(name), shape, dtp))
            self.small = T("small", [128, self.NS], F32)
            self.identf = T("identf", [128, 128], F32)
            self.identb = T("identb", [128, 128], BF16)
            self.epsT = T("epsT", [128, 4], F32)
            self.FM = T("FM", [128, DEPTH * 96], F32)
            self.GM = T("GM", [128, DEPTH * 2 * 16], F32)
            self.smallR = Res("small")
            self.identR = Res("ident")
            self.fmR = Res("FM")
            self.gmR = Res("GM")
            sch.dma("sp", "small", out=self.small[:], in_=self.small_d, writes=[self.smallR])
            sch.dma("sp", "identf", out=self.identf[:], in_=self.ident_d, writes=[self.identR])
            sch.dma("pool", "identb", out=self.identb[:], in_=self.ident_d, writes=[self.identR])
            epsR = Res("eps")
            sch.op("dve", lambda e: e.memset(self.epsT[:, 0:1], EPS), writes=[epsR])
            sch.op("dve", lambda e: e.memset(self.epsT[:, 1:2], 0.0), writes=[epsR])
            sch.op("dve", lambda e: e.memset(self.epsT[:, 2:3], -0.5), writes=[epsR])
            sch.barrier()
            self.prologue()
            phases = []
            src = "in"
            for l in range(DEPTH):
                phases.append(("mix", l, src, 1))
                phases.append(("ffn", l, 1, 0))
                src = 0
            phases.append(("final", 0, 0, "out"))
            if self.phases is not None:
                phases = self.phases
            for kind, l, s_, d_ in phases:
                if kind == "mix":
                    if l % 2 == 0:
                        self.phase_conv(l, s_, d_)
                    else:
                        self.phase_attn(l, s_, d_)
                elif kind == "ffn":
                    self.phase_ffn(l, s_, d_)
                else:
                    self.phase_final(s_, d_)
                sch.barrier()
            if self.debug:
                lastbuf = [p for p in phases if p[0] != "final"][-1][3]
                sch.dma("sp", "dbg", out=self.dbg, in_=self.hbuf[lastbuf][self.S:self.S + CTX, :])
            sch.barrier()
        return nc

    def prologue(self):
        nc, sch = self.nc, self.sch
        with contextlib.ExitStack() as pes:
            T = lambda name, shape, dtp: pes.enter_context(nc.sbuf_tensor(self.uniq(name), shape, dtp))
            P = lambda name, shape, dtp: pes.enter_context(nc.psum_tensor(self.uniq(name), shape, dtp))
            scT = T("scT", [128, 16], F32)
            wch = [T(f"adaw{i}", [128, 8, 512], F32) for i in range(2)]
            wchR = [Res(f"adaw{i}") for i in range(2)]
            adab = T("adab", [2, 6 * D], F32)
            modrow = T("modrow", [2, 6 * D], F32)
            psr = [P(f"psr{i}", [2, 512], F32) for i in range(2)]
            psrR = [Res(f"psr{i}") for i in range(2)]
            psf = P("psf", [128, 96], F32)
            psfR = Res("psf")
            scR, adabR, modR = Res("scT"), Res("adab"), Res("modrow")
            c0 = self.off["c2"][0]
            sch.op("act", lambda e: e.activation(out=scT[:], in_=self.small[:, c0:c0 + 16], func=AF.Silu),
                   reads=[self.smallR], writes=[scR])
            for l in range(DEPTH):
                for s in range(2):
                    sch.dma("sp", f"adab{s}", out=adab[s:s + 1, :], in_=self.ada_b[l:l + 1, :], writes=[adabR])
                for j in range(12):
                    w, wR = wch[j % 2], wchR[j % 2]
                    sch.dma("sp", f"adaw{j % 2}", out=w[:],
                            in_=self.ada_w[l, :, j * 512:(j + 1) * 512].rearrange("(kc p) n -> p kc n", p=128),
                            writes=[wR])
                    ps, psR = psr[j % 2], psrR[j % 2]

                    def mm(e):
                        for kc in range(8):
                            ins = e.matmul(ps[:], lhsT=scT[:, kc * 2:kc * 2 + 2], rhs=w[:, kc, :],
                                           start=(kc == 0), stop=(kc == 7))
                        return ins
                    sch.op("pe", mm, reads=[scR, wR], writes=[psR])
                    sch.op("dve", lambda e: e.tensor_tensor(out=modrow[:, j * 512:(j + 1) * 512], in0=ps[:],
                                                            in1=adab[:, j * 512:(j + 1) * 512], op=ALU.add),
                           reads=[psR, adabR], writes=[modR])
                sch.dma("sp", "modd", out=self.modd[l], in_=modrow[:], reads=[modR])

                def tr(e):
                    for j in range(48):
                        ins = e.transpose(out=psf[:, j * 2:j * 2 + 2], in_=modrow[0:2, j * 128:(j + 1) * 128],
                                          identity=self.identf[0:2, 0:2])
                    return ins
                sch.op("pe", tr, reads=[modR, self.identR], writes=[psfR])
                sch.op("act", lambda e: e.activation(out=self.FM[:, l * 96:(l + 1) * 96], in_=psf[:], func=AF.Identity),
                       reads=[psfR], writes=[self.fmR])
                for which, vec in ((0, 1), (1, 4)):
                    go = (l * 2 + which) * 16
                    fo = l * 96 + vec * 16
                    no = self.off["ng2"][0] + (l * 2 + which) * 16
                    sch.op("dve", lambda e: e.scalar_tensor_tensor(out=self.GM[:, go:go + 16], in0=self.FM[:, fo:fo + 16],
                                                                   scalar=1.0, in1=self.small[:, no:no + 16],
                                                                   op0=ALU.add, op1=ALU.mult),
                           reads=[self.fmR, self.smallR], writes=[self.gmR])
            sch.barrier()

    def gm(self, l, which, kc, s):
        o = (l * 2 + which) * 16 + kc * 2 + s
        return self.GM[:, o:o + 1]

    def fmv(self, l, vec, kc, s):
        o = l * 96 + vec * 16 + kc * 2 + s
        return self.FM[:, o:o + 1]

    def load_w_bf16(self, name, dst, src3, res, nsplit=1):
        K = dst.shape[1]
        N = dst.shape[2]
        nsplit = max(nsplit, (N + 2047) // 2048)
        step = (N + nsplit - 1) // nsplit
        i = 0
        for kc in range(K):
            for c0 in range(0, N, step):
                c1 = min(N, c0 + step)
                self.sch.dma("pool", f"wld{i % 8}", out=dst[:, kc, c0:c1],
                             in_=src3[kc * 128:(kc + 1) * 128, c0:c1], max_dma_last_dim=8192)
                i += 1

    def alloc_norm(self, pes, U, nb_bt=2, nhin=2, npst=1):
        nc = self.nc
        T = lambda name, shape, dtp: pes.enter_context(nc.sbuf_tensor(self.uniq(name), shape, dtp))
        ns = {}
        ns["HIN"] = [(T(f"hin{i}", [128, D], F32), Res(f"hin{i}")) for i in range(nhin)]
        ns["XH"] = [(T(f"xh{i}", [128, D], BF16), Res(f"xh{i}")) for i in range(2)]
        ns["STAT"] = [(T(f"stat{i}", [128, 4], F32), Res(f"stat{i}")) for i in range(4)]
        ns["BT"] = [(T(f"bt{i}", [128, 8, U], BF16), Res(f"bt{i}")) for i in range(nb_bt)]
        ns["psTs"] = [(pes.enter_context(nc.psum_tensor(self.uniq("psT"), [128, 8, 128], BF16)), Res(f"psT{i}")) for i in range(npst)]
        ns["psT"] = ns["psTs"][0]
        return ns

    def stage_norm_pre(self, ns, m, U, ntile, src, seq, l, which, s):
        sch = self.sch
        tpu = U // 128
        for j in range(tpu):
            n = m * tpu + j
            if n >= ntile:
                break
            nhin = len(ns["HIN"])
            hin, hinR = ns["HIN"][n % nhin]
            xh, xhR = ns["XH"][n % 2]
            st, stR = ns["STAT"][n % 4]
            sch.dma("sp", f"hin{n % nhin}", out=hin[:], in_=self.rows(src, seq, n * 128, 128), writes=[hinR])
            sch.op("act", lambda e: e.activation(out=xh[:], in_=hin[:], func=AF.Square, accum_out=st[:, 0:1]),
                   reads=[hinR], writes=[xhR, stR])
            sch.op("pool", lambda e: e.tensor_scalar(out=st[:, 1:2], in0=st[:, 0:1], scalar1=1.0 / D, scalar2=EPS,
                                                     op0=ALU.mult, op1=ALU.add), reads=[stR], writes=[stR])
            sch.op("pool", lambda e: e.tensor_tensor(out=st[:, 2:3], in0=st[:, 1:2], in1=self.epsT[:, 2:3], op=ALU.pow),
                   reads=[stR], writes=[stR])
            sch.op("dve", lambda e: e.tensor_scalar(out=xh[:], in0=hin[:], scalar1=st[:, 2:3], scalar2=None,
                                                    op0=ALU.mult), reads=[hinR, stR], writes=[xhR])

    def stage_norm_post(self, ns, m, U, ntile, src, seq, l, which, s):
        sch = self.sch
        tpu = U // 128
        bt, btR = ns["BT"][m % len(ns["BT"])]
        vsh = 0 if which == 0 else 3
        for j in range(tpu):
            n = m * tpu + j
            if n >= ntile:
                break
            xh, xhR = ns["XH"][n % 2]
            psT, psTR = ns["psTs"][n % len(ns["psTs"])]

            def tr(e):
                for kc in range(8):
                    ins = e.transpose(out=psT[:, kc, :], in_=xh[:, kc * 128:(kc + 1) * 128],
                                      identity=self.identb[:])
                return ins
            sch.op("pe", tr, reads=[xhR], writes=[psTR])
            for kc in range(8):
                sch.op("act", lambda e: e.activation(out=bt[:, kc, j * 128:(j + 1) * 128], in_=psT[:, kc, :], func=AF.Identity,
                                                     scale=self.gm(l, which, kc, s), bias=self.fmv(l, vsh, kc, s)),
                       reads=[psTR], writes=[btR])
        return bt, btR

    def phase_ffn(self, l, src, dst):
        nc, sch = self.nc, self.sch
        U = self.U
        with contextlib.ExitStack() as pes:
            T = lambda name, shape, dtp: pes.enter_context(nc.sbuf_tensor(self.uniq(name), shape, dtp))
            wup = T("wup", [128, 8, 2 * DFF], BF16)
            wdn = T("wdn", [128, NPAIR, D], BF16)
            wupR, wdnR = Res("wup"), Res("wdn")
            self.load_w_bf16("wup", wup, self.w_up[l], wupR, nsplit=2)
            self.load_w_bf16("wdn", wdn, self.w_dn[l], wdnR)
            sch.barrier()
            seqs = [(1, CTX), (0, self.S)] if l < DEPTH - 1 else [(0, self.S)]
            for seq, Tn in seqs:
                fold = (seq == 0)
                if fold:
                    self.fold_gate(l, 5, [(wdn[:, kc, :], wdnR, 128) for kc in range(NPAIR)])
                with contextlib.ExitStack() as ses:
                    self.ffn_seq(ses, l, seq, Tn, src, dst, wup, wupR, wdn, wdnR, fold=fold)
                    sch.barrier()

    def ffn_seq(self, pes, l, s, Tn, src, dst, wup, wupR, wdn, wdnR, fold=False):
        nc, sch = self.nc, self.sch
        U = self.U
        T = lambda name, shape, dtp: pes.enter_context(nc.sbuf_tensor(self.uniq(name), shape, dtp))
        P = lambda name, shape, dtp: pes.enter_context(nc.psum_tensor(self.uniq(name), shape, dtp))
        NU = (Tn + U - 1) // U
        NT = Tn // 128
        tpu = U // 128
        ns = self.alloc_norm(pes, U, nhin=1, npst=2)
        GTS = [(T(f"GT{i}", [128, NPAIR, U + 1], BF16), Res(f"gt{i}")) for i in range(2)]
        CARRY = [T(f"carry{i}", [128, NPAIR, 2, 2], F32) for i in range(2)]
        carryR = [[Res(f"carry{i}_{c}") for c in range(NPAIR)] for i in range(2)]
        NTB = 6
        TB = [(T(f"tb{i}", [128, U + 1], F32), Res(f"tb{i}")) for i in range(NTB)]
        NUC = 3 if fold else 2
        UC = [(T(f"uc{i}", [128, 2, U + 3], F32), Res(f"uc{i}"), Res(f"ucc{i}")) for i in range(NUC)]
        NSA = 3 if fold else 2
        SA = [(T(f"sa{i}", [128, U + 1], F32), Res(f"sa{i}")) for i in range(NSA)]
        PSY = [(P(f"psy{i}", [128, D], F32), Res(f"psy{i}")) for i in range(1)]
        NPS = 4
        PSAB = [(P(f"psab{i}", [128, 2, U], F32), Res(f"psab{i}")) for i in range(NPS)]
        NHR = 3 if fold else 1
        HRES = [(T(f"hres{i}", [128, D], F32), Res(f"hres{i}")) for i in range(NHR)]
        if not fold:
            TMP = [(T("tmp0", [128, D], F32), Res("tmp0"))]
            gbc = T("gbc", [128, D], F32)
            gbcR = Res("gbc")
            sch.dma("sp", "gbc", out=gbc[:], in_=self.modd[l, s:s + 1, 5 * D:6 * D].partition_broadcast(128), writes=[gbcR])
        sch.op("pool", lambda e: e.memset(CARRY[0][:], 0.0), writes=carryR[0])
        for uc, ucR, uccR in UC:
            sch.op("pool", lambda e: e.memset(uc[:], 0.0), writes=[ucR, uccR])
        wo = self.off["ffn_wdw"][0] + l * 132
        bo = self.off["ffn_bdw"][0] + l * 44
        W3 = lambda cc, k: self.small[:, wo + cc * 3 + k: wo + cc * 3 + k + 1]
        B3 = lambda cc: self.small[:, bo + cc: bo + cc + 1]

        tail = [None]

        def stageU(m, c_lo=0, c_hi=NPAIR):
            last = (m == NU - 1)
            ext = 1 if last else 0
            bt, btR = ns["BT"][m % 2]
            cp, cpR = CARRY[m % 2], carryR[m % 2]
            cn, cnR = CARRY[(m + 1) % 2], carryR[(m + 1) % 2]
            GT, gR = GTS[m % 2]
            for c in range(c_lo, c_hi):
                it = m * NPAIR + c
                ps, psR = PSAB[it % NPS]

                def mm(e):
                    for hf in range(2):
                        col = hf * DFF + c * 128
                        for kc in range(8):
                            ins = e.matmul(ps[:, hf, :], lhsT=wup[:, kc, col:col + 128], rhs=bt[:, kc, :],
                                           start=(kc == 0), stop=(kc == 7))
                    return ins
                sch.op("pe", mm, reads=[btR, wupR], writes=[psR])
                tb = [TB[(2 * it) % NTB], TB[(2 * it + 1) % NTB]]
                uc, ucR, uccR = UC[it % NUC]
                sch.op("pool", lambda e: e.tensor_copy(out=uc[:, :, 0:2], in_=cp[:, c, :, :]), reads=[cpR[c]], writes=[uccR])
                sch.op("act", lambda e: e.activation(out=uc[:, :, 2:2 + U], in_=ps[:, :, :], func=AF.Identity),
                       reads=[psR], writes=[ucR])
                for hf in range(2):
                    cc = hf * NPAIR + c
                    Tt, TtR = tb[hf]
                    sch.op("act", lambda e: e.activation(out=Tt[:, 0:U], in_=ps[:, hf, :], func=AF.Identity,
                                                         scale=W3(cc, 2), bias=B3(cc)),
                           reads=[psR, self.smallR], writes=[TtR])
                    if ext:
                        sch.op("act", lambda e: e.activation(out=Tt[:, U:U + 1], in_=self.epsT[:, 1:2], func=AF.Identity,
                                                             scale=1.0, bias=B3(cc)), reads=[], writes=[TtR])
                if not last:
                    sch.op("pool", lambda e: e.tensor_copy(out=cn[:, c, :, :], in_=uc[:, :, U:U + 2]), reads=[ucR], writes=[cnR[c]])
                for k, o in ((1, 1), (0, 0)):
                    for hf in range(2):
                        cc = hf * NPAIR + c
                        Tt, TtR = tb[hf]
                        sch.op("dve", lambda e: e.scalar_tensor_tensor(out=Tt[:, 0:U + ext], in0=uc[:, hf, o:o + U + ext],
                                                                       scalar=W3(cc, k), in1=Tt[:, 0:U + ext],
                                                                       op0=ALU.mult, op1=ALU.add),
                               reads=[ucR, uccR, TtR], writes=[TtR])
                if tail[0] is not None:
                    tail[0]()

                def tail_fn(it=it, c=c, tb=tb, ext=ext, GT=GT, gR=gR):
                    sa, saR = SA[it % NSA]
                    sch.op("act", lambda e: e.activation(out=sa[:, 0:U + ext], in_=tb[0][0][:, 0:U + ext], func=AF.Silu),
                           reads=[tb[0][1]], writes=[saR])
                    sch.op("pool", lambda e: e.tensor_tensor(out=GT[:, c, 0:U + ext], in0=sa[:, 0:U + ext],
                                                             in1=tb[1][0][:, 0:U + ext], op=ALU.mult),
                           reads=[saR, tb[1][1]], writes=[gR])
                tail[0] = tail_fn
            if c_hi == NPAIR and tail[0] is not None:
                tail[0]()
                tail[0] = None

        def stageD(m):
            GT, gR = GTS[m % 2]
            last = (m == NU - 1)
            tiles = [(jt * 128, 128) for jt in range(tpu)] + ([(U, 1)] if last else [])
            for ti, (c0, ncol) in enumerate(tiles):
                tok0 = U * m - 1 + c0
                p0 = 1 if tok0 < 0 else 0
                py, pyR = PSY[0]

                def mm(e):
                    for nb in range(2):
                        for kc in range(NPAIR):
                            ins = e.matmul(py[0:ncol, nb * 512:(nb + 1) * 512], lhsT=GT[:, kc, c0:c0 + ncol],
                                           rhs=wdn[:, kc, nb * 512:(nb + 1) * 512], start=(kc == 0), stop=(kc == NPAIR - 1))
                    return ins
                sch.op("pe", mm, reads=[gR, wdnR], writes=[pyR])
                hi = dcount[0] % NHR
                dcount[0] += 1
                hr, hrR = HRES[hi]
                r0 = tok0 + p0
                nr = ncol - p0
                if p0:
                    sch.op("pool", lambda e: e.memset(hr[0:1, :], 0.0), writes=[hrR])
                sch.dma("sp", f"hres{hi}", out=hr[p0:p0 + nr, :], in_=self.rows(src, s, r0, nr), writes=[hrR])
                if fold:
                    sch.op("dve", lambda e: e.tensor_tensor(out=hr[0:ncol, :], in0=py[0:ncol, :], in1=hr[0:ncol, :], op=ALU.add),
                           reads=[pyR, hrR], writes=[hrR])
                else:
                    tm, tmR = TMP[0]
                    sch.op("dve", lambda e: e.tensor_tensor(out=tm[0:ncol, :], in0=py[0:ncol, :], in1=gbc[0:ncol, :], op=ALU.mult),
                           reads=[pyR, gbcR], writes=[tmR])
                    sch.op("pool", lambda e: e.tensor_tensor(out=hr[0:ncol, :], in0=tm[0:ncol, :], in1=hr[0:ncol, :], op=ALU.add),
                           reads=[tmR, hrR], writes=[hrR])
                self.pend_stores.append(
                    lambda hr=hr, hrR=hrR, hi=hi, p0=p0, nr=nr, r0=r0: sch.dma("sp", f"hres{hi}", out=self.rows(dst, s, r0, nr),
                                                                              in_=hr[p0:p0 + nr, :], reads=[hrR]))
                if not fold:
                    self.flush_stores()

        dcount = [0]
        self.pend_stores = []
        HALF = NPAIR // 2
        nargs = (U, NT, src, s, l, 1, s)
        self.stage_norm_pre(ns, 0, *nargs)
        for step in range(NU + 2):
            if step < NU:
                self.stage_norm_post(ns, step, *nargs)
            if 0 <= step - 1 < NU:
                stageU(step - 1, 0, HALF)
            if step + 1 < NU:
                self.stage_norm_pre(ns, step + 1, *nargs)
            if 0 <= step - 2 < NU:
                stageD(step - 2)
            if 0 <= step - 1 < NU:
                stageU(step - 1, HALF, NPAIR)
            self.flush_stores()

    def phase_final(self, src, dst):
        nc, sch = self.nc, self.sch
        with contextlib.ExitStack() as pes:
            T = lambda name, shape, dtp: pes.enter_context(nc.sbuf_tensor(self.uniq(name), shape, dtp))
            fg = T("fg", [128, D], F32)
            fgR = Res("fg")
            sch.dma("sp", "fg", out=fg[:], in_=self.final_g.partition_broadcast(128), writes=[fgR])
            HIN = [(T(f"fh{i}", [128, D], F32), Res(f"fh{i}")) for i in range(3)]
            SQ = [(T(f"fsq{i}", [128, D], F32), Res(f"fsq{i}")) for i in range(2)]
            STAT = [(T(f"fst{i}", [128, 4], F32), Res(f"fst{i}")) for i in range(3)]
            for n in range(self.S // 128):
                hin, hinR = HIN[n % 3]
                sq, sqR = SQ[n % 2]
                st, stR = STAT[n % 3]
                sch.dma("sp", f"fh{n % 3}", out=hin[:], in_=self.rows(src, 0, n * 128, 128), writes=[hinR])
                sch.op("act", lambda e: e.activation(out=sq[:], in_=hin[:], func=AF.Square, accum_out=st[:, 0:1]),
                       reads=[hinR], writes=[sqR, stR])
                sch.op("pool", lambda e: e.tensor_scalar(out=st[:, 1:2], in0=st[:, 0:1], scalar1=1.0 / D, scalar2=EPS,
                                                         op0=ALU.mult, op1=ALU.add), reads=[stR], writes=[stR])
                sch.op("pool", lambda e: e.tensor_tensor(out=st[:, 2:3], in0=st[:, 1:2], in1=self.epsT[:, 2:3], op=ALU.pow),
                       reads=[stR], writes=[stR])
                sch.op("dve", lambda e: e.scalar_tensor_tensor(out=hin[:], in0=hin[:], scalar=st[:, 2:3], in1=fg[:],
                                                               op0=ALU.mult, op1=ALU.mult),
                       reads=[hinR, stR, fgR], writes=[hinR])
                sch.dma("act", f"fh{n % 3}", out=self.rows(dst, 0, n * 128, 128), in_=hin[:], reads=[hinR])

    def flush_stores(self):
        for fn in getattr(self, "pend_stores", []):
            fn()
        self.pend_stores = []

    def resid_stage(self, RS, n, py, pyR, src, dst, seq):
        sch = self.sch
        nh = len(RS["HRES"])
        hr, hrR = RS["HRES"][n % nh]
        sch.dma("sp", f"hres{n % nh}", out=hr[:], in_=self.rows(src, seq, n * 128, 128), writes=[hrR])
        if RS["fold"]:
            sch.op("dve", lambda e: e.tensor_tensor(out=hr[:], in0=py[:], in1=hr[:], op=ALU.add),
                   reads=[pyR, hrR], writes=[hrR])
        else:
            tm, tmR = RS["TMP"][0]
            gbc, gbcR = RS["gbc"]
            sch.op("dve", lambda e: e.tensor_tensor(out=tm[:], in0=py[:], in1=gbc[:], op=ALU.mult),
                   reads=[pyR, gbcR], writes=[tmR])
            sch.op("pool", lambda e: e.tensor_tensor(out=hr[:], in0=tm[:], in1=hr[:], op=ALU.add),
                   reads=[tmR, hrR], writes=[hrR])
        if not hasattr(self, "pend_stores"):
            self.pend_stores = []
        self.pend_stores.append(lambda: sch.dma("sp", f"hres{n % nh}", out=self.rows(dst, seq, n * 128, 128), in_=hr[:], reads=[hrR]))

    def fold_gate(self, l, vec, targets):
        nc, sch = self.nc, self.sch
        with contextlib.ExitStack() as fes:
            gb = fes.enter_context(nc.sbuf_tensor(self.uniq("gfold"), [128, D], F32))
            gbR = Res("gfold")
            sch.dma("sp", "gfold", out=gb[:], in_=self.modd[l, 0:1, vec * D:(vec + 1) * D].partition_broadcast(128), writes=[gbR])
            for i, (ap, res, np_) in enumerate(targets):
                sch.op("dve" if i % 2 == 0 else "pool",
                       lambda e: e.tensor_tensor(out=ap, in0=ap, in1=gb[0:np_, :], op=ALU.mult),
                       reads=[gbR, res], writes=[res])
            sch.barrier()

    def alloc_resid(self, pes, l, vec, s, fold=False):
        nc, sch = self.nc, self.sch
        T = lambda name, shape, dtp: pes.enter_context(nc.sbuf_tensor(self.uniq(name), shape, dtp))
        RS = {"fold": fold}
        RS["HRES"] = [(T(f"hres{i}", [128, D], F32), Res(f"hres{i}")) for i in range(4 if fold else 2)]
        if fold:
            return RS
        RS["TMP"] = [(T("tmp", [128, D], F32), Res("tmp"))]
        gbc = T("gbc", [128, D], F32)
        gbcR = Res("gbc")
        sch.dma("sp", "gbc", out=gbc[:], in_=self.modd[l, s:s + 1, vec * D:(vec + 1) * D].partition_broadcast(128),
                writes=[gbcR])
        RS["gbc"] = (gbc, gbcR)
        return RS

    def phase_conv(self, l, src, dst):
        nc, sch = self.nc, self.sch
        j = l // 2
        with contextlib.ExitStack() as pes:
            T = lambda name, shape, dtp: pes.enter_context(nc.sbuf_tensor(self.uniq(name), shape, dtp))
            w1 = T("w1", [128, 8, 2 * D], BF16)
            w2 = T("w2", [128, 8, D], BF16)
            dg = T("dg", [128, 8, CW, 128], BF16)
            b2r = T("b2r", [1, D], BF16)
            onesr = T("onesr", [1, 128], BF16)
            onesm = T("onesm", [128, 128], BF16)
            w1R, w2R, dgR, cR = Res("w1"), Res("w2"), Res("dg"), Res("cconst")
            self.load_w_bf16("w1", w1, self.cv_w1[j], w1R)
            self.load_w_bf16("w2", w2, self.cv_w2[j], w2R)
            sch.barrier()
            sch.dma("pool", "b2r", out=b2r[:], in_=self.cv_b2[j:j + 1, :], writes=[cR])
            sch.op("dve", lambda e: e.memset(onesr[:], 1.0), writes=[cR])
            sch.op("dve", lambda e: e.memset(onesm[:], 1.0), writes=[cR])
            wo = self.off["cv_wdw"][0] + j * 8 * CW
            for kc in range(8):
                for k in range(CW):
                    o = wo + kc * CW + k
                    sch.op("dve" if (k % 2 == 0) else "pool",
                           lambda e: e.tensor_scalar(out=dg[:, kc, k, :], in0=self.identb[:], scalar1=self.small[:, o:o + 1],
                                                     scalar2=None, op0=ALU.mult), reads=[self.identR], writes=[dgR])
            W = dict(w1=w1, w2=w2, dg=dg, b2r=b2r, onesr=onesr, onesm=onesm, w1R=w1R, w2R=w2R, dgR=dgR, cR=cR)
            seqs = [(1, CTX), (0, self.S)] if l < DEPTH - 1 else [(0, self.S)]
            for seq, Tn in seqs:
                if seq == 0:
                    self.fold_gate(l, 2, [(w2[:, kc, :], w2R, 128) for kc in range(8)] + [(b2r[0:1, :], cR, 1)])
                with contextlib.ExitStack() as ses:
                    self.conv_seq(ses, l, seq, Tn, src, dst, W, fold=(seq == 0))
                    sch.barrier()

    def conv_seq(self, pes, l, s, Tn, src, dst, W, fold=False):
        nc, sch = self.nc, self.sch
        j = l // 2
        U = 256
        HL = 16
        T = lambda name, shape, dtp: pes.enter_context(nc.sbuf_tensor(self.uniq(name), shape, dtp))
        P = lambda name, shape, dtp: pes.enter_context(nc.psum_tensor(self.uniq(name), shape, dtp))
        NU = Tn // U
        NT = Tn // 128
        tpu = U // 128
        ns = self.alloc_norm(pes, U)
        RS = self.alloc_resid(pes, l, 2, s, fold=fold)
        PSY = (P("psy", [128, D], F32), Res("psy"))
        PSAG = [(P(f"psag{i}", [128, 2, U], F32), Res(f"psag{i}")) for i in range(2)]
        PSC = [(P(f"psc{i}", [128, 512], F32)[:, 0:U], Res(f"psc{i}")) for i in range(2)]
        PSS = (P("pss", [128, 2, U], F32), Res("pss"))
        NV = min(2, NU)
        NB2 = min(2, NU)
        V = [(T(f"v{i}", [128, 8, U + 2 * HL], BF16), Res(f"v{i}")) for i in range(NV)]
        SG = [(T(f"sg{i}", [128, U], F32), Res(f"sg{i}")) for i in range(2)]
        WCS = [(T(f"wc{i}", [128, 8, U], F32), [Res(f"wc{i}_{k}") for k in range(8)]) for i in range(NB2)]
        WB = (T("wb", [128, 8, U], BF16), [Res(f"wb{k}") for k in range(8)])
        WS = (T("wsq", [128, 8, U], BF16), [Res(f"wsq{k}") for k in range(8)])
        MEANS = [(T(f"mean{i}", [128, U], F32), Res(f"mean{i}")) for i in range(NB2)]
        MSQS = [(T(f"msq{i}", [128, U], F32), Res(f"msq{i}")) for i in range(1)]
        RSTDS = [(T(f"rstd{i}", [128, U], F32), Res(f"rstd{i}")) for i in range(NB2)]
        ZT = [(T(f"zt{i}", [128, 8, U], BF16), Res(f"zt{i}")) for i in range(NB2)]
        b1o = self.off["cv_b1"][0] + j * 16
        B1 = lambda c: self.small[:, b1o + c:b1o + c + 1]
        bdo = self.off["cv_bdw"][0] + j * 8
        lgo = self.off["cv_lng"][0] + j * 8
        lbo = self.off["cv_lnb"][0] + j * 8
        w1, w2, dg = W["w1"], W["w2"], W["dg"]

        def stageA(m):
            bt, btR = ns["BT"][m % 2]
            v, vR = V[m % NV]
            for c in range(8):
                ps, psR = PSAG[(m * 8 + c) % 2]

                def mm(e):
                    for hf in range(2):
                        col = hf * D + c * 128
                        for kc in range(8):
                            ins = e.matmul(ps[:, hf, :], lhsT=w1[:, kc, col:col + 128], rhs=bt[:, kc, :],
                                           start=(kc == 0), stop=(kc == 7))
                    return ins
                sch.op("pe", mm, reads=[btR, W["w1R"]], writes=[psR])
                sg, sgR = SG[(m * 8 + c) % 2]
                sch.op("act", lambda e: e.activation(out=sg[:], in_=ps[:, 1, :], func=AF.Sigmoid, bias=B1(8 + c), scale=1.0),
                       reads=[psR], writes=[sgR])
                sch.op("dve", lambda e: e.scalar_tensor_tensor(out=v[:, c, HL:HL + U], in0=ps[:, 0, :], scalar=B1(c),
                                                               in1=sg[:], op0=ALU.add, op1=ALU.mult),
                       reads=[psR, sgR], writes=[vR])
            if m == 0:
                sch.op("pool", lambda e: e.memset(v[:, :, 0:HL], 0.0), writes=[vR])
            else:
                vp, vpR = V[(m - 1) % NV]
                sch.op("pool", lambda e: e.tensor_copy(out=vp[:, :, HL + U:HL + U + HL], in_=v[:, :, HL:2 * HL]),
                       reads=[vR], writes=[vpR])
                sch.op("pool", lambda e: e.tensor_copy(out=v[:, :, 0:HL], in_=vp[:, :, U:U + HL]),
                       reads=[vpR], writes=[vR])
            if m == NU - 1:
                sch.op("pool", lambda e: e.memset(v[:, :, HL + U:HL + U + HL], 0.0), writes=[vR])

        def stageB1(m):
            v, vR = V[m % NV]
            wc, wcR = WCS[m % NB2]
            wb, wbR = WB
            wsq, wsR = WS
            for kc in range(8):
                pc, pcR = PSC[kc % 2]

                def mm(e):
                    for k in range(CW):
                        ins = e.matmul(pc[:, :], lhsT=dg[:, kc, k, :], rhs=v[:, kc, 1 + k:1 + k + U],
                                       start=(k == 0), stop=(k == CW - 1))
                    return ins
                sch.op("pe", mm, reads=[vR, W["dgR"]], writes=[pcR])
                bd = self.small[:, bdo + kc:bdo + kc + 1]
                sch.op("act", lambda e: e.activation(out=wc[:, kc, :], in_=pc[:, :], func=AF.Identity, bias=bd, scale=1.0),
                       reads=[pcR], writes=[wcR[kc]])
                sch.op("act", lambda e: e.activation(out=wsq[:, kc, :], in_=pc[:, :], func=AF.Square, bias=bd, scale=1.0),
                       reads=[pcR], writes=[wsR[kc]])
                sch.op("pool", lambda e: e.tensor_copy(out=wb[:, kc, :], in_=wc[:, kc, :]), reads=[wcR[kc]], writes=[wbR[kc]])
            pss, pssR = PSS

            def mm2(e):
                for q, (src_, _) in enumerate(((wb, wbR), (wsq, wsR))):
                    for kc in range(8):
                        ins = e.matmul(pss[:, q, :], lhsT=W["onesm"][:], rhs=src_[:, kc, :], start=(kc == 0), stop=(kc == 7))
                return ins
            sch.op("pe", mm2, reads=wbR + wsR + [W["cR"]], writes=[pssR])
            mean, meanR = MEANS[m % NB2]
            msq, msqR = MSQS[0]
            rstd, rstdR = RSTDS[m % NB2]
            sch.op("act", lambda e: e.activation(out=mean[:], in_=pss[:, 0, :], func=AF.Identity, scale=1.0 / D),
                   reads=[pssR], writes=[meanR])
            sch.op("dve", lambda e: e.tensor_tensor(out=msq[:], in0=mean[:], in1=mean[:], op=ALU.mult),
                   reads=[meanR], writes=[msqR])
            sch.op("dve", lambda e: e.scalar_tensor_tensor(out=msq[:], in0=pss[:, 1, :], scalar=1.0 / D, in1=msq[:],
                                                           op0=ALU.mult, op1=ALU.subtract),
                   reads=[pssR, msqR], writes=[msqR])
            sch.op("act", lambda e: e.activation(out=rstd[:], in_=msq[:], func=AF.Sqrt, bias=self.epsT[:, 0:1], scale=1.0),
                   reads=[msqR], writes=[rstdR])
            sch.op("dve", lambda e: e.reciprocal(out=rstd[:], in_=rstd[:]), reads=[rstdR], writes=[rstdR])

        def stageB2(m):
            wc, wcR = WCS[m % NB2]
            zt, ztR = ZT[m % NB2]
            mean, meanR = MEANS[m % NB2]
            rstd, rstdR = RSTDS[m % NB2]
            for kc in range(8):
                sch.op("dve", lambda e: e.tensor_tensor(out=wc[:, kc, :], in0=wc[:, kc, :], in1=mean[:], op=ALU.subtract),
                       reads=[wcR[kc], meanR], writes=[wcR[kc]])
            for kc in range(8):
                sch.op("pool" if kc % 2 else "dve",
                       lambda e: e.tensor_tensor(out=wc[:, kc, :], in0=wc[:, kc, :], in1=rstd[:], op=ALU.mult),
                       reads=[wcR[kc], rstdR], writes=[wcR[kc]])
            for kc in range(8):
                sch.op("act", lambda e: e.activation(out=zt[:, kc, :], in_=wc[:, kc, :], func=AF.Silu,
                                                     scale=self.small[:, lgo + kc:lgo + kc + 1],
                                                     bias=self.small[:, lbo + kc:lbo + kc + 1]),
                       reads=[wcR[kc]], writes=[ztR])

        def stageB2b(m):
            zt, ztR = ZT[m % NB2]
            for jt in range(tpu):
                n = m * tpu + jt
                py, pyR = PSY

                def mm3(e):
                    for nb in range(2):
                        e.matmul(py[:, nb * 512:(nb + 1) * 512], lhsT=W["onesr"][0:1, :],
                                 rhs=W["b2r"][0:1, nb * 512:(nb + 1) * 512], start=True, stop=False)
                        for kc in range(8):
                            ins = e.matmul(py[:, nb * 512:(nb + 1) * 512], lhsT=zt[:, kc, jt * 128:(jt + 1) * 128],
                                           rhs=w2[:, kc, nb * 512:(nb + 1) * 512], start=False, stop=(kc == 7))
                    return ins
                sch.op("pe", mm3, reads=[ztR, W["w2R"], W["cR"]], writes=[pyR])
                self.resid_stage(RS, n, py, pyR, src, dst, s)

        nargs = (U, NT, src, s, l, 0, s)
        self.stage_norm_pre(ns, 0, *nargs)
        for step in range(NU + 3):
            self.flush_stores()
            if step < NU:
                self.stage_norm_post(ns, step, *nargs)
            if 0 <= step - 3 < NU:
                stageB2(step - 3)
            if 0 <= step - 1 < NU:
                stageA(step - 1)
            if step + 1 < NU:
                self.stage_norm_pre(ns, step + 1, *nargs)
            if 0 <= step - 2 < NU:
                stageB1(step - 2)
            if 0 <= step - 3 < NU:
                stageB2b(step - 3)
        self.flush_stores()


    def phase_attn(self, l, src, dst):
        nc, sch = self.nc, self.sch
        j = l // 2
        with_ctx_out = l < DEPTH - 1
        with contextlib.ExitStack() as pes:
            T = lambda name, shape, dtp: pes.enter_context(nc.sbuf_tensor(self.uniq(name), shape, dtp))
            wqkv = T("wqkv", [128, 8, 1536], BF16)
            wo = T("wo", [128, 8, D], BF16)
            masks = T("masks", [128, 2, 512], BF16)
            rope = T("rope", [128, self.S // 128, 64], F32)
            esink = T("esink", [128, 16], F32)
            wqR, woR, cR = Res("wqkv"), Res("wo"), Res("aconst")
            self.load_w_bf16("wqkv", wqkv, self.at_wqkv[j], wqR)
            self.load_w_bf16("wo", wo, self.at_wo[j], woR)
            sch.barrier()
            sch.dma("pool", "masks", out=masks[:], in_=self.masks_d, writes=[cR])
            sch.dma("sp", "rope", out=rope[:], in_=self.rope_d, writes=[cR])
            so = self.off["sink"][0] + j * 16
            sch.op("act", lambda e: e.activation(out=esink[:], in_=self.small[:, so:so + 16], func=AF.Exp),
                   reads=[self.smallR], writes=[cR])
            ckt = [(T(f"ckt{i}", [128, 2, 128], BF16), Res(f"ckt{i}")) for i in range(2)]
            cva = [(T(f"cva{i}", [128, 4, 65], BF16), Res(f"cva{i}")) for i in range(2)]
            W = dict(wqkv=wqkv, wo=wo, masks=masks, rope=rope, esink=esink, wqR=wqR, woR=woR, cR=cR, ckt=ckt, cva=cva)
            for seq, Tn in [(1, CTX), (0, self.S)]:
                if seq == 0:
                    self.fold_gate(l, 2, [(wo[:, kc, :], woR, 128) for kc in range(8)])
                with contextlib.ExitStack() as ses:
                    self.attn_seq(ses, l, seq, Tn, src, dst, W, with_ctx_out, fold=(seq == 0))
                    sch.barrier()

    def attn_seq(self, pes, l, s, Tn, src, dst, W, with_ctx_out, fold=False):
        nc, sch = self.nc, self.sch
        T = lambda name, shape, dtp: pes.enter_context(nc.sbuf_tensor(self.uniq(name), shape, dtp))
        P = lambda name, shape, dtp: pes.enter_context(nc.psum_tensor(self.uniq(name), shape, dtp))
        NT = Tn // 128
        is_ctx = (s == 1)
        do_out = (not is_ctx) or with_ctx_out
        psQ = (P("psq", [128, 1536], F32), Res("psq01"))
        pkvR = Res("psq2")
        PSS = [(P(f"pss{i}", [128, 512], F32), Res(f"pss{i}")) for i in range(3)]
        ns = self.alloc_norm(pes, 128)
        psT, psTR = ns["psT"]
        PSO = []
        bankx = P("pso0", [128, 512], F32)
        psO0 = (bankx, Res("pso0"))
        psKT = (bankx[:, 384:512].bitcast(BF16).rearrange("p (c t) -> p c t", c=2), psO0[1])
        PSO = [psO0]
        RS = self.alloc_resid(pes, l, 2, s, fold=fold)
        R = 5
        if is_ctx:
            KT, VA = W["ckt"], W["cva"]
            R = 2
        else:
            KT = [(T(f"kt{i}", [128, 2, 128], BF16), Res(f"kt{i}")) for i in range(R)]
            VA = [(T(f"va{i}", [128, 4, 65], BF16), Res(f"va{i}")) for i in range(R)]
        for va, vaR in VA:
            sch.op("pool", lambda e: e.memset(va[:], 1.0), writes=[vaR])
        QT = [(T(f"qt{i}", [128, 8, 128], BF16), Res(f"qt{i}")) for i in range(3)]
        QR = [(T(f"qr{i}", [128, 1024], BF16), Res(f"qr{i}")) for i in range(2)]
        KR = [(T(f"kr{i}", [128, 256], BF16), Res(f"kr{i}")) for i in range(2)]
        TQ = [(T(f"tq{i}", [128, 16, 32], F32), Res(f"tq{i}")) for i in range(2)]
        TK = [(T(f"tk{i}", [128, 4, 32], F32), Res(f"tk{i}")) for i in range(2)]
        PT = [(T(f"pt{i}", [128, 512], BF16), Res(f"pt{i}")) for i in range(5)]
        OS = [(T(f"os{i}", [128, 260], F32), Res(f"os{i}")) for i in range(2)]
        OB = [(T(f"ob{i}", [128, 1024], BF16), Res(f"ob{i}")) for i in range(2)]
        OT = [(T(f"ot{i}", [128, 8, 128], BF16), Res(f"ot{i}")) for i in range(2)]
        DEN = [(T(f"den{i}", [128, 8], F32), Res(f"den{i}")) for i in range(2)]
        wqkv, wo, masks, rope, esink = W["wqkv"], W["wo"], W["masks"], W["rope"], W["esink"]
        pq, pqR = psQ
        ptc = [0]

        def stageQa(n):
            bt, btR = ns["BT"][n % 2]

            def mm(e):
                for nb in range(3):
                    for kc in range(8):
                        ins = e.matmul(pq[:, nb * 512:(nb + 1) * 512], lhsT=bt[:, kc, :], rhs=wqkv[:, kc, nb * 512:(nb + 1) * 512],
                                       start=(kc == 0), stop=(kc == 7))
                return ins
            sch.op("pe", mm, reads=[btR, W["wqR"]], writes=[pqR, pkvR])
            qr, qrR = QR[n % 2]
            kr, krR = KR[n % 2]
            va, vaR = VA[n % R]
            qv = pq[:, 0:1024].rearrange("p (h d) -> p h d", h=16)
            kv = pq[:, 1024:1280].rearrange("p (h d) -> p h d", h=4)
            qrv = qr[:, :].rearrange("p (h d) -> p h d", h=16)
            krv = kr[:, :].rearrange("p (h d) -> p h d", h=4)
            sch.op("act", lambda e: e.activation(out=va[:, :, 0:64], in_=pq[:, 1280:1536].rearrange("p (h d) -> p h d", h=4),
                                                 func=AF.Identity), reads=[pkvR], writes=[vaR])
            if is_ctx:
                if do_out:
                    sch.op("act", lambda e: e.activation(out=qr[:], in_=pq[:, 0:1024], func=AF.Identity), reads=[pqR], writes=[qrR])
                sch.op("act", lambda e: e.activation(out=kr[:], in_=pq[:, 1024:1280], func=AF.Identity), reads=[pkvR], writes=[krR])
            else:
                for (xv, ov, nh, tbufs, oR, pR) in ((kv, krv, 4, TK, krR, pkvR), (qv, qrv, 16, TQ, qrR, pqR)):
                    cosb = rope[:, n, 0:32].unsqueeze(1).to_broadcast([128, nh, 32])
                    sinb = rope[:, n, 32:64].unsqueeze(1).to_broadcast([128, nh, 32])
                    (t1, t1R), (t2, t2R) = tbufs
                    x1, x2 = xv[:, :, 0:32], xv[:, :, 32:64]
                    sch.op("dve", lambda e: e.tensor_tensor(out=t1[:], in0=x1, in1=cosb, op=ALU.mult), reads=[pR, W["cR"]], writes=[t1R])
                    sch.op("dve", lambda e: e.tensor_tensor(out=t2[:], in0=x2, in1=sinb, op=ALU.mult), reads=[pR, W["cR"]], writes=[t2R])
                    sch.op("dve", lambda e: e.tensor_tensor(out=ov[:, :, 0:32], in0=t1[:], in1=t2[:], op=ALU.subtract),
                           reads=[t1R, t2R], writes=[oR])
                    sch.op("dve", lambda e: e.tensor_tensor(out=t1[:], in0=x2, in1=cosb, op=ALU.mult), reads=[pR, W["cR"]], writes=[t1R])
                    sch.op("dve", lambda e: e.tensor_tensor(out=t2[:], in0=x1, in1=sinb, op=ALU.mult), reads=[pR, W["cR"]], writes=[t2R])
                    sch.op("dve", lambda e: e.tensor_tensor(out=ov[:, :, 32:64], in0=t1[:], in1=t2[:], op=ALU.add),
                           reads=[t1R, t2R], writes=[oR])

        def stageQb(n):
            qr, qrR = QR[n % 2]
            kr, krR = KR[n % 2]
            kt, ktR = KT[n % R]
            pk, pkR = psKT

            def trk(e):
                for c in range(2):
                    ins = e.transpose(out=pk[:, c, :], in_=kr[:, c * 128:(c + 1) * 128], identity=self.identb[:])
                return ins
            sch.op("pe", trk, reads=[krR], writes=[pkR])
            sch.op("act", lambda e: e.activation(out=kt[:], in_=pk[:], func=AF.Identity), reads=[pkR], writes=[ktR])
            if do_out:
                qt, qtR = QT[n % 3]

                def trq(e):
                    for c in range(8):
                        ins = e.transpose(out=psT[:, c, :], in_=qr[:, c * 128:(c + 1) * 128], identity=self.identb[:])
                    return ins
                sch.op("pe", trq, reads=[qrR], writes=[psTR])
                sch.op("act", lambda e: e.activation(out=qt[:], in_=psT[:], func=AF.Identity), reads=[psTR], writes=[qtR])

        def stageS(i):
            qt, qtR = QT[i % 3]
            ob, obR = OB[i % 2]
            chunks = []
            chunks.append((W["ckt"][0], W["cva"][0], None))
            chunks.append((W["ckt"][1], W["cva"][1], None))
            if not is_ctx:
                chunks.append((KT[i % R], VA[i % R], None))
                if i - 1 >= 0:
                    chunks.append((KT[(i - 1) % R], VA[(i - 1) % R], 0))
                if i + 1 < NT:
                    chunks.append((KT[(i + 1) % R], VA[(i + 1) % R], 1))
            items = [(g, ci) for g in range(4) for ci in range(len(chunks))]
            LOOK = 2
            bufs = {}

            def score(k):
                g, ci = items[k]
                (kt, ktR), (va, vaR), mk = chunks[ci]
                hp = 64 * (g % 2)
                c0 = 4 * (g // 2)
                ps, psR = PSS[ptc[0] % len(PSS)]
                pt, ptR = PT[ptc[0] % len(PT)]
                ptc[0] += 1
                bufs[k] = (pt, ptR)
                sch.op("pe", lambda e: e.matmul(ps[:], lhsT=kt[hp:hp + 64, g // 2, :], rhs=qt[hp:hp + 64, c0:c0 + 4, :],
                                                start=True, stop=True), reads=[ktR, qtR], writes=[psR])
                sch.op("act", lambda e: e.activation(out=pt[:], in_=ps[:], func=AF.Exp, scale=HD ** -0.5),
                       reads=[psR], writes=[ptR])
                if mk is not None:
                    sch.op("pool", lambda e: e.tensor_tensor(out=pt[:], in0=pt[:], in1=masks[:, mk, :], op=ALU.mult),
                           reads=[ptR, W["cR"]], writes=[ptR])

            def pvs(k):
                g, ci = items[k]
                (kt, ktR), (va, vaR), mk = chunks[ci]
                pt, ptR = bufs.pop(k)
                po, poR = PSO[0]

                def pv(e):
                    for hh in range(4):
                        ins = e.matmul(po[:, hh * 65:(hh + 1) * 65],
                                       lhsT=pt[:, hh * 128:(hh + 1) * 128], rhs=va[:, g, :],
                                       start=(ci == 0 and hh == 0), stop=(ci == len(chunks) - 1 and hh == 3))
                    return ins
                sch.op("pe", pv, reads=[ptR, vaR], writes=[poR])
                if ci == len(chunks) - 1:
                    osb, osR = OS[g % 2]
                    den, denR = DEN[g % 2]
                    sch.op("dve", lambda e: e.tensor_copy(out=osb[:], in_=po[:, 0:260]), reads=[poR], writes=[osR])
                    osv = osb[:, :].rearrange("p (h d) -> p h d", h=4)
                    sch.op("pool", lambda e: e.tensor_tensor(out=den[:, 0:4], in0=osv[:, :, 64], in1=esink[:, 4 * g:4 * g + 4], op=ALU.add),
                           reads=[osR, W["cR"]], writes=[denR])
                    sch.op("dve", lambda e: e.reciprocal(out=den[:, 4:8], in_=den[:, 0:4]), reads=[denR], writes=[denR])
                    obv = ob[:, 256 * g:256 * (g + 1)].rearrange("p (h d) -> p h d", h=4)
                    sch.op("dve", lambda e: e.tensor_tensor(out=obv, in0=osv[:, :, 0:64],
                                                            in1=den[:, 4:8].unsqueeze(2).to_broadcast([128, 4, 64]), op=ALU.mult),
                           reads=[osR, denR], writes=[obR])

            for k in range(len(items) + LOOK):
                if k < len(items):
                    score(k)
                if k - LOOK >= 0:
                    pvs(k - LOOK)

        def stageO(i):
            ob, obR = OB[i % 2]
            ot, otR = OT[i % 2]

            def tro(e):
                for c in range(8):
                    ins = e.transpose(out=psT[:, c, :], in_=ob[:, c * 128:(c + 1) * 128], identity=self.identb[:])
                return ins
            sch.op("pe", tro, reads=[obR], writes=[psTR])
            sch.op("act", lambda e: e.activation(out=ot[:], in_=psT[:], func=AF.Identity), reads=[psTR], writes=[otR])

            def mmo(e):
                for nb in range(2):
                    for kc in range(8):
                        ins = e.matmul(pq[:, nb * 512:(nb + 1) * 512], lhsT=ot[:, kc, :], rhs=wo[:, kc, nb * 512:(nb + 1) * 512],
                                       start=(kc == 0), stop=(kc == 7))
                return ins
            sch.op("pe", mmo, reads=[otR, W["woR"]], writes=[pqR])
            self.resid_stage(RS, i, pq[:, 0:1024], pqR, src, dst, s)

        nargs = (128, NT, src, s, l, 0, s)
        self.stage_norm_pre(ns, 0, *nargs)
        for step in range(NT + 5):
            self.flush_stores()
            if step < NT:
                self.stage_norm_post(ns, step, *nargs)
            if 0 <= step - 1 < NT:
                stageQa(step - 1)
            if step + 1 < NT:
                self.stage_norm_pre(ns, step + 1, *nargs)
            if do_out and 0 <= step - 3 < NT:
                stageS(step - 3)
            if do_out and 0 <= step - 4 < NT:
                stageO(step - 4)
            if 0 <= step - 1 < NT:
                stageQb(step - 1)
        self.flush_stores()


QPERM = np.concatenate([np.concatenate([np.arange(lo * 64, lo * 64 + 64), np.arange(hi * 64, hi * 64 + 64)])
                        for lo, hi in [(c, 4 + c) if c < 4 else (8 + c - 4, 12 + c - 4) for c in range(8)]])


def rope_table(S):
    t = np.arange(S)
    row = (t // 64).astype(np.float32)
    col = (t % 64).astype(np.float32)
    inv = (np.float32(10000.0) ** (-(np.arange(16, dtype=np.float32)) / np.float32(16))).astype(np.float32)
    ang = np.concatenate([row[:, None] * inv, col[:, None] * inv], axis=-1).astype(np.float32)
    tab = np.concatenate([np.cos(ang), np.sin(ang)], axis=-1).astype(np.float32)
    return np.ascontiguousarray(tab.reshape(S // 128, 128, 64).transpose(1, 0, 2))


def mask_table():
    kl = np.arange(128)[:, None]
    ql = np.arange(128)[None, :]
    mprev = (kl >= ql).astype(np.float32)
    mnext = (kl <= ql).astype(np.float32)
    m = np.stack([np.tile(mprev, (1, 4)), np.tile(mnext, (1, 4))], axis=1)
    return np.ascontiguousarray(m)


def make_in_map(inp, b):
    S = inp["x"].shape[1]
    wqkv = np.asarray(inp["attn_w_qkv"], np.float32)
    wqkv_p = np.concatenate([wqkv[:, :, :1024][:, :, QPERM], wqkv[:, :, 1024:]], axis=-1)
    m = {
        "attn_w_qkv": wqkv_p, "attn_w_o": inp["attn_w_o"], "masks": mask_table(), "rope": rope_table(S),
        "x": np.ascontiguousarray(inp["x"][b]),
        "ctx": np.ascontiguousarray(inp["ctx"][b]),
        "small": pack_small(inp, b),
        "ident": np.eye(128, dtype=np.float32),
        "ada_w": inp["ada_w"], "ada_b": inp["ada_b"],
        "final_g": np.ascontiguousarray(inp["final_g"]).reshape(1, D),
        "ffn_w_up": inp["ffn_w_up"], "ffn_w_down": inp["ffn_w_down"],
        "conv_w_pw1": inp["conv_w_pw1"], "conv_w_pw2": inp["conv_w_pw2"], "conv_b_pw2": inp["conv_b_pw2"],
    }
    return {k: np.ascontiguousarray(v, dtype=np.float32) for k, v in m.items()}


def kernel(**inputs):
    inp = {k: np.asarray(v) for k, v in inputs.items()}
    B, S, _ = inp["x"].shape
    prog = Prog(S)
    nc = prog.build()
    in_maps = [make_in_map(inp, b) for b in range(B)]
    res = run_bass_kernel_spmd(nc, in_maps, core_ids=list(range(B)))
    return np.stack([r["out"] for r in res.results], axis=0).astype(np.float32)
```

```python
import contextlib
import numpy as np
import concourse.bass as bass
import concourse.mybir as mybir
from concourse.bass_utils import run_bass_kernel_spmd

F32, BF16 = mybir.dt.float32, mybir.dt.bfloat16
AF = mybir.ActivationFunctionType
ALU = mybir.AluOpType

D = 1024
CTX = 256
DFF = 2816
NPAIR = 22
DEPTH = 4
EPS = 1e-6
CW = 31
HD = 64
NH = 16
NKV = 4


class Res:
    __slots__ = ("name", "last_w", "readers", "excl")

    def __init__(self, name, excl=False):
        self.name = name
        self.last_w = None
        self.readers = {}
        self.excl = excl or name.startswith("ps")


class Sched:
    def __init__(self, nc, es):
        self.nc = nc
        self.es = es
        self.eng = {"pe": nc.tensor, "act": nc.scalar, "dve": nc.vector, "pool": nc.gpsimd, "sp": nc.sync}
        self.sem = {k: es.enter_context(nc.semaphore("sem_" + k)) for k in ("pe", "act", "dve", "pool")}
        self.cnt = {k: 0 for k in self.sem}
        self.waited = {k: {} for k in self.eng}
        self.chans = {}

    def _deps(self, reads, writes, e=None):
        deps = []
        for r in reads:
            if r.last_w is not None:
                deps.append(r.last_w)
            if r.excl:
                deps.extend(t for k, t in r.readers.items() if k != e)
        for w in writes:
            if w.last_w is not None:
                deps.append(w.last_w)
            deps.extend(w.readers.values())
        return deps

    def _wait(self, e, deps):
        best = {}
        for s, v, key in deps:
            if key not in best or best[key][1] < v:
                best[key] = (s, v)
        for key, (s, v) in best.items():
            if self.waited[e].get(key, 0) < v:
                self.eng[e].wait_ge(s, v)
                self.waited[e][key] = v

    def _mark(self, tok, reads, writes):
        for w in writes:
            w.last_w = tok
            w.readers = {}
        for r in reads:
            if r not in writes:
                r.readers[tok[2]] = tok

    def op(self, e, fn, reads=(), writes=()):
        deps = self._deps(reads, writes, e)
        if e == "pe":
            deps = [d for d in deps if d[2] != "pe"]
        self._wait(e, deps)
        ins = fn(self.eng[e])
        self.cnt[e] += 1
        ins.then_inc(self.sem[e], 1)
        tok = (self.sem[e], self.cnt[e], e)
        self._mark(tok, reads, writes)
        return tok

    def dma(self, q, ch, out, in_, reads=(), writes=(), **kw):
        if ch not in self.chans:
            self.chans[ch] = [self.es.enter_context(self.nc.semaphore("ch_" + ch)), 0]
        c = self.chans[ch]
        deps = self._deps(reads, writes)
        if c[1] > 0:
            deps.append((c[0], c[1], "ch_" + ch))
        self._wait(q, deps)
        self.eng[q].dma_start(out=out, in_=in_, **kw).then_inc(c[0], 16)
        c[1] += 16
        tok = (c[0], c[1], "ch_" + ch)
        self._mark(tok, reads, writes)
        return tok

    def barrier(self, engines=("pe", "act", "dve", "pool", "sp")):
        deps = [(self.sem[k], self.cnt[k], k) for k in self.sem if self.cnt[k] > 0]
        deps += [(c[0], c[1], "ch_" + n) for n, c in self.chans.items() if c[1] > 0]
        for e in engines:
            self._wait(e, deps)


def _fm(v, nch):
    v = np.asarray(v, np.float32)
    lead = v.shape[:-1]
    return np.moveaxis(v.reshape(lead + (nch, 128)), -1, 0)


class SmallPack:
    def __init__(self):
        self.parts = []
        self.off = {}
        self.n = 0

    def add(self, name, arr):
        arr = np.ascontiguousarray(arr, np.float32).reshape(128, -1)
        self.off[name] = self.n
        self.parts.append(arr)
        self.n += arr.shape[1]

    def array(self):
        return np.ascontiguousarray(np.concatenate(self.parts, axis=1))


def small_layout():
    off = {}
    n = 0
    for name, w in (("c2", 16), ("ng2", DEPTH * 2 * 8 * 2), ("ffn_wdw", DEPTH * 44 * 3), ("ffn_bdw", DEPTH * 44),
                    ("cv_b1", 2 * 16), ("cv_wdw", 2 * 8 * CW), ("cv_bdw", 2 * 8), ("cv_lng", 2 * 8), ("cv_lnb", 2 * 8), ("sink", 2 * 16)):
        off[name] = (n, w)
        n += w
    return off, n


def pack_small(inp, b):
    sp = SmallPack()
    c2 = np.stack([_fm(inp["c"][b], 8), _fm(inp["c_ctx"], 8)], axis=-1)
    sp.add("c2", c2)
    ng = np.stack([_fm(inp["norm1_g"], 8), _fm(inp["norm2_g"], 8)], axis=2)
    ng2 = np.repeat(ng[..., None], 2, axis=-1)
    sp.add("ng2", ng2)
    wdw = _fm(inp["ffn_w_dw"], 44)
    sp.add("ffn_wdw", np.transpose(wdw, (0, 1, 3, 2)))
    sp.add("ffn_bdw", _fm(inp["ffn_b_dw"], 44))
    sp.add("cv_b1", _fm(inp["conv_b_pw1"], 16))
    cw = _fm(inp["conv_w_dw"], 8)
    sp.add("cv_wdw", np.transpose(cw, (0, 1, 3, 2)))
    sp.add("cv_bdw", _fm(inp["conv_b_dw"], 8))
    sp.add("cv_lng", _fm(inp["conv_ln_g"], 8))
    sp.add("cv_lnb", _fm(inp["conv_ln_b"], 8))
    sp.add("sink", np.broadcast_to(np.asarray(inp["attn_sink"], np.float32).reshape(1, 32), (128, 32)))
    off, n = small_layout()
    assert sp.n == n and all(sp.off[k] == off[k][0] for k in off)
    return sp.array()


class Prog:
    def __init__(self, S, phases=None, U=256, debug=False):
        self.S = S
        self.U = U
        self.phases = phases
        self.debug = debug
        self.nc = nc = bass.Bass("TRN2", target_bir_lowering=False)
        dt = nc.dram_tensor
        self.off, self.NS = small_layout()
        self.x = dt("x", [S, D], F32, kind="ExternalInput").ap()
        self.ctx = dt("ctx", [CTX, D], F32, kind="ExternalInput").ap()
        self.small_d = dt("small", [128, self.NS], F32, kind="ExternalInput").ap()
        self.ident_d = dt("ident", [128, 128], F32, kind="ExternalInput").ap()
        self.ada_w = dt("ada_w", [DEPTH, D, 6 * D], F32, kind="ExternalInput").ap()
        self.ada_b = dt("ada_b", [DEPTH, 6 * D], F32, kind="ExternalInput").ap()
        self.final_g = dt("final_g", [1, D], F32, kind="ExternalInput").ap()
        self.w_up = dt("ffn_w_up", [DEPTH, D, 2 * DFF], F32, kind="ExternalInput").ap()
        self.w_dn = dt("ffn_w_down", [DEPTH, DFF, D], F32, kind="ExternalInput").ap()
        self.cv_w1 = dt("conv_w_pw1", [2, D, 2 * D], F32, kind="ExternalInput").ap()
        self.cv_w2 = dt("conv_w_pw2", [2, D, D], F32, kind="ExternalInput").ap()
        self.cv_b2 = dt("conv_b_pw2", [2, D], F32, kind="ExternalInput").ap()
        self.at_wqkv = dt("attn_w_qkv", [2, D, 1536], F32, kind="ExternalInput").ap()
        self.at_wo = dt("attn_w_o", [2, D, D], F32, kind="ExternalInput").ap()
        self.masks_d = dt("masks", [128, 2, 512], F32, kind="ExternalInput").ap()
        self.rope_d = dt("rope", [128, S // 128, 64], F32, kind="ExternalInput").ap()
        self.out = dt("out", [S, D], F32, kind="ExternalOutput").ap()
        self.hbuf = [dt("hA", [S + CTX, D], F32).ap(), dt("hB", [S + CTX, D], F32).ap()]
        self.modd = dt("modd", [DEPTH, 2, 6 * D], F32).ap()
        if debug:
            self.dbg = dt("dbg", [CTX, D], F32, kind="ExternalOutput").ap()

    def uniq(self, name):
        self._uid = getattr(self, "_uid", 0) + 1
        return f"t{self._uid}_{name}"

    def sm(self, name, idx, width=1):
        o = self.off[name][0] + idx
        return self.small[:, o:o + width]

    def rows(self, buf, seq, r0, n):
        if buf == "in":
            src = self.x if seq == 0 else self.ctx
            return src[r0:r0 + n, :]
        if buf == "out":
            return self.out[r0:r0 + n, :]
        base = 0 if seq == 0 else self.S
        return self.hbuf[buf][base + r0:base + r0 + n, :]

    def build(self):
        nc = self.nc
        with contextlib.ExitStack() as es:
            self.es = es
            self.sch = sch = Sched(nc, es)
            T = lambda name, shape, dtp: es.enter_context(nc.sbuf_tensor(self.uniq(name), shape, dtp))
            self.small = T("small", [128, self.NS], F32)
            self.identf = T("identf", [128, 128], F32)
            self.identb = T("identb", [128, 128], BF16)
            self.epsT = T("epsT", [128, 4], F32)
            self.FM = T("FM", [128, DEPTH * 96], F32)
            self.GM = T("GM", [128, DEPTH * 2 * 16], F32)
            self.smallR = Res("small")
            self.identR = Res("ident")
            self.fmR = Res("FM")
            self.gmR = Res("GM")
            sch.dma("sp", "small", out=self.small[:], in_=self.small_d, writes=[self.smallR])
            sch.dma("sp", "identf", out=self.identf[:], in_=self.ident_d, writes=[self.identR])
            sch.dma("pool", "identb", out=self.identb[:], in_=self.ident_d, writes=[self.identR])
            epsR = Res("eps")
            sch.op("dve", lambda e: e.memset(self.epsT[:, 0:1], EPS), writes=[epsR])
            sch.op("dve", lambda e: e.memset(self.epsT[:, 1:2], 0.0), writes=[epsR])
            sch.op("dve", lambda e: e.memset(self.epsT[:, 2:3], -0.5), writes=[epsR])
            sch.barrier()
            self.prologue()
            phases = []
            src = "in"
            for l in range(DEPTH):
                phases.append(("mix", l, src, 1))
                phases.append(("ffn", l, 1, 0))
                src = 0
            phases.append(("final", 0, 0, "out"))
            if self.phases is not None:
                phases = self.phases
            for kind, l, s_, d_ in phases:
                if kind == "mix":
                    if l % 2 == 0:
                        self.phase_conv(l, s_, d_)
                    else:
                        self.phase_attn(l, s_, d_)
                elif kind == "ffn":
                    self.phase_ffn(l, s_, d_)
                else:
                    self.phase_final(s_, d_)
                sch.barrier()
            if self.debug:
                lastbuf = [p for p in phases if p[0] != "final"][-1][3]
                sch.dma("sp", "dbg", out=self.dbg, in_=self.hbuf[lastbuf][self.S:self.S + CTX, :])
            sch.barrier()
        return nc

    def prologue(self):
        nc, sch = self.nc, self.sch
        with contextlib.ExitStack() as pes:
            T = lambda name, shape, dtp: pes.enter_context(nc.sbuf_tensor(self.uniq(name), shape, dtp))
            P = lambda name, shape, dtp: pes.enter_context(nc.psum_tensor(self.uniq(name), shape, dtp))
            scT = T("scT", [128, 16], F32)
            wch = [T(f"adaw{i}", [128, 8, 512], F32) for i in range(2)]
            wchR = [Res(f"adaw{i}") for i in range(2)]
            adab = T("adab", [2, 6 * D], F32)
            modrow = T("modrow", [2, 6 * D], F32)
            psr = [P(f"psr{i}", [2, 512], F32) for i in range(2)]
            psrR = [Res(f"psr{i}") for i in range(2)]
            psf = P("psf", [128, 96], F32)
            psfR = Res("psf")
            scR, adabR, modR = Res("scT"), Res("adab"), Res("modrow")
            c0 = self.off["c2"][0]
            sch.op("act", lambda e: e.activation(out=scT[:], in_=self.small[:, c0:c0 + 16], func=AF.Silu),
                   reads=[self.smallR], writes=[scR])
            for l in range(DEPTH):
                for s in range(2):
                    sch.dma("sp", f"adab{s}", out=adab[s:s + 1, :], in_=self.ada_b[l:l + 1, :], writes=[adabR])
                for j in range(12):
                    w, wR = wch[j % 2], wchR[j % 2]
                    sch.dma("sp", f"adaw{j % 2}", out=w[:],
                            in_=self.ada_w[l, :, j * 512:(j + 1) * 512].rearrange("(kc p) n -> p kc n", p=128),
                            writes=[wR])
                    ps, psR = psr[j % 2], psrR[j % 2]

                    def mm(e):
                        for kc in range(8):
                            ins = e.matmul(ps[:], lhsT=scT[:, kc * 2:kc * 2 + 2], rhs=w[:, kc, :],
                                           start=(kc == 0), stop=(kc == 7))
                        return ins
                    sch.op("pe", mm, reads=[scR, wR], writes=[psR])
                    sch.op("dve", lambda e: e.tensor_tensor(out=modrow[:, j * 512:(j + 1) * 512], in0=ps[:],
                                                            in1=adab[:, j * 512:(j + 1) * 512], op=ALU.add),
                           reads=[psR, adabR], writes=[modR])
                sch.dma("sp", "modd", out=self.modd[l], in_=modrow[:], reads=[modR])

                def tr(e):
                    for j in range(48):
                        ins = e.transpose(out=psf[:, j * 2:j * 2 + 2], in_=modrow[0:2, j * 128:(j + 1) * 128],
                                          identity=self.identf[0:2, 0:2])
                    return ins
                sch.op("pe", tr, reads=[modR, self.identR], writes=[psfR])
                sch.op("act", lambda e: e.activation(out=self.FM[:, l * 96:(l + 1) * 96], in_=psf[:], func=AF.Identity),
                       reads=[psfR], writes=[self.fmR])
                for which, vec in ((0, 1), (1, 4)):
                    go = (l * 2 + which) * 16
                    fo = l * 96 + vec * 16
                    no = self.off["ng2"][0] + (l * 2 + which) * 16
                    sch.op("dve", lambda e: e.scalar_tensor_tensor(out=self.GM[:, go:go + 16], in0=self.FM[:, fo:fo + 16],
                                                                   scalar=1.0, in1=self.small[:, no:no + 16],
                                                                   op0=ALU.add, op1=ALU.mult),
                           reads=[self.fmR, self.smallR], writes=[self.gmR])
            sch.barrier()

    def gm(self, l, which, kc, s):
        o = (l * 2 + which) * 16 + kc * 2 + s
        return self.GM[:, o:o + 1]

    def fmv(self, l, vec, kc, s):
        o = l * 96 + vec * 16 + kc * 2 + s
        return self.FM[:, o:o + 1]

    def load_w_bf16(self, name, dst, src3, res, nsplit=1):
        K = dst.shape[1]
        N = dst.shape[2]
        nsplit = max(nsplit, (N + 2047) // 2048)
        step = (N + nsplit - 1) // nsplit
        i = 0
        for kc in range(K):
            for c0 in range(0, N, step):
                c1 = min(N, c0 + step)
                self.sch.dma("pool", f"wld{i % 8}", out=dst[:, kc, c0:c1],
                             in_=src3[kc * 128:(kc + 1) * 128, c0:c1], max_dma_last_dim=8192)
                i += 1

    def alloc_norm(self, pes, U, nb_bt=2, nhin=2, npst=1):
        nc = self.nc
        T = lambda name, shape, dtp: pes.enter_context(nc.sbuf_tensor(self.uniq(name), shape, dtp))
        ns = {}
        ns["HIN"] = [(T(f"hin{i}", [128, D], F32), Res(f"hin{i}")) for i in range(nhin)]
        ns["XH"] = [(T(f"xh{i}", [128, D], BF16), Res(f"xh{i}")) for i in range(2)]
        ns["STAT"] = [(T(f"stat{i}", [128, 4], F32), Res(f"stat{i}")) for i in range(4)]
        ns["BT"] = [(T(f"bt{i}", [128, 8, U], BF16), Res(f"bt{i}")) for i in range(nb_bt)]
        ns["psTs"] = [(pes.enter_context(nc.psum_tensor(self.uniq("psT"), [128, 8, 128], BF16)), Res(f"psT{i}")) for i in range(npst)]
        ns["psT"] = ns["psTs"][0]
        return ns

    def stage_norm_pre(self, ns, m, U, ntile, src, seq, l, which, s):
        sch = self.sch
        tpu = U // 128
        for j in range(tpu):
            n = m * tpu + j
            if n >= ntile:
                break
            nhin = len(ns["HIN"])
            hin, hinR = ns["HIN"][n % nhin]
            xh, xhR = ns["XH"][n % 2]
            st, stR = ns["STAT"][n % 4]
            sch.dma("sp", f"hin{n % nhin}", out=hin[:], in_=self.rows(src, seq, n * 128, 128), writes=[hinR])
            sch.op("act", lambda e: e.activation(out=xh[:], in_=hin[:], func=AF.Square, accum_out=st[:, 0:1]),
                   reads=[hinR], writes=[xhR, stR])
            sch.op("pool", lambda e: e.tensor_scalar(out=st[:, 1:2], in0=st[:, 0:1], scalar1=1.0 / D, scalar2=EPS,
                                                     op0=ALU.mult, op1=ALU.add), reads=[stR], writes=[stR])
            sch.op("pool", lambda e: e.tensor_tensor(out=st[:, 2:3], in0=st[:, 1:2], in1=self.epsT[:, 2:3], op=ALU.pow),
                   reads=[stR], writes=[stR])
            sch.op("dve", lambda e: e.tensor_scalar(out=xh[:], in0=hin[:], scalar1=st[:, 2:3], scalar2=None,
                                                    op0=ALU.mult), reads=[hinR, stR], writes=[xhR])

    def stage_norm_post(self, ns, m, U, ntile, src, seq, l, which, s):
        sch = self.sch
        tpu = U // 128
        bt, btR = ns["BT"][m % len(ns["BT"])]
        vsh = 0 if which == 0 else 3
        for j in range(tpu):
            n = m * tpu + j
            if n >= ntile:
                break
            xh, xhR = ns["XH"][n % 2]
            psT, psTR = ns["psTs"][n % len(ns["psTs"])]

            def tr(e):
                for kc in range(8):
                    ins = e.transpose(out=psT[:, kc, :], in_=xh[:, kc * 128:(kc + 1) * 128],
                                      identity=self.identb[:])
                return ins
            sch.op("pe", tr, reads=[xhR], writes=[psTR])
            for kc in range(8):
                sch.op("act", lambda e: e.activation(out=bt[:, kc, j * 128:(j + 1) * 128], in_=psT[:, kc, :], func=AF.Identity,
                                                     scale=self.gm(l, which, kc, s), bias=self.fmv(l, vsh, kc, s)),
                       reads=[psTR], writes=[btR])
        return bt, btR

    def phase_ffn(self, l, src, dst):
        nc, sch = self.nc, self.sch
        U = self.U
        with contextlib.ExitStack() as pes:
            T = lambda name, shape, dtp: pes.enter_context(nc.sbuf_tensor(self.uniq(name), shape, dtp))
            wup = T("wup", [128, 8, 2 * DFF], BF16)
            wdn = T("wdn", [128, NPAIR, D], BF16)
            wupR, wdnR = Res("wup"), Res("wdn")
            self.load_w_bf16("wup", wup, self.w_up[l], wupR, nsplit=2)
            self.load_w_bf16("wdn", wdn, self.w_dn[l], wdnR)
            sch.barrier()
            seqs = [(1, CTX), (0, self.S)] if l < DEPTH - 1 else [(0, self.S)]
            for seq, Tn in seqs:
                fold = (seq == 0)
                if fold:
                    self.fold_gate(l, 5, [(wdn[:, kc, :], wdnR, 128) for kc in range(NPAIR)])
                with contextlib.ExitStack() as ses:
                    self.ffn_seq(ses, l, seq, Tn, src, dst, wup, wupR, wdn, wdnR, fold=fold)
                    sch.barrier()

    def ffn_seq(self, pes, l, s, Tn, src, dst, wup, wupR, wdn, wdnR, fold=False):
        nc, sch = self.nc, self.sch
        U = self.U
        T = lambda name, shape, dtp: pes.enter_context(nc.sbuf_tensor(self.uniq(name), shape, dtp))
        P = lambda name, shape, dtp: pes.enter_context(nc.psum_tensor(self.uniq(name), shape, dtp))
        NU = (Tn + U - 1) // U
        NT = Tn // 128
        tpu = U // 128
        ns = self.alloc_norm(pes, U, nhin=1, npst=1)
        GTS = [(T(f"GT{i}", [128, NPAIR, U + 1], BF16), Res(f"gt{i}")) for i in range(2)]
        CARRY = [T(f"carry{i}", [128, NPAIR, 2, 2], F32) for i in range(2)]
        carryR = [[Res(f"carry{i}_{c}") for c in range(NPAIR)] for i in range(2)]
        NTB = 6
        TB = [(T(f"tb{i}", [128, U + 1], F32), Res(f"tb{i}")) for i in range(NTB)]
        NUC = 3 if fold else 2
        UC = [(T(f"uc{i}", [128, 2, U + 3], F32), Res(f"uc{i}"), Res(f"ucc{i}")) for i in range(NUC)]
        NSA = 3 if fold else 2
        SA = [(T(f"sa{i}", [128, U + 1], F32), Res(f"sa{i}")) for i in range(NSA)]
        PSY = [(P(f"psy{i}", [128, D], F32), Res(f"psy{i}")) for i in range(2)]
        NPS = 3
        PSAB = [(P(f"psab{i}", [128, 2, U], F32), Res(f"psab{i}")) for i in range(NPS)]
        NHR = 3 if fold else 1
        HRES = [(T(f"hres{i}", [128, D], F32), Res(f"hres{i}")) for i in range(NHR)]
        if not fold:
            TMP = [(T("tmp0", [128, D], F32), Res("tmp0"))]
            gbc = T("gbc", [128, D], F32)
            gbcR = Res("gbc")
            sch.dma("sp", "gbc", out=gbc[:], in_=self.modd[l, s:s + 1, 5 * D:6 * D].partition_broadcast(128), writes=[gbcR])
        sch.op("pool", lambda e: e.memset(CARRY[0][:], 0.0), writes=carryR[0])
        for uc, ucR, uccR in UC:
            sch.op("pool", lambda e: e.memset(uc[:], 0.0), writes=[ucR, uccR])
        wo = self.off["ffn_wdw"][0] + l * 132
        bo = self.off["ffn_bdw"][0] + l * 44
        W3 = lambda cc, k: self.small[:, wo + cc * 3 + k: wo + cc * 3 + k + 1]
        B3 = lambda cc: self.small[:, bo + cc: bo + cc + 1]

        tail = [None]

        def stageU(m, c_lo=0, c_hi=NPAIR):
            last = (m == NU - 1)
            ext = 1 if last else 0
            bt, btR = ns["BT"][m % 2]
            cp, cpR = CARRY[m % 2], carryR[m % 2]
            cn, cnR = CARRY[(m + 1) % 2], carryR[(m + 1) % 2]
            GT, gR = GTS[m % 2]
            for c in range(c_lo, c_hi):
                it = m * NPAIR + c
                ps, psR = PSAB[it % NPS]

                def mm(e):
                    for hf in range(2):
                        col = hf * DFF + c * 128
                        for kc in range(8):
                            ins = e.matmul(ps[:, hf, :], lhsT=wup[:, kc, col:col + 128], rhs=bt[:, kc, :],
                                           start=(kc == 0), stop=(kc == 7))
                    return ins
                sch.op("pe", mm, reads=[btR, wupR], writes=[psR])
                tb = [TB[(2 * it) % NTB], TB[(2 * it + 1) % NTB]]
                uc, ucR, uccR = UC[it % NUC]
                sch.op("pool", lambda e: e.tensor_copy(out=uc[:, :, 0:2], in_=cp[:, c, :, :]), reads=[cpR[c]], writes=[uccR])
                sch.op("act", lambda e: e.activation(out=uc[:, :, 2:2 + U], in_=ps[:, :, :], func=AF.Identity),
                       reads=[psR], writes=[ucR])
                for hf in range(2):
                    cc = hf * NPAIR + c
                    Tt, TtR = tb[hf]
                    sch.op("act", lambda e: e.activation(out=Tt[:, 0:U], in_=ps[:, hf, :], func=AF.Identity,
                                                         scale=W3(cc, 2), bias=B3(cc)),
                           reads=[psR, self.smallR], writes=[TtR])
                    if ext:
                        sch.op("act", lambda e: e.activation(out=Tt[:, U:U + 1], in_=self.epsT[:, 1:2], func=AF.Identity,
                                                             scale=1.0, bias=B3(cc)), reads=[], writes=[TtR])
                if not last:
                    sch.op("pool", lambda e: e.tensor_copy(out=cn[:, c, :, :], in_=uc[:, :, U:U + 2]), reads=[ucR], writes=[cnR[c]])
                for k, o in ((1, 1), (0, 0)):
                    for hf in range(2):
                        cc = hf * NPAIR + c
                        Tt, TtR = tb[hf]
                        sch.op("dve", lambda e: e.scalar_tensor_tensor(out=Tt[:, 0:U + ext], in0=uc[:, hf, o:o + U + ext],
                                                                       scalar=W3(cc, k), in1=Tt[:, 0:U + ext],
                                                                       op0=ALU.mult, op1=ALU.add),
                               reads=[ucR, uccR, TtR], writes=[TtR])
                if tail[0] is not None:
                    tail[0]()

                def tail_fn(it=it, c=c, tb=tb, ext=ext, GT=GT, gR=gR):
                    sa, saR = SA[it % NSA]
                    sch.op("act", lambda e: e.activation(out=sa[:, 0:U + ext], in_=tb[0][0][:, 0:U + ext], func=AF.Silu),
                           reads=[tb[0][1]], writes=[saR])
                    sch.op("pool", lambda e: e.tensor_tensor(out=GT[:, c, 0:U + ext], in0=sa[:, 0:U + ext],
                                                             in1=tb[1][0][:, 0:U + ext], op=ALU.mult),
                           reads=[saR, tb[1][1]], writes=[gR])
                tail[0] = tail_fn
            if c_hi == NPAIR and tail[0] is not None:
                tail[0]()
                tail[0] = None

        def stageD(m):
            GT, gR = GTS[m % 2]
            last = (m == NU - 1)
            tiles = [(jt * 128, 128) for jt in range(tpu)] + ([(U, 1)] if last else [])
            for ti, (c0, ncol) in enumerate(tiles):
                tok0 = U * m - 1 + c0
                p0 = 1 if tok0 < 0 else 0
                py, pyR = PSY[dcount[0] % 2]

                def mm(e):
                    for nb in range(2):
                        for kc in range(NPAIR):
                            ins = e.matmul(py[0:ncol, nb * 512:(nb + 1) * 512], lhsT=GT[:, kc, c0:c0 + ncol],
                                           rhs=wdn[:, kc, nb * 512:(nb + 1) * 512], start=(kc == 0), stop=(kc == NPAIR - 1))
                    return ins
                sch.op("pe", mm, reads=[gR, wdnR], writes=[pyR])
                hi = dcount[0] % NHR
                dcount[0] += 1
                hr, hrR = HRES[hi]
                r0 = tok0 + p0
                nr = ncol - p0
                if p0:
                    sch.op("pool", lambda e: e.memset(hr[0:1, :], 0.0), writes=[hrR])
                sch.dma("sp", f"hres{hi}", out=hr[p0:p0 + nr, :], in_=self.rows(src, s, r0, nr), writes=[hrR])
                if fold:
                    sch.op("dve", lambda e: e.tensor_tensor(out=hr[0:ncol, :], in0=py[0:ncol, :], in1=hr[0:ncol, :], op=ALU.add),
                           reads=[pyR, hrR], writes=[hrR])
                else:
                    tm, tmR = TMP[0]
                    sch.op("dve", lambda e: e.tensor_tensor(out=tm[0:ncol, :], in0=py[0:ncol, :], in1=gbc[0:ncol, :], op=ALU.mult),
                           reads=[pyR, gbcR], writes=[tmR])
                    sch.op("pool", lambda e: e.tensor_tensor(out=hr[0:ncol, :], in0=tm[0:ncol, :], in1=hr[0:ncol, :], op=ALU.add),
                           reads=[tmR, hrR], writes=[hrR])
                self.pend_stores.append(
                    lambda hr=hr, hrR=hrR, hi=hi, p0=p0, nr=nr, r0=r0: sch.dma("sp", f"hres{hi}", out=self.rows(dst, s, r0, nr),
                                                                              in_=hr[p0:p0 + nr, :], reads=[hrR]))
                if not fold:
                    self.flush_stores()

        dcount = [0]
        self.pend_stores = []
        HALF = NPAIR // 2
        nargs = (U, NT, src, s, l, 1, s)
        self.stage_norm_pre(ns, 0, *nargs)
        for step in range(NU + 2):
            if step < NU:
                self.stage_norm_post(ns, step, *nargs)
            if 0 <= step - 1 < NU:
                stageU(step - 1, 0, HALF)
            if step + 1 < NU:
                self.stage_norm_pre(ns, step + 1, *nargs)
            if 0 <= step - 2 < NU:
                stageD(step - 2)
            if 0 <= step - 1 < NU:
                stageU(step - 1, HALF, NPAIR)
            self.flush_stores()

    def phase_final(self, src, dst):
        nc, sch = self.nc, self.sch
        with contextlib.ExitStack() as pes:
            T = lambda name, shape, dtp: pes.enter_context(nc.sbuf_tensor(self.uniq(name), shape, dtp))
            fg = T("fg", [128, D], F32)
            fgR = Res("fg")
            sch.dma("sp", "fg", out=fg[:], in_=self.final_g.partition_broadcast(128), writes=[fgR])
            HIN = [(T(f"fh{i}", [128, D], F32), Res(f"fh{i}")) for i in range(3)]
            SQ = [(T(f"fsq{i}", [128, D], F32), Res(f"fsq{i}")) for i in range(2)]
            STAT = [(T(f"fst{i}", [128, 4], F32), Res(f"fst{i}")) for i in range(3)]
            for n in range(self.S // 128):
                hin, hinR = HIN[n % 3]
                sq, sqR = SQ[n % 2]
                st, stR = STAT[n % 3]
                sch.dma("sp", f"fh{n % 3}", out=hin[:], in_=self.rows(src, 0, n * 128, 128), writes=[hinR])
                sch.op("act", lambda e: e.activation(out=sq[:], in_=hin[:], func=AF.Square, accum_out=st[:, 0:1]),
                       reads=[hinR], writes=[sqR, stR])
                sch.op("pool", lambda e: e.tensor_scalar(out=st[:, 1:2], in0=st[:, 0:1], scalar1=1.0 / D, scalar2=EPS,
                                                         op0=ALU.mult, op1=ALU.add), reads=[stR], writes=[stR])
                sch.op("pool", lambda e: e.tensor_tensor(out=st[:, 2:3], in0=st[:, 1:2], in1=self.epsT[:, 2:3], op=ALU.pow),
                       reads=[stR], writes=[stR])
                sch.op("dve", lambda e: e.scalar_tensor_tensor(out=hin[:], in0=hin[:], scalar=st[:, 2:3], in1=fg[:],
                                                               op0=ALU.mult, op1=ALU.mult),
                       reads=[hinR, stR, fgR], writes=[hinR])
                sch.dma("act", f"fh{n % 3}", out=self.rows(dst, 0, n * 128, 128), in_=hin[:], reads=[hinR])

    def flush_stores(self):
        for fn in getattr(self, "pend_stores", []):
            fn()
        self.pend_stores = []

    def resid_stage(self, RS, n, py, pyR, src, dst, seq):
        sch = self.sch
        nh = len(RS["HRES"])
        hr, hrR = RS["HRES"][n % nh]
        sch.dma("sp", f"hres{n % nh}", out=hr[:], in_=self.rows(src, seq, n * 128, 128), writes=[hrR])
        if RS["fold"]:
            sch.op("dve", lambda e: e.tensor_tensor(out=hr[:], in0=py[:], in1=hr[:], op=ALU.add),
                   reads=[pyR, hrR], writes=[hrR])
        else:
            tm, tmR = RS["TMP"][0]
            gbc, gbcR = RS["gbc"]
            sch.op("dve", lambda e: e.tensor_tensor(out=tm[:], in0=py[:], in1=gbc[:], op=ALU.mult),
                   reads=[pyR, gbcR], writes=[tmR])
            sch.op("pool", lambda e: e.tensor_tensor(out=hr[:], in0=tm[:], in1=hr[:], op=ALU.add),
                   reads=[tmR, hrR], writes=[hrR])
        if not hasattr(self, "pend_stores"):
            self.pend_stores = []
        self.pend_stores.append(lambda: sch.dma("sp", f"hres{n % nh}", out=self.rows(dst, seq, n * 128, 128), in_=hr[:], reads=[hrR]))

    def fold_gate(self, l, vec, targets):
        nc, sch = self.nc, self.sch
        with contextlib.ExitStack() as fes:
            gb = fes.enter_context(nc.sbuf_tensor(self.uniq("gfold"), [128, D], F32))
            gbR = Res("gfold")
            sch.dma("sp", "gfold", out=gb[:], in_=self.modd[l, 0:1, vec * D:(vec + 1) * D].partition_broadcast(128), writes=[gbR])
            for i, (ap, res, np_) in enumerate(targets):
                sch.op("dve" if i % 2 == 0 else "pool",
                       lambda e: e.tensor_tensor(out=ap, in0=ap, in1=gb[0:np_, :], op=ALU.mult),
                       reads=[gbR, res], writes=[res])
            sch.barrier()

    def alloc_resid(self, pes, l, vec, s, fold=False):
        nc, sch = self.nc, self.sch
        T = lambda name, shape, dtp: pes.enter_context(nc.sbuf_tensor(self.uniq(name), shape, dtp))
        RS = {"fold": fold}
        RS["HRES"] = [(T(f"hres{i}", [128, D], F32), Res(f"hres{i}")) for i in range(4 if fold else 2)]
        if fold:
            return RS
        RS["TMP"] = [(T("tmp", [128, D], F32), Res("tmp"))]
        gbc = T("gbc", [128, D], F32)
        gbcR = Res("gbc")
        sch.dma("sp", "gbc", out=gbc[:], in_=self.modd[l, s:s + 1, vec * D:(vec + 1) * D].partition_broadcast(128),
                writes=[gbcR])
        RS["gbc"] = (gbc, gbcR)
        return RS

    def phase_conv(self, l, src, dst):
        nc, sch = self.nc, self.sch
        j = l // 2
        with contextlib.ExitStack() as pes:
            T = lambda name, shape, dtp: pes.enter_context(nc.sbuf_tensor(self.uniq(name), shape, dtp))
            w1 = T("w1", [128, 8, 2 * D], BF16)
            w2 = T("w2", [128, 8, D], BF16)
            dg = T("dg", [128, 8, CW, 128], BF16)
            b2r = T("b2r", [1, D], BF16)
            onesr = T("onesr", [1, 128], BF16)
            onesm = T("onesm", [128, 128], BF16)
            w1R, w2R, dgR, cR = Res("w1"), Res("w2"), Res("dg"), Res("cconst")
            self.load_w_bf16("w1", w1, self.cv_w1[j], w1R)
            self.load_w_bf16("w2", w2, self.cv_w2[j], w2R)
            sch.barrier()
            sch.dma("pool", "b2r", out=b2r[:], in_=self.cv_b2[j:j + 1, :], writes=[cR])
            sch.op("dve", lambda e: e.memset(onesr[:], 1.0), writes=[cR])
            sch.op("dve", lambda e: e.memset(onesm[:], 1.0), writes=[cR])
            wo = self.off["cv_wdw"][0] + j * 8 * CW
            for kc in range(8):
                for k in range(CW):
                    o = wo + kc * CW + k
                    sch.op("dve" if (k % 2 == 0) else "pool",
                           lambda e: e.tensor_scalar(out=dg[:, kc, k, :], in0=self.identb[:], scalar1=self.small[:, o:o + 1],
                                                     scalar2=None, op0=ALU.mult), reads=[self.identR], writes=[dgR])
            W = dict(w1=w1, w2=w2, dg=dg, b2r=b2r, onesr=onesr, onesm=onesm, w1R=w1R, w2R=w2R, dgR=dgR, cR=cR)
            seqs = [(1, CTX), (0, self.S)] if l < DEPTH - 1 else [(0, self.S)]
            for seq, Tn in seqs:
                if seq == 0:
                    self.fold_gate(l, 2, [(w2[:, kc, :], w2R, 128) for kc in range(8)] + [(b2r[0:1, :], cR, 1)])
                with contextlib.ExitStack() as ses:
                    self.conv_seq(ses, l, seq, Tn, src, dst, W, fold=(seq == 0))
                    sch.barrier()

    def conv_seq(self, pes, l, s, Tn, src, dst, W, fold=False):
        nc, sch = self.nc, self.sch
        j = l // 2
        U = 256
        HL = 16
        T = lambda name, shape, dtp: pes.enter_context(nc.sbuf_tensor(self.uniq(name), shape, dtp))
        P = lambda name, shape, dtp: pes.enter_context(nc.psum_tensor(self.uniq(name), shape, dtp))
        NU = Tn // U
        NT = Tn // 128
        tpu = U // 128
        ns = self.alloc_norm(pes, U)
        RS = self.alloc_resid(pes, l, 2, s, fold=fold)
        PSY = (P("psy", [128, D], F32), Res("psy"))
        PSAG = [(P(f"psag{i}", [128, 2, U], F32), Res(f"psag{i}")) for i in range(2)]
        PSC = [(P(f"psc{i}", [128, 512], F32)[:, 0:U], Res(f"psc{i}")) for i in range(2)]
        PSS = (P("pss", [128, 2, U], F32), Res("pss"))
        NV = min(2, NU)
        NB2 = min(2, NU)
        V = [(T(f"v{i}", [128, 8, U + 2 * HL], BF16), Res(f"v{i}")) for i in range(NV)]
        SG = [(T(f"sg{i}", [128, U], F32), Res(f"sg{i}")) for i in range(2)]
        WCS = [(T(f"wc{i}", [128, 8, U], F32), [Res(f"wc{i}_{k}") for k in range(8)]) for i in range(NB2)]
        WB = (T("wb", [128, 8, U], BF16), [Res(f"wb{k}") for k in range(8)])
        WS = (T("wsq", [128, 8, U], BF16), [Res(f"wsq{k}") for k in range(8)])
        MEANS = [(T(f"mean{i}", [128, U], F32), Res(f"mean{i}")) for i in range(NB2)]
        MSQS = [(T(f"msq{i}", [128, U], F32), Res(f"msq{i}")) for i in range(1)]
        RSTDS = [(T(f"rstd{i}", [128, U], F32), Res(f"rstd{i}")) for i in range(NB2)]
        ZT = [(T(f"zt{i}", [128, 8, U], BF16), Res(f"zt{i}")) for i in range(NB2)]
        b1o = self.off["cv_b1"][0] + j * 16
        B1 = lambda c: self.small[:, b1o + c:b1o + c + 1]
        bdo = self.off["cv_bdw"][0] + j * 8
        lgo = self.off["cv_lng"][0] + j * 8
        lbo = self.off["cv_lnb"][0] + j * 8
        w1, w2, dg = W["w1"], W["w2"], W["dg"]

        def stageA(m):
            bt, btR = ns["BT"][m % 2]
            v, vR = V[m % NV]
            for c in range(8):
                ps, psR = PSAG[(m * 8 + c) % 2]

                def mm(e):
                    for hf in range(2):
                        col = hf * D + c * 128
                        for kc in range(8):
                            ins = e.matmul(ps[:, hf, :], lhsT=w1[:, kc, col:col + 128], rhs=bt[:, kc, :],
                                           start=(kc == 0), stop=(kc == 7))
                    return ins
                sch.op("pe", mm, reads=[btR, W["w1R"]], writes=[psR])
                sg, sgR = SG[(m * 8 + c) % 2]
                sch.op("act", lambda e: e.activation(out=sg[:], in_=ps[:, 1, :], func=AF.Sigmoid, bias=B1(8 + c), scale=1.0),
                       reads=[psR], writes=[sgR])
                sch.op("dve", lambda e: e.scalar_tensor_tensor(out=v[:, c, HL:HL + U], in0=ps[:, 0, :], scalar=B1(c),
                                                               in1=sg[:], op0=ALU.add, op1=ALU.mult),
                       reads=[psR, sgR], writes=[vR])
            if m == 0:
                sch.op("pool", lambda e: e.memset(v[:, :, 0:HL], 0.0), writes=[vR])
            else:
                vp, vpR = V[(m - 1) % NV]
                sch.op("pool", lambda e: e.tensor_copy(out=vp[:, :, HL + U:HL + U + HL], in_=v[:, :, HL:2 * HL]),
                       reads=[vR], writes=[vpR])
                sch.op("pool", lambda e: e.tensor_copy(out=v[:, :, 0:HL], in_=vp[:, :, U:U + HL]),
                       reads=[vpR], writes=[vR])
            if m == NU - 1:
                sch.op("pool", lambda e: e.memset(v[:, :, HL + U:HL + U + HL], 0.0), writes=[vR])

        def stageB1(m):
            v, vR = V[m % NV]
            wc, wcR = WCS[m % NB2]
            wb, wbR = WB
            wsq, wsR = WS
            for kc in range(8):
                pc, pcR = PSC[kc % 2]

                def mm(e):
                    for k in range(CW):
                        ins = e.matmul(pc[:, :], lhsT=dg[:, kc, k, :], rhs=v[:, kc, 1 + k:1 + k + U],
                                       start=(k == 0), stop=(k == CW - 1))
                    return ins
                sch.op("pe", mm, reads=[vR, W["dgR"]], writes=[pcR])
                bd = self.small[:, bdo + kc:bdo + kc + 1]
                sch.op("act", lambda e: e.activation(out=wc[:, kc, :], in_=pc[:, :], func=AF.Identity, bias=bd, scale=1.0),
                       reads=[pcR], writes=[wcR[kc]])
                sch.op("act", lambda e: e.activation(out=wsq[:, kc, :], in_=pc[:, :], func=AF.Square, bias=bd, scale=1.0),
                       reads=[pcR], writes=[wsR[kc]])
                sch.op("pool", lambda e: e.tensor_copy(out=wb[:, kc, :], in_=wc[:, kc, :]), reads=[wcR[kc]], writes=[wbR[kc]])
            pss, pssR = PSS

            def mm2(e):
                for q, (src_, _) in enumerate(((wb, wbR), (wsq, wsR))):
                    for kc in range(8):
                        ins = e.matmul(pss[:, q, :], lhsT=W["onesm"][:], rhs=src_[:, kc, :], start=(kc == 0), stop=(kc == 7))
                return ins
            sch.op("pe", mm2, reads=wbR + wsR + [W["cR"]], writes=[pssR])
            mean, meanR = MEANS[m % NB2]
            msq, msqR = MSQS[0]
            rstd, rstdR = RSTDS[m % NB2]
            sch.op("act", lambda e: e.activation(out=mean[:], in_=pss[:, 0, :], func=AF.Identity, scale=1.0 / D),
                   reads=[pssR], writes=[meanR])
            sch.op("dve", lambda e: e.tensor_tensor(out=msq[:], in0=mean[:], in1=mean[:], op=ALU.mult),
                   reads=[meanR], writes=[msqR])
            sch.op("dve", lambda e: e.scalar_tensor_tensor(out=msq[:], in0=pss[:, 1, :], scalar=1.0 / D, in1=msq[:],
                                                           op0=ALU.mult, op1=ALU.subtract),
                   reads=[pssR, msqR], writes=[msqR])
            sch.op("act", lambda e: e.activation(out=rstd[:], in_=msq[:], func=AF.Sqrt, bias=self.epsT[:, 0:1], scale=1.0),
                   reads=[msqR], writes=[rstdR])
            sch.op("dve", lambda e: e.reciprocal(out=rstd[:], in_=rstd[:]), reads=[rstdR], writes=[rstdR])

        def stageB2(m):
            wc, wcR = WCS[m % NB2]
            zt, ztR = ZT[m % NB2]
            mean, meanR = MEANS[m % NB2]
            rstd, rstdR = RSTDS[m % NB2]
            for kc in range(8):
                sch.op("dve", lambda e: e.tensor_tensor(out=wc[:, kc, :], in0=wc[:, kc, :], in1=mean[:], op=ALU.subtract),
                       reads=[wcR[kc], meanR], writes=[wcR[kc]])
            for kc in range(8):
                sch.op("pool" if kc % 2 else "dve",
                       lambda e: e.tensor_tensor(out=wc[:, kc, :], in0=wc[:, kc, :], in1=rstd[:], op=ALU.mult),
                       reads=[wcR[kc], rstdR], writes=[wcR[kc]])
            for kc in range(8):
                sch.op("act", lambda e: e.activation(out=zt[:, kc, :], in_=wc[:, kc, :], func=AF.Silu,
                                                     scale=self.small[:, lgo + kc:lgo + kc + 1],
                                                     bias=self.small[:, lbo + kc:lbo + kc + 1]),
                       reads=[wcR[kc]], writes=[ztR])

        def stageB2b(m):
            zt, ztR = ZT[m % NB2]
            for jt in range(tpu):
                n = m * tpu + jt
                py, pyR = PSY

                def mm3(e):
                    for nb in range(2):
                        e.matmul(py[:, nb * 512:(nb + 1) * 512], lhsT=W["onesr"][0:1, :],
                                 rhs=W["b2r"][0:1, nb * 512:(nb + 1) * 512], start=True, stop=False)
                        for kc in range(8):
                            ins = e.matmul(py[:, nb * 512:(nb + 1) * 512], lhsT=zt[:, kc, jt * 128:(jt + 1) * 128],
                                           rhs=w2[:, kc, nb * 512:(nb + 1) * 512], start=False, stop=(kc == 7))
                    return ins
                sch.op("pe", mm3, reads=[ztR, W["w2R"], W["cR"]], writes=[pyR])
                self.resid_stage(RS, n, py, pyR, src, dst, s)

        nargs = (U, NT, src, s, l, 0, s)
        self.stage_norm_pre(ns, 0, *nargs)
        for step in range(NU + 3):
            self.flush_stores()
            if step < NU:
                self.stage_norm_post(ns, step, *nargs)
            if 0 <= step - 3 < NU:
                stageB2(step - 3)
            if 0 <= step - 1 < NU:
                stageA(step - 1)
            if step + 1 < NU:
                self.stage_norm_pre(ns, step + 1, *nargs)
            if 0 <= step - 2 < NU:
                stageB1(step - 2)
            if 0 <= step - 3 < NU:
                stageB2b(step - 3)
        self.flush_stores()


    def phase_attn(self, l, src, dst):
        nc, sch = self.nc, self.sch
        j = l // 2
        with_ctx_out = l < DEPTH - 1
        with contextlib.ExitStack() as pes:
            T = lambda name, shape, dtp: pes.enter_context(nc.sbuf_tensor(self.uniq(name), shape, dtp))
            wqkv = T("wqkv", [128, 8, 1536], BF16)
            wo = T("wo", [128, 8, D], BF16)
            masks = T("masks", [128, 2, 512], BF16)
            rope = T("rope", [128, self.S // 128, 64], F32)
            esink = T("esink", [128, 16], F32)
            wqR, woR, cR = Res("wqkv"), Res("wo"), Res("aconst")
            self.load_w_bf16("wqkv", wqkv, self.at_wqkv[j], wqR)
            self.load_w_bf16("wo", wo, self.at_wo[j], woR)
            sch.barrier()
            sch.dma("pool", "masks", out=masks[:], in_=self.masks_d, writes=[cR])
            sch.dma("sp", "rope", out=rope[:], in_=self.rope_d, writes=[cR])
            so = self.off["sink"][0] + j * 16
            sch.op("act", lambda e: e.activation(out=esink[:], in_=self.small[:, so:so + 16], func=AF.Exp),
                   reads=[self.smallR], writes=[cR])
            ckt = [(T(f"ckt{i}", [128, 4, 128], BF16), Res(f"ckt{i}")) for i in range(2)]
            for kt_, ktR_ in ckt:
                sch.op("pool", lambda e: e.memset(kt_[:], 0.0), writes=[ktR_])
            cva = [(T(f"cva{i}", [128, 4, 65], BF16), Res(f"cva{i}")) for i in range(2)]
            W = dict(wqkv=wqkv, wo=wo, masks=masks, rope=rope, esink=esink, wqR=wqR, woR=woR, cR=cR, ckt=ckt, cva=cva)
            for seq, Tn in [(1, CTX), (0, self.S)]:
                if seq == 0:
                    self.fold_gate(l, 2, [(wo[:, kc, :], woR, 128) for kc in range(8)])
                with contextlib.ExitStack() as ses:
                    self.attn_seq(ses, l, seq, Tn, src, dst, W, with_ctx_out, fold=(seq == 0))
                    sch.barrier()

    def attn_seq(self, pes, l, s, Tn, src, dst, W, with_ctx_out, fold=False):
        nc, sch = self.nc, self.sch
        T = lambda name, shape, dtp: pes.enter_context(nc.sbuf_tensor(self.uniq(name), shape, dtp))
        P = lambda name, shape, dtp: pes.enter_context(nc.psum_tensor(self.uniq(name), shape, dtp))
        NT = Tn // 128
        is_ctx = (s == 1)
        do_out = (not is_ctx) or with_ctx_out
        psQ = (P("psq", [128, 1536], F32), Res("psq01"))
        pkvR = Res("psq2")
        PSS = [(P(f"pss{i}", [128, 512], F32), Res(f"pss{i}")) for i in range(3)]
        ns = self.alloc_norm(pes, 128)
        psT, psTR = ns["psT"]
        PSO = []
        bankx = P("pso0", [128, 512], F32)
        psO0 = (bankx, Res("pso0"))
        psKT = (bankx[:, 384:512].bitcast(BF16).rearrange("p (c t) -> p c t", c=2), psO0[1])
        PSO = [psO0]
        RS = self.alloc_resid(pes, l, 2, s, fold=fold)
        R = 5
        if is_ctx:
            KT, VA = W["ckt"], W["cva"]
            R = 2
        else:
            KT = [(T(f"kt{i}", [128, 4, 128], BF16), Res(f"kt{i}")) for i in range(R)]
            for kt_, ktR_ in KT:
                sch.op("pool", lambda e: e.memset(kt_[:], 0.0), writes=[ktR_])
            VA = [(T(f"va{i}", [128, 4, 65], BF16), Res(f"va{i}")) for i in range(R)]
        for va, vaR in VA:
            sch.op("pool", lambda e: e.memset(va[:], 1.0), writes=[vaR])
        QT = [(T(f"qt{i}", [128, 8, 128], BF16), Res(f"qt{i}")) for i in range(3)]
        QR = [(T(f"qr{i}", [128, 1024], BF16), Res(f"qr{i}")) for i in range(2)]
        KR = [(T(f"kr{i}", [128, 256], BF16), Res(f"kr{i}")) for i in range(2)]
        TQ = [(T(f"tq{i}", [128, 16, 32], F32), Res(f"tq{i}")) for i in range(2)]
        TK = [(T(f"tk{i}", [128, 4, 32], F32), Res(f"tk{i}")) for i in range(2)]
        PT = [(T(f"pt{i}", [128, 512], BF16), Res(f"pt{i}")) for i in range(5)]
        OS = [(T(f"os{i}", [128, 260], F32), Res(f"os{i}")) for i in range(2)]
        OB = [(T(f"ob{i}", [128, 1024], BF16), Res(f"ob{i}")) for i in range(2)]
        OT = [(T(f"ot{i}", [128, 8, 128], BF16), Res(f"ot{i}")) for i in range(2)]
        DEN = [(T(f"den{i}", [128, 8], F32), Res(f"den{i}")) for i in range(2)]
        wqkv, wo, masks, rope, esink = W["wqkv"], W["wo"], W["masks"], W["rope"], W["esink"]
        pq, pqR = psQ
        ptc = [0]

        def stageQa(n):
            bt, btR = ns["BT"][n % 2]

            def mm(e):
                for nb in range(3):
                    for kc in range(8):
                        ins = e.matmul(pq[:, nb * 512:(nb + 1) * 512], lhsT=bt[:, kc, :], rhs=wqkv[:, kc, nb * 512:(nb + 1) * 512],
                                       start=(kc == 0), stop=(kc == 7))
                return ins
            sch.op("pe", mm, reads=[btR, W["wqR"]], writes=[pqR, pkvR])
            qr, qrR = QR[n % 2]
            kr, krR = KR[n % 2]
            va, vaR = VA[n % R]
            qv = pq[:, 0:1024].rearrange("p (h d) -> p h d", h=16)
            kv = pq[:, 1024:1280].rearrange("p (h d) -> p h d", h=4)
            qrv = qr[:, :].rearrange("p (h d) -> p h d", h=16)
            krv = kr[:, :].rearrange("p (h d) -> p h d", h=4)
            sch.op("act", lambda e: e.activation(out=va[:, :, 0:64], in_=pq[:, 1280:1536].rearrange("p (h d) -> p h d", h=4),
                                                 func=AF.Identity), reads=[pkvR], writes=[vaR])
            if is_ctx:
                if do_out:
                    sch.op("act", lambda e: e.activation(out=qr[:], in_=pq[:, 0:1024], func=AF.Identity), reads=[pqR], writes=[qrR])
                sch.op("act", lambda e: e.activation(out=kr[:], in_=pq[:, 1024:1280], func=AF.Identity), reads=[pkvR], writes=[krR])
            else:
                for (xv, ov, nh, tbufs, oR, pR) in ((kv, krv, 4, TK, krR, pkvR), (qv, qrv, 16, TQ, qrR, pqR)):
                    cosb = rope[:, n, 0:32].unsqueeze(1).to_broadcast([128, nh, 32])
                    sinb = rope[:, n, 32:64].unsqueeze(1).to_broadcast([128, nh, 32])
                    (t1, t1R), (t2, t2R) = tbufs
                    x1, x2 = xv[:, :, 0:32], xv[:, :, 32:64]
                    sch.op("dve", lambda e: e.tensor_tensor(out=t1[:], in0=x1, in1=cosb, op=ALU.mult), reads=[pR, W["cR"]], writes=[t1R])
                    sch.op("dve", lambda e: e.tensor_tensor(out=t2[:], in0=x2, in1=sinb, op=ALU.mult), reads=[pR, W["cR"]], writes=[t2R])
                    sch.op("dve", lambda e: e.tensor_tensor(out=ov[:, :, 0:32], in0=t1[:], in1=t2[:], op=ALU.subtract),
                           reads=[t1R, t2R], writes=[oR])
                    sch.op("dve", lambda e: e.tensor_tensor(out=t1[:], in0=x2, in1=cosb, op=ALU.mult), reads=[pR, W["cR"]], writes=[t1R])
                    sch.op("dve", lambda e: e.tensor_tensor(out=t2[:], in0=x1, in1=sinb, op=ALU.mult), reads=[pR, W["cR"]], writes=[t2R])
                    sch.op("dve", lambda e: e.tensor_tensor(out=ov[:, :, 32:64], in0=t1[:], in1=t2[:], op=ALU.add),
                           reads=[t1R, t2R], writes=[oR])

        def stageQb(n):
            qr, qrR = QR[n % 2]
            kr, krR = KR[n % 2]
            kt, ktR = KT[n % R]
            pk, pkR = psKT

            def trk(e):
                for c in range(2):
                    ins = e.transpose(out=pk[:, c, :], in_=kr[:, c * 128:(c + 1) * 128], identity=self.identb[:])
                return ins
            sch.op("pe", trk, reads=[krR], writes=[pkR])
            ktv = kt[:, :, :].rearrange("p (c two) t -> p c two t", two=2)
            sch.op("act", lambda e: e.activation(out=ktv[0:64, :, 0, :], in_=pk[0:64, :, :], func=AF.Identity), reads=[pkR], writes=[ktR])
            sch.op("act", lambda e: e.activation(out=ktv[64:128, :, 1, :], in_=pk[64:128, :, :], func=AF.Identity), reads=[pkR], writes=[ktR])
            if do_out:
                qt, qtR = QT[n % 3]

                def trq(e):
                    for c in range(8):
                        ins = e.transpose(out=psT[:, c, :], in_=qr[:, c * 128:(c + 1) * 128], identity=self.identb[:])
                    return ins
                sch.op("pe", trq, reads=[qrR], writes=[psTR])
                sch.op("act", lambda e: e.activation(out=qt[:], in_=psT[:], func=AF.Identity), reads=[psTR], writes=[qtR])

        def stageS(i):
            qt, qtR = QT[i % 3]
            ob, obR = OB[i % 2]
            chunks = []
            chunks.append((W["ckt"][0], W["cva"][0], None))
            chunks.append((W["ckt"][1], W["cva"][1], None))
            if not is_ctx:
                chunks.append((KT[i % R], VA[i % R], None))
                if i - 1 >= 0:
                    chunks.append((KT[(i - 1) % R], VA[(i - 1) % R], 0))
                if i + 1 < NT:
                    chunks.append((KT[(i + 1) % R], VA[(i + 1) % R], 1))
            items = [(g, ci) for g in range(4) for ci in range(len(chunks))]
            LOOK = 2
            bufs = {}

            def score(k):
                g, ci = items[k]
                (kt, ktR), (va, vaR), mk = chunks[ci]
                hp = 64 * (g % 2)
                c0 = 4 * (g // 2)
                ps, psR = PSS[ptc[0] % len(PSS)]
                pt, ptR = PT[ptc[0] % len(PT)]
                ptc[0] += 1
                bufs[k] = (pt, ptR)
                sch.op("pe", lambda e: e.matmul(ps[:], lhsT=kt[:, g, :], rhs=qt[:, c0:c0 + 4, :],
                                                start=True, stop=True), reads=[ktR, qtR], writes=[psR])
                sch.op("act", lambda e: e.activation(out=pt[:], in_=ps[:], func=AF.Exp, scale=HD ** -0.5),
                       reads=[psR], writes=[ptR])
                if mk is not None:
                    sch.op("pool", lambda e: e.tensor_tensor(out=pt[:], in0=pt[:], in1=masks[:, mk, :], op=ALU.mult),
                           reads=[ptR, W["cR"]], writes=[ptR])

            def pvs(k):
                g, ci = items[k]
                (kt, ktR), (va, vaR), mk = chunks[ci]
                pt, ptR = bufs.pop(k)
                po, poR = PSO[0]

                def pv(e):
                    for hh in range(4):
                        ins = e.matmul(po[:, hh * 65:(hh + 1) * 65],
                                       lhsT=pt[:, hh * 128:(hh + 1) * 128], rhs=va[:, g, :],
                                       start=(ci == 0 and hh == 0), stop=(ci == len(chunks) - 1 and hh == 3))
                    return ins
                sch.op("pe", pv, reads=[ptR, vaR], writes=[poR])
                if ci == len(chunks) - 1:
                    osb, osR = OS[g % 2]
                    den, denR = DEN[g % 2]
                    sch.op("dve", lambda e: e.tensor_copy(out=osb[:], in_=po[:, 0:260]), reads=[poR], writes=[osR])
                    osv = osb[:, :].rearrange("p (h d) -> p h d", h=4)
                    sch.op("pool", lambda e: e.tensor_tensor(out=den[:, 0:4], in0=osv[:, :, 64], in1=esink[:, 4 * g:4 * g + 4], op=ALU.add),
                           reads=[osR, W["cR"]], writes=[denR])
                    sch.op("dve", lambda e: e.reciprocal(out=den[:, 4:8], in_=den[:, 0:4]), reads=[denR], writes=[denR])
                    obv = ob[:, 256 * g:256 * (g + 1)].rearrange("p (h d) -> p h d", h=4)
                    sch.op("dve", lambda e: e.tensor_tensor(out=obv, in0=osv[:, :, 0:64],
                                                            in1=den[:, 4:8].unsqueeze(2).to_broadcast([128, 4, 64]), op=ALU.mult),
                           reads=[osR, denR], writes=[obR])

            for k in range(len(items) + LOOK):
                if k < len(items):
                    score(k)
                if k - LOOK >= 0:
                    pvs(k - LOOK)

        def stageO(i):
            ob, obR = OB[i % 2]
            ot, otR = OT[i % 2]

            def tro(e):
                for c in range(8):
                    ins = e.transpose(out=psT[:, c, :], in_=ob[:, c * 128:(c + 1) * 128], identity=self.identb[:])
                return ins
            sch.op("pe", tro, reads=[obR], writes=[psTR])
            sch.op("act", lambda e: e.activation(out=ot[:], in_=psT[:], func=AF.Identity), reads=[psTR], writes=[otR])

            def mmo(e):
                for nb in range(2):
                    for kc in range(8):
                        ins = e.matmul(pq[:, nb * 512:(nb + 1) * 512], lhsT=ot[:, kc, :], rhs=wo[:, kc, nb * 512:(nb + 1) * 512],
                                       start=(kc == 0), stop=(kc == 7))
                return ins
            sch.op("pe", mmo, reads=[otR, W["woR"]], writes=[pqR])
            self.resid_stage(RS, i, pq[:, 0:1024], pqR, src, dst, s)

        nargs = (128, NT, src, s, l, 0, s)
        self.stage_norm_pre(ns, 0, *nargs)
        for step in range(NT + 5):
            self.flush_stores()
            if step < NT:
                self.stage_norm_post(ns, step, *nargs)
            if 0 <= step - 1 < NT:
                stageQa(step - 1)
            if step + 1 < NT:
                self.stage_norm_pre(ns, step + 1, *nargs)
            if do_out and 0 <= step - 3 < NT:
                stageS(step - 3)
            if do_out and 0 <= step - 4 < NT:
                stageO(step - 4)
            if 0 <= step - 1 < NT:
                stageQb(step - 1)
        self.flush_stores()


QPERM = np.concatenate([np.concatenate([np.arange(lo * 64, lo * 64 + 64), np.arange(hi * 64, hi * 64 + 64)])
                        for lo, hi in [(c, 4 + c) if c < 4 else (8 + c - 4, 12 + c - 4) for c in range(8)]])


def rope_table(S):
    t = np.arange(S)
    row = (t // 64).astype(np.float32)
    col = (t % 64).astype(np.float32)
    inv = (np.float32(10000.0) ** (-(np.arange(16, dtype=np.float32)) / np.float32(16))).astype(np.float32)
    ang = np.concatenate([row[:, None] * inv, col[:, None] * inv], axis=-1).astype(np.float32)
    tab = np.concatenate([np.cos(ang), np.sin(ang)], axis=-1).astype(np.float32)
    return np.ascontiguousarray(tab.reshape(S // 128, 128, 64).transpose(1, 0, 2))


def mask_table():
    kl = np.arange(128)[:, None]
    ql = np.arange(128)[None, :]
    mprev = (kl >= ql).astype(np.float32)
    mnext = (kl <= ql).astype(np.float32)
    m = np.stack([np.tile(mprev, (1, 4)), np.tile(mnext, (1, 4))], axis=1)
    return np.ascontiguousarray(m)


def make_in_map(inp, b):
    S = inp["x"].shape[1]
    wqkv = np.asarray(inp["attn_w_qkv"], np.float32)
    wqkv_p = np.concatenate([wqkv[:, :, :1024][:, :, QPERM], wqkv[:, :, 1024:]], axis=-1)
    m = {
        "attn_w_qkv": wqkv_p, "attn_w_o": inp["attn_w_o"], "masks": mask_table(), "rope": rope_table(S),
        "x": np.ascontiguousarray(inp["x"][b]),
        "ctx": np.ascontiguousarray(inp["ctx"][b]),
        "small": pack_small(inp, b),
        "ident": np.eye(128, dtype=np.float32),
        "ada_w": inp["ada_w"], "ada_b": inp["ada_b"],
        "final_g": np.ascontiguousarray(inp["final_g"]).reshape(1, D),
        "ffn_w_up": inp["ffn_w_up"], "ffn_w_down": inp["ffn_w_down"],
        "conv_w_pw1": inp["conv_w_pw1"], "conv_w_pw2": inp["conv_w_pw2"], "conv_b_pw2": inp["conv_b_pw2"],
    }
    return {k: np.ascontiguousarray(v, dtype=np.float32) for k, v in m.items()}


def kernel(**inputs):
    inp = {k: np.asarray(v) for k, v in inputs.items()}
    B, S, _ = inp["x"].shape
    prog = Prog(S)
    nc = prog.build()
    in_maps = [make_in_map(inp, b) for b in range(B)]
    res = run_bass_kernel_spmd(nc, in_maps, core_ids=list(range(B)))
    return np.stack([r["out"] for r in res.results], axis=0).astype(np.float32)
```

```python
import contextlib
import numpy as np
import concourse.bass as bass
import concourse.mybir as mybir
from concourse.bass_utils import run_bass_kernel_spmd

F32, BF16 = mybir.dt.float32, mybir.dt.bfloat16
AF = mybir.ActivationFunctionType
ALU = mybir.AluOpType

D = 1024
CTX = 256
DFF = 2816
NPAIR = 22
DEPTH = 4
EPS = 1e-6
CW = 31
HD = 64
NH = 16
NKV = 4


class Res:
    __slots__ = ("name", "last_w", "readers", "excl")

    def __init__(self, name, excl=False):
        self.name = name
        self.last_w = None
        self.readers = {}
        self.excl = excl or name.startswith("ps")


class Sched:
    def __init__(self, nc, es):
        self.nc = nc
        self.es = es
        self.eng = {"pe": nc.tensor, "act": nc.scalar, "dve": nc.vector, "pool": nc.gpsimd, "sp": nc.sync}
        self.sem = {k: es.enter_context(nc.semaphore("sem_" + k)) for k in ("pe", "act", "dve", "pool")}
        self.cnt = {k: 0 for k in self.sem}
        self.waited = {k: {} for k in self.eng}
        self.chans = {}

    def _deps(self, reads, writes, e=None):
        deps = []
        for r in reads:
            if r.last_w is not None:
                deps.append(r.last_w)
            if r.excl:
                deps.extend(t for k, t in r.readers.items() if k != e)
        for w in writes:
            if w.last_w is not None:
                deps.append(w.last_w)
            deps.extend(w.readers.values())
        return deps

    def _wait(self, e, deps):
        best = {}
        for s, v, key in deps:
            if key not in best or best[key][1] < v:
                best[key] = (s, v)
        for key, (s, v) in best.items():
            if self.waited[e].get(key, 0) < v:
                self.eng[e].wait_ge(s, v)
                self.waited[e][key] = v

    def _mark(self, tok, reads, writes):
        for w in writes:
            w.last_w = tok
            w.readers = {}
        for r in reads:
            if r not in writes:
                r.readers[tok[2]] = tok

    def op(self, e, fn, reads=(), writes=()):
        deps = self._deps(reads, writes, e)
        if e == "pe":
            deps = [d for d in deps if d[2] != "pe"]
        self._wait(e, deps)
        ins = fn(self.eng[e])
        self.cnt[e] += 1
        ins.then_inc(self.sem[e], 1)
        tok = (self.sem[e], self.cnt[e], e)
        self._mark(tok, reads, writes)
        return tok

    def dma(self, q, ch, out, in_, reads=(), writes=(), **kw):
        if ch not in self.chans:
            self.chans[ch] = [self.es.enter_context(self.nc.semaphore("ch_" + ch)), 0]
        c = self.chans[ch]
        deps = self._deps(reads, writes)
        if c[1] > 0:
            deps.append((c[0], c[1], "ch_" + ch))
        self._wait(q, deps)
        self.eng[q].dma_start(out=out, in_=in_, **kw).then_inc(c[0], 16)
        c[1] += 16
        tok = (c[0], c[1], "ch_" + ch)
        self._mark(tok, reads, writes)
        return tok

    def barrier(self, engines=("pe", "act", "dve", "pool", "sp")):
        deps = [(self.sem[k], self.cnt[k], k) for k in self.sem if self.cnt[k] > 0]
        deps += [(c[0], c[1], "ch_" + n) for n, c in self.chans.items() if c[1] > 0]
        for e in engines:
            self._wait(e, deps)


def _fm(v, nch):
    v = np.asarray(v, np.float32)
    lead = v.shape[:-1]
    return np.moveaxis(v.reshape(lead + (nch, 128)), -1, 0)


class SmallPack:
    def __init__(self):
        self.parts = []
        self.off = {}
        self.n = 0

    def add(self, name, arr):
        arr = np.ascontiguousarray(arr, np.float32).reshape(128, -1)
        self.off[name] = self.n
        self.parts.append(arr)
        self.n += arr.shape[1]

    def array(self):
        return np.ascontiguousarray(np.concatenate(self.parts, axis=1))


def small_layout():
    off = {}
    n = 0
    for name, w in (("c2", 16), ("ng2", DEPTH * 2 * 8 * 2), ("ffn_wdw", DEPTH * 44 * 3), ("ffn_bdw", DEPTH * 44),
                    ("cv_b1", 2 * 16), ("cv_wdw", 2 * 8 * CW), ("cv_bdw", 2 * 8), ("cv_lng", 2 * 8), ("cv_lnb", 2 * 8), ("sink", 2 * 16)):
        off[name] = (n, w)
        n += w
    return off, n


def pack_small(inp, b):
    sp = SmallPack()
    c2 = np.stack([_fm(inp["c"][b], 8), _fm(inp["c_ctx"], 8)], axis=-1)
    sp.add("c2", c2)
    ng = np.stack([_fm(inp["norm1_g"], 8), _fm(inp["norm2_g"], 8)], axis=2)
    ng2 = np.repeat(ng[..., None], 2, axis=-1)
    sp.add("ng2", ng2)
    wdw = _fm(inp["ffn_w_dw"], 44)
    sp.add("ffn_wdw", np.transpose(wdw, (0, 1, 3, 2)))
    sp.add("ffn_bdw", _fm(inp["ffn_b_dw"], 44))
    sp.add("cv_b1", _fm(inp["conv_b_pw1"], 16))
    cw = _fm(inp["conv_w_dw"], 8)
    sp.add("cv_wdw", np.transpose(cw, (0, 1, 3, 2)))
    sp.add("cv_bdw", _fm(inp["conv_b_dw"], 8))
    sp.add("cv_lng", _fm(inp["conv_ln_g"], 8))
    sp.add("cv_lnb", _fm(inp["conv_ln_b"], 8))
    sp.add("sink", np.broadcast_to(np.asarray(inp["attn_sink"], np.float32).reshape(1, 32), (128, 32)))
    off, n = small_layout()
    assert sp.n == n and all(sp.off[k] == off[k][0] for k in off)
    return sp.array()


class Prog:
    def __init__(self, S, phases=None, U=256, debug=False):
        self.S = S
        self.U = U
        self.phases = phases
        self.debug = debug
        self.nc = nc = bass.Bass("TRN2", target_bir_lowering=False)
        dt = nc.dram_tensor
        self.off, self.NS = small_layout()
        self.x = dt("x", [S, D], F32, kind="ExternalInput").ap()
        self.ctx = dt("ctx", [CTX, D], F32, kind="ExternalInput").ap()
        self.small_d = dt("small", [128, self.NS], F32, kind="ExternalInput").ap()
        self.ident_d = dt("ident", [128, 128], F32, kind="ExternalInput").ap()
        self.ada_w = dt("ada_w", [DEPTH, D, 6 * D], F32, kind="ExternalInput").ap()
        self.ada_b = dt("ada_b", [DEPTH, 6 * D], F32, kind="ExternalInput").ap()
        self.final_g = dt("final_g", [1, D], F32, kind="ExternalInput").ap()
        self.w_up = dt("ffn_w_up", [DEPTH, D, 2 * DFF], F32, kind="ExternalInput").ap()
        self.w_dn = dt("ffn_w_down", [DEPTH, DFF, D], F32, kind="ExternalInput").ap()
        self.cv_w1 = dt("conv_w_pw1", [2, D, 2 * D], F32, kind="ExternalInput").ap()
        self.cv_w2 = dt("conv_w_pw2", [2, D, D], F32, kind="ExternalInput").ap()
        self.cv_b2 = dt("conv_b_pw2", [2, D], F32, kind="ExternalInput").ap()
        self.at_wqkv = dt("attn_w_qkv", [2, D, 1536], F32, kind="ExternalInput").ap()
        self.at_wo = dt("attn_w_o", [2, D, D], F32, kind="ExternalInput").ap()
        self.masks_d = dt("masks", [128, 2, 512], F32, kind="ExternalInput").ap()
        self.rope_d = dt("rope", [128, S // 128, 64], F32, kind="ExternalInput").ap()
        self.out = dt("out", [S, D], F32, kind="ExternalOutput").ap()
        self.hbuf = [dt("hA", [S + CTX, D], F32).ap(), dt("hB", [S + CTX, D], F32).ap()]
        self.modd = dt("modd", [DEPTH, 2, 6 * D], F32).ap()
        if debug:
            self.dbg = dt("dbg", [CTX, D], F32, kind="ExternalOutput").ap()

    def uniq(self, name):
        self._uid = getattr(self, "_uid", 0) + 1
        return f"t{self._uid}_{name}"

    def sm(self, name, idx, width=1):
        o = self.off[name][0] + idx
        return self.small[:, o:o + width]

    def rows(self, buf, seq, r0, n):
        if buf == "in":
            src = self.x if seq == 0 else self.ctx
            return src[r0:r0 + n, :]
        if buf == "out":
            return self.out[r0:r0 + n, :]
        base = 0 if seq == 0 else self.S
        return self.hbuf[buf][base + r0:base + r0 + n, :]

    def build(self):
        nc = self.nc
        with contextlib.ExitStack() as es:
            self.es = es
            self.sch = sch = Sched(nc, es)
            T = lambda name, shape, dtp: es.enter_context(nc.sbuf_tensor(self.uniq(name), shape, dtp))
            self.small = T("small", [128, self.NS], F32)
            self.identf = T("identf", [128, 128], F32)
            self.identb = T("identb", [128, 128], BF16)
            self.epsT = T("epsT", [128, 4], F32)
            self.FM = T("FM", [128, DEPTH * 96], F32)
            self.GM = T("GM", [128, DEPTH * 2 * 16], F32)
            self.smallR = Res("small")
            self.identR = Res("ident")
            self.fmR = Res("FM")
            self.gmR = Res("GM")
            sch.dma("sp", "small", out=self.small[:], in_=self.small_d, writes=[self.smallR])
            sch.dma("sp", "identf", out=self.identf[:], in_=self.ident_d, writes=[self.identR])
            sch.dma("pool", "identb", out=self.identb[:], in_=self.ident_d, writes=[self.identR])
            epsR = Res("eps")
            sch.op("dve", lambda e: e.memset(self.epsT[:, 0:1], EPS), writes=[epsR])
            sch.op("dve", lambda e: e.memset(self.epsT[:, 1:2], 0.0), writes=[epsR])
            sch.op("dve", lambda e: e.memset(self.epsT[:, 2:3], -0.5), writes=[epsR])
            sch.barrier()
            self.prologue()
            phases = []
            src = "in"
            for l in range(DEPTH):
                phases.append(("mix", l, src, 1))
                phases.append(("ffn", l, 1, 0))
                src = 0
            phases.append(("final", 0, 0, "out"))
            if self.phases is not None:
                phases = self.phases
            for kind, l, s_, d_ in phases:
                if kind == "mix":
                    if l % 2 == 0:
                        self.phase_conv(l, s_, d_)
                    else:
                        self.phase_attn(l, s_, d_)
                elif kind == "ffn":
                    self.phase_ffn(l, s_, d_)
                else:
                    self.phase_final(s_, d_)
                sch.barrier()
            if self.debug:
                lastbuf = [p for p in phases if p[0] != "final"][-1][3]
                sch.dma("sp", "dbg", out=self.dbg, in_=self.hbuf[lastbuf][self.S:self.S + CTX, :])
            sch.barrier()
        return nc

    def prologue(self):
        nc, sch = self.nc, self.sch
        with contextlib.ExitStack() as pes:
            T = lambda name, shape, dtp: pes.enter_context(nc.sbuf_tensor(self.uniq(name), shape, dtp))
            P = lambda name, shape, dtp: pes.enter_context(nc.psum_tensor(self.uniq(name), shape, dtp))
            scT = T("scT", [128, 16], F32)
            wch = [T(f"adaw{i}", [128, 8, 512], F32) for i in range(2)]
            wchR = [Res(f"adaw{i}") for i in range(2)]
            adab = T("adab", [2, 6 * D], F32)
            modrow = T("modrow", [2, 6 * D], F32)
            psr = [P(f"psr{i}", [2, 512], F32) for i in range(2)]
            psrR = [Res(f"psr{i}") for i in range(2)]
            psf = P("psf", [128, 96], F32)
            psfR = Res("psf")
            scR, adabR, modR = Res("scT"), Res("adab"), Res("modrow")
            c0 = self.off["c2"][0]
            sch.op("act", lambda e: e.activation(out=scT[:], in_=self.small[:, c0:c0 + 16], func=AF.Silu),
                   reads=[self.smallR], writes=[scR])
            for l in range(DEPTH):
                for s in range(2):
                    sch.dma("sp", f"adab{s}", out=adab[s:s + 1, :], in_=self.ada_b[l:l + 1, :], writes=[adabR])
                for j in range(12):
                    w, wR = wch[j % 2], wchR[j % 2]
                    sch.dma("sp", f"adaw{j % 2}", out=w[:],
                            in_=self.ada_w[l, :, j * 512:(j + 1) * 512].rearrange("(kc p) n -> p kc n", p=128),
                            writes=[wR])
                    ps, psR = psr[j % 2], psrR[j % 2]

                    def mm(e):
                        for kc in range(8):
                            ins = e.matmul(ps[:], lhsT=scT[:, kc * 2:kc * 2 + 2], rhs=w[:, kc, :],
                                           start=(kc == 0), stop=(kc == 7))
                        return ins
                    sch.op("pe", mm, reads=[scR, wR], writes=[psR])
                    sch.op("dve", lambda e: e.tensor_tensor(out=modrow[:, j * 512:(j + 1) * 512], in0=ps[:],
                                                            in1=adab[:, j * 512:(j + 1) * 512], op=ALU.add),
                           reads=[psR, adabR], writes=[modR])
                sch.dma("sp", "modd", out=self.modd[l], in_=modrow[:], reads=[modR])

                def tr(e):
                    for j in range(48):
                        ins = e.transpose(out=psf[:, j * 2:j * 2 + 2], in_=modrow[0:2, j * 128:(j + 1) * 128],
                                          identity=self.identf[0:2, 0:2])
                    return ins
                sch.op("pe", tr, reads=[modR, self.identR], writes=[psfR])
                sch.op("act", lambda e: e.activation(out=self.FM[:, l * 96:(l + 1) * 96], in_=psf[:], func=AF.Identity),
                       reads=[psfR], writes=[self.fmR])
                for which, vec in ((0, 1), (1, 4)):
                    go = (l * 2 + which) * 16
                    fo = l * 96 + vec * 16
                    no = self.off["ng2"][0] + (l * 2 + which) * 16
                    sch.op("dve", lambda e: e.scalar_tensor_tensor(out=self.GM[:, go:go + 16], in0=self.FM[:, fo:fo + 16],
                                                                   scalar=1.0, in1=self.small[:, no:no + 16],
                                                                   op0=ALU.add, op1=ALU.mult),
                           reads=[self.fmR, self.smallR], writes=[self.gmR])
            sch.barrier()

    def gm(self, l, which, kc, s):
        o = (l * 2 + which) * 16 + kc * 2 + s
        return self.GM[:, o:o + 1]

    def fmv(self, l, vec, kc, s):
        o = l * 96 + vec * 16 + kc * 2 + s
        return self.FM[:, o:o + 1]

    def load_w_bf16(self, name, dst, src3, res, nsplit=1):
        K = dst.shape[1]
        N = dst.shape[2]
        nsplit = max(nsplit, (N + 2047) // 2048)
        step = (N + nsplit - 1) // nsplit
        i = 0
        for kc in range(K):
            for c0 in range(0, N, step):
                c1 = min(N, c0 + step)
                self.sch.dma("pool", f"wld{i % 8}", out=dst[:, kc, c0:c1],
                             in_=src3[kc * 128:(kc + 1) * 128, c0:c1], max_dma_last_dim=8192)
                i += 1

    def alloc_norm(self, pes, U, nb_bt=2, nhin=2, npst=1):
        nc = self.nc
        T = lambda name, shape, dtp: pes.enter_context(nc.sbuf_tensor(self.uniq(name), shape, dtp))
        ns = {}
        ns["HIN"] = [(T(f"hin{i}", [128, D], F32), Res(f"hin{i}")) for i in range(nhin)]
        ns["XH"] = [(T(f"xh{i}", [128, D], BF16), Res(f"xh{i}")) for i in range(2)]
        ns["STAT"] = [(T(f"stat{i}", [128, 4], F32), Res(f"stat{i}")) for i in range(4)]
        ns["BT"] = [(T(f"bt{i}", [128, 8, U], BF16), Res(f"bt{i}")) for i in range(nb_bt)]
        ns["psTs"] = [(pes.enter_context(nc.psum_tensor(self.uniq("psT"), [128, 8, 128], BF16)), Res(f"psT{i}")) for i in range(npst)]
        ns["psT"] = ns["psTs"][0]
        return ns

    def stage_norm_pre(self, ns, m, U, ntile, src, seq, l, which, s):
        sch = self.sch
        tpu = U // 128
        for j in range(tpu):
            n = m * tpu + j
            if n >= ntile:
                break
            nhin = len(ns["HIN"])
            hin, hinR = ns["HIN"][n % nhin]
            xh, xhR = ns["XH"][n % 2]
            st, stR = ns["STAT"][n % 4]
            sch.dma("sp", f"hin{n % nhin}", out=hin[:], in_=self.rows(src, seq, n * 128, 128), writes=[hinR])
            sch.op("act", lambda e: e.activation(out=xh[:], in_=hin[:], func=AF.Square, accum_out=st[:, 0:1]),
                   reads=[hinR], writes=[xhR, stR])
            sch.op("pool", lambda e: e.tensor_scalar(out=st[:, 1:2], in0=st[:, 0:1], scalar1=1.0 / D, scalar2=EPS,
                                                     op0=ALU.mult, op1=ALU.add), reads=[stR], writes=[stR])
            sch.op("pool", lambda e: e.tensor_tensor(out=st[:, 2:3], in0=st[:, 1:2], in1=self.epsT[:, 2:3], op=ALU.pow),
                   reads=[stR], writes=[stR])
            sch.op("dve", lambda e: e.tensor_scalar(out=xh[:], in0=hin[:], scalar1=st[:, 2:3], scalar2=None,
                                                    op0=ALU.mult), reads=[hinR, stR], writes=[xhR])

    def stage_norm_post(self, ns, m, U, ntile, src, seq, l, which, s):
        sch = self.sch
        tpu = U // 128
        bt, btR = ns["BT"][m % len(ns["BT"])]
        vsh = 0 if which == 0 else 3
        for j in range(tpu):
            n = m * tpu + j
            if n >= ntile:
                break
            xh, xhR = ns["XH"][n % 2]
            psT, psTR = ns["psTs"][n % len(ns["psTs"])]

            def tr(e):
                for kc in range(8):
                    ins = e.transpose(out=psT[:, kc, :], in_=xh[:, kc * 128:(kc + 1) * 128],
                                      identity=self.identb[:])
                return ins
            sch.op("pe", tr, reads=[xhR], writes=[psTR])
            for kc in range(8):
                sch.op("act", lambda e: e.activation(out=bt[:, kc, j * 128:(j + 1) * 128], in_=psT[:, kc, :], func=AF.Identity,
                                                     scale=self.gm(l, which, kc, s), bias=self.fmv(l, vsh, kc, s)),
                       reads=[psTR], writes=[btR])
        return bt, btR

    def phase_ffn(self, l, src, dst):
        nc, sch = self.nc, self.sch
        U = self.U
        with contextlib.ExitStack() as pes:
            T = lambda name, shape, dtp: pes.enter_context(nc.sbuf_tensor(self.uniq(name), shape, dtp))
            wup = T("wup", [128, 8, 2 * DFF], BF16)
            wdn = T("wdn", [128, NPAIR, D], BF16)
            wupR, wdnR = Res("wup"), Res("wdn")
            self.load_w_bf16("wup", wup, self.w_up[l], wupR, nsplit=2)
            self.load_w_bf16("wdn", wdn, self.w_dn[l], wdnR)
            sch.barrier()
            seqs = [(1, CTX), (0, self.S)] if l < DEPTH - 1 else [(0, self.S)]
            for seq, Tn in seqs:
                fold = (seq == 0)
                if fold:
                    self.fold_gate(l, 5, [(wdn[:, kc, :], wdnR, 128) for kc in range(NPAIR)])
                with contextlib.ExitStack() as ses:
                    self.ffn_seq(ses, l, seq, Tn, src, dst, wup, wupR, wdn, wdnR, fold=fold)
                    sch.barrier()

    def ffn_seq(self, pes, l, s, Tn, src, dst, wup, wupR, wdn, wdnR, fold=False):
        nc, sch = self.nc, self.sch
        U = self.U
        T = lambda name, shape, dtp: pes.enter_context(nc.sbuf_tensor(self.uniq(name), shape, dtp))
        P = lambda name, shape, dtp: pes.enter_context(nc.psum_tensor(self.uniq(name), shape, dtp))
        NU = (Tn + U - 1) // U
        NT = Tn // 128
        tpu = U // 128
        ns = self.alloc_norm(pes, U, nhin=1, npst=1)
        GTS = [(T(f"GT{i}", [128, NPAIR, U + 1], BF16), Res(f"gt{i}")) for i in range(2)]
        CARRY = [T(f"carry{i}", [128, NPAIR, 2, 2], F32) for i in range(2)]
        carryR = [[Res(f"carry{i}_{c}") for c in range(NPAIR)] for i in range(2)]
        NTB = 6
        TB = [(T(f"tb{i}", [128, U + 1], F32), Res(f"tb{i}")) for i in range(NTB)]
        NUC = 3 if fold else 2
        UC = [(T(f"uc{i}", [128, 2, U + 3], F32), Res(f"uc{i}"), Res(f"ucc{i}")) for i in range(NUC)]
        NSA = 3 if fold else 2
        SA = [(T(f"sa{i}", [128, U + 1], F32), Res(f"sa{i}")) for i in range(NSA)]
        PSY = [(P(f"psy{i}", [128, D], F32), Res(f"psy{i}")) for i in range(2)]
        NPS = 3
        PSAB = [(P(f"psab{i}", [128, 2, U], F32), Res(f"psab{i}")) for i in range(NPS)]
        NHR = 3 if fold else 1
        HRES = [(T(f"hres{i}", [128, D], F32), Res(f"hres{i}")) for i in range(NHR)]
        if not fold:
            TMP = [(T("tmp0", [128, D], F32), Res("tmp0"))]
            gbc = T("gbc", [128, D], F32)
            gbcR = Res("gbc")
            sch.dma("sp", "gbc", out=gbc[:], in_=self.modd[l, s:s + 1, 5 * D:6 * D].partition_broadcast(128), writes=[gbcR])
        sch.op("pool", lambda e: e.memset(CARRY[0][:], 0.0), writes=carryR[0])
        for uc, ucR, uccR in UC:
            sch.op("pool", lambda e: e.memset(uc[:], 0.0), writes=[ucR, uccR])
        wo = self.off["ffn_wdw"][0] + l * 132
        bo = self.off["ffn_bdw"][0] + l * 44
        W3 = lambda cc, k: self.small[:, wo + cc * 3 + k: wo + cc * 3 + k + 1]
        B3 = lambda cc: self.small[:, bo + cc: bo + cc + 1]

        tail = [None]

        def stageU(m, c_lo=0, c_hi=NPAIR):
            last = (m == NU - 1)
            ext = 1 if last else 0
            bt, btR = ns["BT"][m % 2]
            cp, cpR = CARRY[m % 2], carryR[m % 2]
            cn, cnR = CARRY[(m + 1) % 2], carryR[(m + 1) % 2]
            GT, gR = GTS[m % 2]
            for c in range(c_lo, c_hi):
                it = m * NPAIR + c
                ps, psR = PSAB[it % NPS]

                def mm(e):
                    for hf in range(2):
                        col = hf * DFF + c * 128
                        for kc in range(8):
                            ins = e.matmul(ps[:, hf, :], lhsT=wup[:, kc, col:col + 128], rhs=bt[:, kc, :],
                                           start=(kc == 0), stop=(kc == 7))
                    return ins
                sch.op("pe", mm, reads=[btR, wupR], writes=[psR])
                tb = [TB[(2 * it) % NTB], TB[(2 * it + 1) % NTB]]
                uc, ucR, uccR = UC[it % NUC]
                sch.op("pool", lambda e: e.tensor_copy(out=uc[:, :, 0:2], in_=cp[:, c, :, :]), reads=[cpR[c]], writes=[uccR])
                sch.op("act", lambda e: e.activation(out=uc[:, :, 2:2 + U], in_=ps[:, :, :], func=AF.Identity),
                       reads=[psR], writes=[ucR])
                for hf in range(2):
                    cc = hf * NPAIR + c
                    Tt, TtR = tb[hf]
                    sch.op("act", lambda e: e.activation(out=Tt[:, 0:U], in_=ps[:, hf, :], func=AF.Identity,
                                                         scale=W3(cc, 2), bias=B3(cc)),
                           reads=[psR, self.smallR], writes=[TtR])
                    if ext:
                        sch.op("act", lambda e: e.activation(out=Tt[:, U:U + 1], in_=self.epsT[:, 1:2], func=AF.Identity,
                                                             scale=1.0, bias=B3(cc)), reads=[], writes=[TtR])
                if not last:
                    sch.op("pool", lambda e: e.tensor_copy(out=cn[:, c, :, :], in_=uc[:, :, U:U + 2]), reads=[ucR], writes=[cnR[c]])
                for k, o in ((1, 1), (0, 0)):
                    for hf in range(2):
                        cc = hf * NPAIR + c
                        Tt, TtR = tb[hf]
                        sch.op("dve", lambda e: e.scalar_tensor_tensor(out=Tt[:, 0:U + ext], in0=uc[:, hf, o:o + U + ext],
                                                                       scalar=W3(cc, k), in1=Tt[:, 0:U + ext],
                                                                       op0=ALU.mult, op1=ALU.add),
                               reads=[ucR, uccR, TtR], writes=[TtR])
                if tail[0] is not None:
                    tail[0]()

                def tail_fn(it=it, c=c, tb=tb, ext=ext, GT=GT, gR=gR):
                    sa, saR = SA[it % NSA]
                    sch.op("act", lambda e: e.activation(out=sa[:, 0:U + ext], in_=tb[0][0][:, 0:U + ext], func=AF.Silu),
                           reads=[tb[0][1]], writes=[saR])
                    sch.op("pool", lambda e: e.tensor_tensor(out=GT[:, c, 0:U + ext], in0=sa[:, 0:U + ext],
                                                             in1=tb[1][0][:, 0:U + ext], op=ALU.mult),
                           reads=[saR, tb[1][1]], writes=[gR])
                tail[0] = tail_fn
            if c_hi == NPAIR and tail[0] is not None:
                tail[0]()
                tail[0] = None

        def stageD(m):
            GT, gR = GTS[m % 2]
            last = (m == NU - 1)
            tiles = [(jt * 128, 128) for jt in range(tpu)] + ([(U, 1)] if last else [])
            for ti, (c0, ncol) in enumerate(tiles):
                tok0 = U * m - 1 + c0
                p0 = 1 if tok0 < 0 else 0
                py, pyR = PSY[dcount[0] % 2]

                def mm(e):
                    for nb in range(2):
                        for kc in range(NPAIR):
                            ins = e.matmul(py[0:ncol, nb * 512:(nb + 1) * 512], lhsT=GT[:, kc, c0:c0 + ncol],
                                           rhs=wdn[:, kc, nb * 512:(nb + 1) * 512], start=(kc == 0), stop=(kc == NPAIR - 1))
                    return ins
                sch.op("pe", mm, reads=[gR, wdnR], writes=[pyR])
                hi = dcount[0] % NHR
                dcount[0] += 1
                hr, hrR = HRES[hi]
                r0 = tok0 + p0
                nr = ncol - p0
                if p0:
                    sch.op("pool", lambda e: e.memset(hr[0:1, :], 0.0), writes=[hrR])
                sch.dma("sp", f"hres{hi}", out=hr[p0:p0 + nr, :], in_=self.rows(src, s, r0, nr), writes=[hrR])
                if fold:
                    sch.op("dve", lambda e: e.tensor_tensor(out=hr[0:ncol, :], in0=py[0:ncol, :], in1=hr[0:ncol, :], op=ALU.add),
                           reads=[pyR, hrR], writes=[hrR])
                else:
                    tm, tmR = TMP[0]
                    sch.op("dve", lambda e: e.tensor_tensor(out=tm[0:ncol, :], in0=py[0:ncol, :], in1=gbc[0:ncol, :], op=ALU.mult),
                           reads=[pyR, gbcR], writes=[tmR])
                    sch.op("pool", lambda e: e.tensor_tensor(out=hr[0:ncol, :], in0=tm[0:ncol, :], in1=hr[0:ncol, :], op=ALU.add),
                           reads=[tmR, hrR], writes=[hrR])
                self.pend_stores.append(
                    lambda hr=hr, hrR=hrR, hi=hi, p0=p0, nr=nr, r0=r0: sch.dma("sp", f"hres{hi}", out=self.rows(dst, s, r0, nr),
                                                                              in_=hr[p0:p0 + nr, :], reads=[hrR]))
                if not fold:
                    self.flush_stores()

        dcount = [0]
        self.pend_stores = []
        HALF = NPAIR // 2
        nargs = (U, NT, src, s, l, 1, s)
        self.stage_norm_pre(ns, 0, *nargs)
        for step in range(NU + 2):
            if step < NU:
                self.stage_norm_post(ns, step, *nargs)
            if 0 <= step - 1 < NU:
                stageU(step - 1, 0, HALF)
            if step + 1 < NU:
                self.stage_norm_pre(ns, step + 1, *nargs)
            if 0 <= step - 2 < NU:
                stageD(step - 2)
            if 0 <= step - 1 < NU:
                stageU(step - 1, HALF, NPAIR)
            self.flush_stores()

    def phase_final(self, src, dst):
        nc, sch = self.nc, self.sch
        with contextlib.ExitStack() as pes:
            T = lambda name, shape, dtp: pes.enter_context(nc.sbuf_tensor(self.uniq(name), shape, dtp))
            fg = T("fg", [128, D], F32)
            fgR = Res("fg")
            sch.dma("sp", "fg", out=fg[:], in_=self.final_g.partition_broadcast(128), writes=[fgR])
            HIN = [(T(f"fh{i}", [128, D], F32), Res(f"fh{i}")) for i in range(3)]
            SQ = [(T(f"fsq{i}", [128, D], F32), Res(f"fsq{i}")) for i in range(2)]
            STAT = [(T(f"fst{i}", [128, 4], F32), Res(f"fst{i}")) for i in range(3)]
            for n in range(self.S // 128):
                hin, hinR = HIN[n % 3]
                sq, sqR = SQ[n % 2]
                st, stR = STAT[n % 3]
                sch.dma("sp", f"fh{n % 3}", out=hin[:], in_=self.rows(src, 0, n * 128, 128), writes=[hinR])
                sch.op("act", lambda e: e.activation(out=sq[:], in_=hin[:], func=AF.Square, accum_out=st[:, 0:1]),
                       reads=[hinR], writes=[sqR, stR])
                sch.op("pool", lambda e: e.tensor_scalar(out=st[:, 1:2], in0=st[:, 0:1], scalar1=1.0 / D, scalar2=EPS,
                                                         op0=ALU.mult, op1=ALU.add), reads=[stR], writes=[stR])
                sch.op("pool", lambda e: e.tensor_tensor(out=st[:, 2:3], in0=st[:, 1:2], in1=self.epsT[:, 2:3], op=ALU.pow),
                       reads=[stR], writes=[stR])
                sch.op("dve", lambda e: e.scalar_tensor_tensor(out=hin[:], in0=hin[:], scalar=st[:, 2:3], in1=fg[:],
                                                               op0=ALU.mult, op1=ALU.mult),
                       reads=[hinR, stR, fgR], writes=[hinR])
                sch.dma("act", f"fh{n % 3}", out=self.rows(dst, 0, n * 128, 128), in_=hin[:], reads=[hinR])

    def flush_stores(self):
        for fn in getattr(self, "pend_stores", []):
            fn()
        self.pend_stores = []

    def resid_stage(self, RS, n, py, pyR, src, dst, seq):
        sch = self.sch
        nh = len(RS["HRES"])
        hr, hrR = RS["HRES"][n % nh]
        sch.dma("sp", f"hres{n % nh}", out=hr[:], in_=self.rows(src, seq, n * 128, 128), writes=[hrR])
        if RS["fold"]:
            sch.op("dve", lambda e: e.tensor_tensor(out=hr[:], in0=py[:], in1=hr[:], op=ALU.add),
                   reads=[pyR, hrR], writes=[hrR])
        else:
            tm, tmR = RS["TMP"][0]
            gbc, gbcR = RS["gbc"]
            sch.op("dve", lambda e: e.tensor_tensor(out=tm[:], in0=py[:], in1=gbc[:], op=ALU.mult),
                   reads=[pyR, gbcR], writes=[tmR])
            sch.op("pool", lambda e: e.tensor_tensor(out=hr[:], in0=tm[:], in1=hr[:], op=ALU.add),
                   reads=[tmR, hrR], writes=[hrR])
        if not hasattr(self, "pend_stores"):
            self.pend_stores = []
        self.pend_stores.append(lambda: sch.dma("sp", f"hres{n % nh}", out=self.rows(dst, seq, n * 128, 128), in_=hr[:], reads=[hrR]))

    def fold_gate(self, l, vec, targets):
        nc, sch = self.nc, self.sch
        with contextlib.ExitStack() as fes:
            gb = fes.enter_context(nc.sbuf_tensor(self.uniq("gfold"), [128, D], F32))
            gbR = Res("gfold")
            sch.dma("sp", "gfold", out=gb[:], in_=self.modd[l, 0:1, vec * D:(vec + 1) * D].partition_broadcast(128), writes=[gbR])
            for i, (ap, res, np_) in enumerate(targets):
                sch.op("dve" if i % 2 == 0 else "pool",
                       lambda e: e.tensor_tensor(out=ap, in0=ap, in1=gb[0:np_, :], op=ALU.mult),
                       reads=[gbR, res], writes=[res])
            sch.barrier()

    def alloc_resid(self, pes, l, vec, s, fold=False):
        nc, sch = self.nc, self.sch
        T = lambda name, shape, dtp: pes.enter_context(nc.sbuf_tensor(self.uniq(name), shape, dtp))
        RS = {"fold": fold}
        RS["HRES"] = [(T(f"hres{i}", [128, D], F32), Res(f"hres{i}")) for i in range(4 if fold else 2)]
        if fold:
            return RS
        RS["TMP"] = [(T("tmp", [128, D], F32), Res("tmp"))]
        gbc = T("gbc", [128, D], F32)
        gbcR = Res("gbc")
        sch.dma("sp", "gbc", out=gbc[:], in_=self.modd[l, s:s + 1, vec * D:(vec + 1) * D].partition_broadcast(128),
                writes=[gbcR])
        RS["gbc"] = (gbc, gbcR)
        return RS

    def phase_conv(self, l, src, dst):
        nc, sch = self.nc, self.sch
        j = l // 2
        with contextlib.ExitStack() as pes:
            T = lambda name, shape, dtp: pes.enter_context(nc.sbuf_tensor(self.uniq(name), shape, dtp))
            w1 = T("w1", [128, 8, 2 * D], BF16)
            w2 = T("w2", [128, 8, D], BF16)
            dg = T("dg", [128, 8, CW, 128], BF16)
            b2r = T("b2r", [1, D], BF16)
            onesr = T("onesr", [1, 128], BF16)
            onesm = T("onesm", [128, 128], BF16)
            w1R, w2R, dgR, cR = Res("w1"), Res("w2"), Res("dg"), Res("cconst")
            self.load_w_bf16("w1", w1, self.cv_w1[j], w1R)
            self.load_w_bf16("w2", w2, self.cv_w2[j], w2R)
            sch.barrier()
            sch.dma("pool", "b2r", out=b2r[:], in_=self.cv_b2[j:j + 1, :], writes=[cR])
            sch.op("dve", lambda e: e.memset(onesr[:], 1.0), writes=[cR])
            sch.op("dve", lambda e: e.memset(onesm[:], 1.0), writes=[cR])
            wo = self.off["cv_wdw"][0] + j * 8 * CW
            for kc in range(8):
                for k in range(CW):
                    o = wo + kc * CW + k
                    sch.op("dve" if (k % 2 == 0) else "pool",
                           lambda e: e.tensor_scalar(out=dg[:, kc, k, :], in0=self.identb[:], scalar1=self.small[:, o:o + 1],
                                                     scalar2=None, op0=ALU.mult), reads=[self.identR], writes=[dgR])
            W = dict(w1=w1, w2=w2, dg=dg, b2r=b2r, onesr=onesr, onesm=onesm, w1R=w1R, w2R=w2R, dgR=dgR, cR=cR)
            seqs = [(1, CTX), (0, self.S)] if l < DEPTH - 1 else [(0, self.S)]
            for seq, Tn in seqs:
                if seq == 0:
                    self.fold_gate(l, 2, [(w2[:, kc, :], w2R, 128) for kc in range(8)] + [(b2r[0:1, :], cR, 1)])
                with contextlib.ExitStack() as ses:
                    self.conv_seq(ses, l, seq, Tn, src, dst, W, fold=(seq == 0))
                    sch.barrier()

    def conv_seq(self, pes, l, s, Tn, src, dst, W, fold=False):
        nc, sch = self.nc, self.sch
        j = l // 2
        U = 256
        HL = 16
        T = lambda name, shape, dtp: pes.enter_context(nc.sbuf_tensor(self.uniq(name), shape, dtp))
        P = lambda name, shape, dtp: pes.enter_context(nc.psum_tensor(self.uniq(name), shape, dtp))
        NU = Tn // U
        NT = Tn // 128
        tpu = U // 128
        ns = self.alloc_norm(pes, U)
        RS = self.alloc_resid(pes, l, 2, s, fold=fold)
        PSY = (P("psy", [128, D], F32), Res("psy"))
        PSAG = [(P(f"psag{i}", [128, 2, U], F32), Res(f"psag{i}")) for i in range(2)]
        PSC = [(P(f"psc{i}", [128, 512], F32)[:, 0:U], Res(f"psc{i}")) for i in range(2)]
        PSS = (P("pss", [128, 2, U], F32), Res("pss"))
        NV = min(2, NU)
        NB2 = min(2, NU)
        V = [(T(f"v{i}", [128, 8, U + 2 * HL], BF16), Res(f"v{i}")) for i in range(NV)]
        SG = [(T(f"sg{i}", [128, U], F32), Res(f"sg{i}")) for i in range(2)]
        WCS = [(T(f"wc{i}", [128, 8, U], F32), [Res(f"wc{i}_{k}") for k in range(8)]) for i in range(NB2)]
        WB = (T("wb", [128, 8, U], BF16), [Res(f"wb{k}") for k in range(8)])
        WS = (T("wsq", [128, 8, U], BF16), [Res(f"wsq{k}") for k in range(8)])
        MEANS = [(T(f"mean{i}", [128, U], F32), Res(f"mean{i}")) for i in range(NB2)]
        MSQS = [(T(f"msq{i}", [128, U], F32), Res(f"msq{i}")) for i in range(1)]
        RSTDS = [(T(f"rstd{i}", [128, U], F32), Res(f"rstd{i}")) for i in range(NB2)]
        ZT = [(T(f"zt{i}", [128, 8, U], BF16), Res(f"zt{i}")) for i in range(NB2)]
        b1o = self.off["cv_b1"][0] + j * 16
        B1 = lambda c: self.small[:, b1o + c:b1o + c + 1]
        bdo = self.off["cv_bdw"][0] + j * 8
        lgo = self.off["cv_lng"][0] + j * 8
        lbo = self.off["cv_lnb"][0] + j * 8
        w1, w2, dg = W["w1"], W["w2"], W["dg"]

        def stageA(m):
            bt, btR = ns["BT"][m % 2]
            v, vR = V[m % NV]
            for c in range(8):
                ps, psR = PSAG[(m * 8 + c) % 2]

                def mm(e):
                    for hf in range(2):
                        col = hf * D + c * 128
                        for kc in range(8):
                            ins = e.matmul(ps[:, hf, :], lhsT=w1[:, kc, col:col + 128], rhs=bt[:, kc, :],
                                           start=(kc == 0), stop=(kc == 7))
                    return ins
                sch.op("pe", mm, reads=[btR, W["w1R"]], writes=[psR])
                sg, sgR = SG[(m * 8 + c) % 2]
                sch.op("act", lambda e: e.activation(out=sg[:], in_=ps[:, 1, :], func=AF.Sigmoid, bias=B1(8 + c), scale=1.0),
                       reads=[psR], writes=[sgR])
                sch.op("dve", lambda e: e.scalar_tensor_tensor(out=v[:, c, HL:HL + U], in0=ps[:, 0, :], scalar=B1(c),
                                                               in1=sg[:], op0=ALU.add, op1=ALU.mult),
                       reads=[psR, sgR], writes=[vR])
            if m == 0:
                sch.op("pool", lambda e: e.memset(v[:, :, 0:HL], 0.0), writes=[vR])
            else:
                vp, vpR = V[(m - 1) % NV]
                sch.op("pool", lambda e: e.tensor_copy(out=vp[:, :, HL + U:HL + U + HL], in_=v[:, :, HL:2 * HL]),
                       reads=[vR], writes=[vpR])
                sch.op("pool", lambda e: e.tensor_copy(out=v[:, :, 0:HL], in_=vp[:, :, U:U + HL]),
                       reads=[vpR], writes=[vR])
            if m == NU - 1:
                sch.op("pool", lambda e: e.memset(v[:, :, HL + U:HL + U + HL], 0.0), writes=[vR])

        def stageB1(m):
            v, vR = V[m % NV]
            wc, wcR = WCS[m % NB2]
            wb, wbR = WB
            wsq, wsR = WS
            for kc in range(8):
                pc, pcR = PSC[kc % 2]

                def mm(e):
                    for k in range(CW):
                        ins = e.matmul(pc[:, :], lhsT=dg[:, kc, k, :], rhs=v[:, kc, 1 + k:1 + k + U],
                                       start=(k == 0), stop=(k == CW - 1))
                    return ins
                sch.op("pe", mm, reads=[vR, W["dgR"]], writes=[pcR])
                bd = self.small[:, bdo + kc:bdo + kc + 1]
                sch.op("act", lambda e: e.activation(out=wc[:, kc, :], in_=pc[:, :], func=AF.Identity, bias=bd, scale=1.0),
                       reads=[pcR], writes=[wcR[kc]])
                sch.op("act", lambda e: e.activation(out=wsq[:, kc, :], in_=pc[:, :], func=AF.Square, bias=bd, scale=1.0),
                       reads=[pcR], writes=[wsR[kc]])
                sch.op("pool", lambda e: e.tensor_copy(out=wb[:, kc, :], in_=wc[:, kc, :]), reads=[wcR[kc]], writes=[wbR[kc]])
            pss, pssR = PSS

            def mm2(e):
                for q, (src_, _) in enumerate(((wb, wbR), (wsq, wsR))):
                    for kc in range(8):
                        ins = e.matmul(pss[:, q, :], lhsT=W["onesm"][:], rhs=src_[:, kc, :], start=(kc == 0), stop=(kc == 7))
                return ins
            sch.op("pe", mm2, reads=wbR + wsR + [W["cR"]], writes=[pssR])
            mean, meanR = MEANS[m % NB2]
            msq, msqR = MSQS[0]
            rstd, rstdR = RSTDS[m % NB2]
            sch.op("act", lambda e: e.activation(out=mean[:], in_=pss[:, 0, :], func=AF.Identity, scale=1.0 / D),
                   reads=[pssR], writes=[meanR])
            sch.op("dve", lambda e: e.tensor_tensor(out=msq[:], in0=mean[:], in1=mean[:], op=ALU.mult),
                   reads=[meanR], writes=[msqR])
            sch.op("dve", lambda e: e.scalar_tensor_tensor(out=msq[:], in0=pss[:, 1, :], scalar=1.0 / D, in1=msq[:],
                                                           op0=ALU.mult, op1=ALU.subtract),
                   reads=[pssR, msqR], writes=[msqR])
            sch.op("act", lambda e: e.activation(out=rstd[:], in_=msq[:], func=AF.Sqrt, bias=self.epsT[:, 0:1], scale=1.0),
                   reads=[msqR], writes=[rstdR])
            sch.op("dve", lambda e: e.reciprocal(out=rstd[:], in_=rstd[:]), reads=[rstdR], writes=[rstdR])

        def stageB2(m):
            wc, wcR = WCS[m % NB2]
            zt, ztR = ZT[m % NB2]
            mean, meanR = MEANS[m % NB2]
            rstd, rstdR = RSTDS[m % NB2]
            for kc in range(8):
                sch.op("dve", lambda e: e.tensor_tensor(out=wc[:, kc, :], in0=wc[:, kc, :], in1=mean[:], op=ALU.subtract),
                       reads=[wcR[kc], meanR], writes=[wcR[kc]])
            for kc in range(8):
                sch.op("pool" if kc % 2 else "dve",
                       lambda e: e.tensor_tensor(out=wc[:, kc, :], in0=wc[:, kc, :], in1=rstd[:], op=ALU.mult),
                       reads=[wcR[kc], rstdR], writes=[wcR[kc]])
            for kc in range(8):
                sch.op("act", lambda e: e.activation(out=zt[:, kc, :], in_=wc[:, kc, :], func=AF.Silu,
                                                     scale=self.small[:, lgo + kc:lgo + kc + 1],
                                                     bias=self.small[:, lbo + kc:lbo + kc + 1]),
                       reads=[wcR[kc]], writes=[ztR])

        def stageB2b(m):
            zt, ztR = ZT[m % NB2]
            for jt in range(tpu):
                n = m * tpu + jt
                py, pyR = PSY

                def mm3(e):
                    for nb in range(2):
                        e.matmul(py[:, nb * 512:(nb + 1) * 512], lhsT=W["onesr"][0:1, :],
                                 rhs=W["b2r"][0:1, nb * 512:(nb + 1) * 512], start=True, stop=False)
                        for kc in range(8):
                            ins = e.matmul(py[:, nb * 512:(nb + 1) * 512], lhsT=zt[:, kc, jt * 128:(jt + 1) * 128],
                                           rhs=w2[:, kc, nb * 512:(nb + 1) * 512], start=False, stop=(kc == 7))
                    return ins
                sch.op("pe", mm3, reads=[ztR, W["w2R"], W["cR"]], writes=[pyR])
                self.resid_stage(RS, n, py, pyR, src, dst, s)

        nargs = (U, NT, src, s, l, 0, s)
        self.stage_norm_pre(ns, 0, *nargs)
        for step in range(NU + 3):
            self.flush_stores()
            if step < NU:
                self.stage_norm_post(ns, step, *nargs)
            if 0 <= step - 1 < NU:
                stageA(step - 1)
            if 0 <= step - 3 < NU:
                stageB2(step - 3)
            if step + 1 < NU:
                self.stage_norm_pre(ns, step + 1, *nargs)
            if 0 <= step - 2 < NU:
                stageB1(step - 2)
            if 0 <= step - 3 < NU:
                stageB2b(step - 3)
        self.flush_stores()


    def phase_attn(self, l, src, dst):
        nc, sch = self.nc, self.sch
        j = l // 2
        with_ctx_out = l < DEPTH - 1
        with contextlib.ExitStack() as pes:
            T = lambda name, shape, dtp: pes.enter_context(nc.sbuf_tensor(self.uniq(name), shape, dtp))
            wqkv = T("wqkv", [128, 8, 1536], BF16)
            wo = T("wo", [128, 8, D], BF16)
            masks = T("masks", [128, 2, 512], BF16)
            rope = T("rope", [128, self.S // 128, 64], F32)
            esink = T("esink", [128, 16], F32)
            wqR, woR, cR = Res("wqkv"), Res("wo"), Res("aconst")
            self.load_w_bf16("wqkv", wqkv, self.at_wqkv[j], wqR)
            self.load_w_bf16("wo", wo, self.at_wo[j], woR)
            sch.barrier()
            sch.dma("pool", "masks", out=masks[:], in_=self.masks_d, writes=[cR])
            sch.dma("sp", "rope", out=rope[:], in_=self.rope_d, writes=[cR])
            so = self.off["sink"][0] + j * 16
            sch.op("act", lambda e: e.activation(out=esink[:], in_=self.small[:, so:so + 16], func=AF.Exp),
                   reads=[self.smallR], writes=[cR])
            ckt = [(T(f"ckt{i}", [128, 4, 128], BF16), Res(f"ckt{i}")) for i in range(2)]
            for kt_, ktR_ in ckt:
                sch.op("pool", lambda e: e.memset(kt_[:], 0.0), writes=[ktR_])
            cva = [(T(f"cva{i}", [128, 4, 65], BF16), Res(f"cva{i}")) for i in range(2)]
            W = dict(wqkv=wqkv, wo=wo, masks=masks, rope=rope, esink=esink, wqR=wqR, woR=woR, cR=cR, ckt=ckt, cva=cva)
            for seq, Tn in [(1, CTX), (0, self.S)]:
                if seq == 0:
                    self.fold_gate(l, 2, [(wo[:, kc, :], woR, 128) for kc in range(8)])
                with contextlib.ExitStack() as ses:
                    self.attn_seq(ses, l, seq, Tn, src, dst, W, with_ctx_out, fold=(seq == 0))
                    sch.barrier()

    def attn_seq(self, pes, l, s, Tn, src, dst, W, with_ctx_out, fold=False):
        nc, sch = self.nc, self.sch
        T = lambda name, shape, dtp: pes.enter_context(nc.sbuf_tensor(self.uniq(name), shape, dtp))
        P = lambda name, shape, dtp: pes.enter_context(nc.psum_tensor(self.uniq(name), shape, dtp))
        NT = Tn // 128
        is_ctx = (s == 1)
        do_out = (not is_ctx) or with_ctx_out
        psQ = (P("psq", [128, 1536], F32), Res("psq01"))
        pkvR = Res("psq2")
        PSS = [(P(f"pss{i}", [128, 512], F32), Res(f"pss{i}")) for i in range(3)]
        ns = self.alloc_norm(pes, 128)
        psT, psTR = ns["psT"]
        PSO = []
        bankx = P("pso0", [128, 512], F32)
        psO0 = (bankx, Res("pso0"))
        psKT = (bankx[:, 384:512].bitcast(BF16).rearrange("p (c t) -> p c t", c=2), psO0[1])
        PSO = [psO0]
        RS = self.alloc_resid(pes, l, 2, s, fold=fold)
        R = 5
        if is_ctx:
            KT, VA = W["ckt"], W["cva"]
            R = 2
        else:
            KT = [(T(f"kt{i}", [128, 4, 128], BF16), Res(f"kt{i}")) for i in range(R)]
            for kt_, ktR_ in KT:
                sch.op("pool", lambda e: e.memset(kt_[:], 0.0), writes=[ktR_])
            VA = [(T(f"va{i}", [128, 4, 65], BF16), Res(f"va{i}")) for i in range(R)]
        for va, vaR in VA:
            sch.op("pool", lambda e: e.memset(va[:], 1.0), writes=[vaR])
        QT = [(T(f"qt{i}", [128, 8, 128], BF16), Res(f"qt{i}")) for i in range(3)]
        QR = [(T(f"qr{i}", [128, 1024], BF16), Res(f"qr{i}")) for i in range(2)]
        KR = [(T(f"kr{i}", [128, 256], BF16), Res(f"kr{i}")) for i in range(2)]
        TQ = [(T(f"tq{i}", [128, 16, 32], F32), Res(f"tq{i}")) for i in range(2)]
        TK = [(T(f"tk{i}", [128, 4, 32], F32), Res(f"tk{i}")) for i in range(2)]
        PT = [(T(f"pt{i}", [128, 512], BF16), Res(f"pt{i}")) for i in range(5)]
        OS = [(T(f"os{i}", [128, 260], F32), Res(f"os{i}")) for i in range(2)]
        OB = [(T(f"ob{i}", [128, 1024], BF16), Res(f"ob{i}")) for i in range(2)]
        OT = [(T(f"ot{i}", [128, 8, 128], BF16), Res(f"ot{i}")) for i in range(2)]
        DEN = [(T(f"den{i}", [128, 8], F32), Res(f"den{i}")) for i in range(2)]
        wqkv, wo, masks, rope, esink = W["wqkv"], W["wo"], W["masks"], W["rope"], W["esink"]
        pq, pqR = psQ
        ptc = [0]

        def stageQa(n):
            bt, btR = ns["BT"][n % 2]

            def mm(e):
                for nb in range(3):
                    for kc in range(8):
                        ins = e.matmul(pq[:, nb * 512:(nb + 1) * 512], lhsT=bt[:, kc, :], rhs=wqkv[:, kc, nb * 512:(nb + 1) * 512],
                                       start=(kc == 0), stop=(kc == 7))
                return ins
            sch.op("pe", mm, reads=[btR, W["wqR"]], writes=[pqR, pkvR])
            qr, qrR = QR[n % 2]
            kr, krR = KR[n % 2]
            va, vaR = VA[n % R]
            qv = pq[:, 0:1024].rearrange("p (h d) -> p h d", h=16)
            kv = pq[:, 1024:1280].rearrange("p (h d) -> p h d", h=4)
            qrv = qr[:, :].rearrange("p (h d) -> p h d", h=16)
            krv = kr[:, :].rearrange("p (h d) -> p h d", h=4)
            sch.op("act", lambda e: e.activation(out=va[:, :, 0:64], in_=pq[:, 1280:1536].rearrange("p (h d) -> p h d", h=4),
                                                 func=AF.Identity), reads=[pkvR], writes=[vaR])
            if is_ctx:
                if do_out:
                    sch.op("act", lambda e: e.activation(out=qr[:], in_=pq[:, 0:1024], func=AF.Identity), reads=[pqR], writes=[qrR])
                sch.op("act", lambda e: e.activation(out=kr[:], in_=pq[:, 1024:1280], func=AF.Identity), reads=[pkvR], writes=[krR])
            else:
                for (xv, ov, nh, tbufs, oR, pR) in ((kv, krv, 4, TK, krR, pkvR), (qv, qrv, 16, TQ, qrR, pqR)):
                    cosb = rope[:, n, 0:32].unsqueeze(1).to_broadcast([128, nh, 32])
                    sinb = rope[:, n, 32:64].unsqueeze(1).to_broadcast([128, nh, 32])
                    (t1, t1R), (t2, t2R) = tbufs
                    x1, x2 = xv[:, :, 0:32], xv[:, :, 32:64]
                    sch.op("dve", lambda e: e.tensor_tensor(out=t1[:], in0=x1, in1=cosb, op=ALU.mult), reads=[pR, W["cR"]], writes=[t1R])
                    sch.op("dve", lambda e: e.tensor_tensor(out=t2[:], in0=x2, in1=sinb, op=ALU.mult), reads=[pR, W["cR"]], writes=[t2R])
                    sch.op("dve", lambda e: e.tensor_tensor(out=ov[:, :, 0:32], in0=t1[:], in1=t2[:], op=ALU.subtract),
                           reads=[t1R, t2R], writes=[oR])
                    sch.op("dve", lambda e: e.tensor_tensor(out=t1[:], in0=x2, in1=cosb, op=ALU.mult), reads=[pR, W["cR"]], writes=[t1R])
                    sch.op("dve", lambda e: e.tensor_tensor(out=t2[:], in0=x1, in1=sinb, op=ALU.mult), reads=[pR, W["cR"]], writes=[t2R])
                    sch.op("dve", lambda e: e.tensor_tensor(out=ov[:, :, 32:64], in0=t1[:], in1=t2[:], op=ALU.add),
                           reads=[t1R, t2R], writes=[oR])

        def stageQb(n):
            qr, qrR = QR[n % 2]
            kr, krR = KR[n % 2]
            kt, ktR = KT[n % R]
            pk, pkR = psKT

            def trk(e):
                for c in range(2):
                    ins = e.transpose(out=pk[:, c, :], in_=kr[:, c * 128:(c + 1) * 128], identity=self.identb[:])
                return ins
            sch.op("pe", trk, reads=[krR], writes=[pkR])
            ktv = kt[:, :, :].rearrange("p (c two) t -> p c two t", two=2)
            sch.op("act", lambda e: e.activation(out=ktv[0:64, :, 0, :], in_=pk[0:64, :, :], func=AF.Identity), reads=[pkR], writes=[ktR])
            sch.op("act", lambda e: e.activation(out=ktv[64:128, :, 1, :], in_=pk[64:128, :, :], func=AF.Identity), reads=[pkR], writes=[ktR])
            if do_out:
                qt, qtR = QT[n % 3]

                def trq(e):
                    for c in range(8):
                        ins = e.transpose(out=psT[:, c, :], in_=qr[:, c * 128:(c + 1) * 128], identity=self.identb[:])
                    return ins
                sch.op("pe", trq, reads=[qrR], writes=[psTR])
                sch.op("act", lambda e: e.activation(out=qt[:], in_=psT[:], func=AF.Identity), reads=[psTR], writes=[qtR])

        def stageS(i):
            qt, qtR = QT[i % 3]
            ob, obR = OB[i % 2]
            chunks = []
            chunks.append((W["ckt"][0], W["cva"][0], None))
            chunks.append((W["ckt"][1], W["cva"][1], None))
            if not is_ctx:
                chunks.append((KT[i % R], VA[i % R], None))
                if i - 1 >= 0:
                    chunks.append((KT[(i - 1) % R], VA[(i - 1) % R], 0))
                if i + 1 < NT:
                    chunks.append((KT[(i + 1) % R], VA[(i + 1) % R], 1))
            items = [(g, ci) for g in range(4) for ci in range(len(chunks))]
            LOOK = 2
            bufs = {}

            def score(k):
                g, ci = items[k]
                (kt, ktR), (va, vaR), mk = chunks[ci]
                hp = 64 * (g % 2)
                c0 = 4 * (g // 2)
                ps, psR = PSS[ptc[0] % len(PSS)]
                pt, ptR = PT[ptc[0] % len(PT)]
                ptc[0] += 1
                bufs[k] = (pt, ptR)
                sch.op("pe", lambda e: e.matmul(ps[:], lhsT=kt[:, g, :], rhs=qt[:, c0:c0 + 4, :],
                                                start=True, stop=True), reads=[ktR, qtR], writes=[psR])
                sch.op("act", lambda e: e.activation(out=pt[:], in_=ps[:], func=AF.Exp, scale=HD ** -0.5),
                       reads=[psR], writes=[ptR])
                if mk is not None:
                    sch.op("pool", lambda e: e.tensor_tensor(out=pt[:], in0=pt[:], in1=masks[:, mk, :], op=ALU.mult),
                           reads=[ptR, W["cR"]], writes=[ptR])

            def pvs(k):
                g, ci = items[k]
                (kt, ktR), (va, vaR), mk = chunks[ci]
                pt, ptR = bufs.pop(k)
                po, poR = PSO[0]

                def pv(e):
                    for hh in range(4):
                        ins = e.matmul(po[:, hh * 65:(hh + 1) * 65],
                                       lhsT=pt[:, hh * 128:(hh + 1) * 128], rhs=va[:, g, :],
                                       start=(ci == 0 and hh == 0), stop=(ci == len(chunks) - 1 and hh == 3))
                    return ins
                sch.op("pe", pv, reads=[ptR, vaR], writes=[poR])
                if ci == len(chunks) - 1:
                    osb, osR = OS[g % 2]
                    den, denR = DEN[g % 2]
                    sch.op("dve", lambda e: e.tensor_copy(out=osb[:], in_=po[:, 0:260]), reads=[poR], writes=[osR])
                    osv = osb[:, :].rearrange("p (h d) -> p h d", h=4)
                    sch.op("pool", lambda e: e.tensor_tensor(out=den[:, 0:4], in0=osv[:, :, 64], in1=esink[:, 4 * g:4 * g + 4], op=ALU.add),
                           reads=[osR, W["cR"]], writes=[denR])
                    sch.op("dve", lambda e: e.reciprocal(out=den[:, 4:8], in_=den[:, 0:4]), reads=[denR], writes=[denR])
                    obv = ob[:, 256 * g:256 * (g + 1)].rearrange("p (h d) -> p h d", h=4)
                    sch.op("dve", lambda e: e.tensor_tensor(out=obv, in0=osv[:, :, 0:64],
                                                            in1=den[:, 4:8].unsqueeze(2).to_broadcast([128, 4, 64]), op=ALU.mult),
                           reads=[osR, denR], writes=[obR])

            for k in range(len(items) + LOOK):
                if k < len(items):
                    score(k)
                if k - LOOK >= 0:
                    pvs(k - LOOK)

        def stageO(i):
            ob, obR = OB[i % 2]
            ot, otR = OT[i % 2]

            def tro(e):
                for c in range(8):
                    ins = e.transpose(out=psT[:, c, :], in_=ob[:, c * 128:(c + 1) * 128], identity=self.identb[:])
                return ins
            sch.op("pe", tro, reads=[obR], writes=[psTR])
            sch.op("act", lambda e: e.activation(out=ot[:], in_=psT[:], func=AF.Identity), reads=[psTR], writes=[otR])

            def mmo(e):
                for nb in range(2):
                    for kc in range(8):
                        ins = e.matmul(pq[:, nb * 512:(nb + 1) * 512], lhsT=ot[:, kc, :], rhs=wo[:, kc, nb * 512:(nb + 1) * 512],
                                       start=(kc == 0), stop=(kc == 7))
                return ins
            sch.op("pe", mmo, reads=[otR, W["woR"]], writes=[pqR])
            self.resid_stage(RS, i, pq[:, 0:1024], pqR, src, dst, s)

        nargs = (128, NT, src, s, l, 0, s)
        self.stage_norm_pre(ns, 0, *nargs)
        for step in range(NT + 5):
            self.flush_stores()
            if step < NT:
                self.stage_norm_post(ns, step, *nargs)
            if 0 <= step - 1 < NT:
                stageQa(step - 1)
            if step + 1 < NT:
                self.stage_norm_pre(ns, step + 1, *nargs)
            if do_out and 0 <= step - 3 < NT:
                stageS(step - 3)
            if do_out and 0 <= step - 4 < NT:
                stageO(step - 4)
            if 0 <= step - 1 < NT:
                stageQb(step - 1)
        self.flush_stores()


QPERM = np.concatenate([np.concatenate([np.arange(lo * 64, lo * 64 + 64), np.arange(hi * 64, hi * 64 + 64)])
                        for lo, hi in [(c, 4 + c) if c < 4 else (8 + c - 4, 12 + c - 4) for c in range(8)]])


def rope_table(S):
    t = np.arange(S)
    row = (t // 64).astype(np.float32)
    col = (t % 64).astype(np.float32)
    inv = (np.float32(10000.0) ** (-(np.arange(16, dtype=np.float32)) / np.float32(16))).astype(np.float32)
    ang = np.concatenate([row[:, None] * inv, col[:, None] * inv], axis=-1).astype(np.float32)
    tab = np.concatenate([np.cos(ang), np.sin(ang)], axis=-1).astype(np.float32)
    return np.ascontiguousarray(tab.reshape(S // 128, 128, 64).transpose(1, 0, 2))


def mask_table():
    kl = np.arange(128)[:, None]
    ql = np.arange(128)[None, :]
    mprev = (kl >= ql).astype(np.float32)
    mnext = (kl <= ql).astype(np.float32)
    m = np.stack([np.tile(mprev, (1, 4)), np.tile(mnext, (1, 4))], axis=1)
    return np.ascontiguousarray(m)


def make_in_map(inp, b):
    S = inp["x"].shape[1]
    wqkv = np.asarray(inp["attn_w_qkv"], np.float32)
    wqkv_p = np.concatenate([wqkv[:, :, :1024][:, :, QPERM], wqkv[:, :, 1024:]], axis=-1)
    m = {
        "attn_w_qkv": wqkv_p, "attn_w_o": inp["attn_w_o"], "masks": mask_table(), "rope": rope_table(S),
        "x": np.ascontiguousarray(inp["x"][b]),
        "ctx": np.ascontiguousarray(inp["ctx"][b]),
        "small": pack_small(inp, b),
        "ident": np.eye(128, dtype=np.float32),
        "ada_w": inp["ada_w"], "ada_b": inp["ada_b"],
        "final_g": np.ascontiguousarray(inp["final_g"]).reshape(1, D),
        "ffn_w_up": inp["ffn_w_up"], "ffn_w_down": inp["ffn_w_down"],
        "conv_w_pw1": inp["conv_w_pw1"], "conv_w_pw2": inp["conv_w_pw2"], "conv_b_pw2": inp["conv_b_pw2"],
    }
    return {k: np.ascontiguousarray(v, dtype=np.float32) for k, v in m.items()}


def kernel(**inputs):
    inp = {k: np.asarray(v) for k, v in inputs.items()}
    B, S, _ = inp["x"].shape
    prog = Prog(S)
    nc = prog.build()
    in_maps = [make_in_map(inp, b) for b in range(B)]
    res = run_bass_kernel_spmd(nc, in_maps, core_ids=list(range(B)))
    return np.stack([r["out"] for r in res.results], axis=0).astype(np.float32)
```
